# Optimizing a Trainium2 kernel written in Bass

```python
import math
import jax
import jax.numpy as jnp
from jax import lax
import numpy as np

D_MODEL = 2048
BATCH = 4
SEQ = 4096
DEPTH = 4

GRID_W = 64
CTX_LEN = 256
N_MOD = 6

NA_HEADS = 8
NA_HEAD_DIM = 128
NA_WIDTH = NA_HEADS * NA_HEAD_DIM
NA_WIN_ROWS = 8
NA_WIN_COLS = 16

HY_WIDTH = D_MODEL - NA_WIDTH
HY_ORDER = 2
HY_DIRS = 2
HY_SHORT = 3
HY_EMB = 33
HY_BANDS = (HY_EMB - 1) // 2
HY_HIDDEN = 64
HY_DECAY_SHORT = 0.3
HY_DECAY_LONG = 1.5
HY_DECAY_TARGET = 1e-2
EVEN_IN = 3 * NA_WIDTH + (HY_ORDER + 1) * HY_WIDTH
EVEN_OUT = NA_WIDTH + HY_WIDTH

MLA_HEADS = 16
MLA_Q_LORA = 768
MLA_KV_LORA = 512
MLA_NOPE = 128
MLA_ROPE = 64
MLA_V = 128
MLA_QK = MLA_NOPE + MLA_ROPE
MLA_DOWN = MLA_Q_LORA + MLA_KV_LORA + MLA_ROPE
ROPE_BASE = 10000.0
Q_BLOCK = 128

N_EXPERTS = 16
EC_CAPACITY = 2
EXPERT_FF = 1024

EPS = 1e-6
NEG_INF = -1e30

kernel_name = 'hybrid_natten_hyena_mla_ecmoe_dit'


def rms_norm(x, g):
    xf = x.astype(jnp.float32)
    y = xf * lax.rsqrt(jnp.mean(xf * xf, axis=-1, keepdims=True) + EPS)
    return (y * g.astype(jnp.float32)).astype(x.dtype)


def modulate(h, shift, scale):
    return h * (1 + scale) + shift


def softmax_attend(q, k, v, scale):
    s = jnp.einsum('bqhd,bkhd->bhqk', q, k).astype(jnp.float32) * scale
    p = jax.nn.softmax(s, axis=-1).astype(v.dtype)
    return jnp.einsum('bhqk,bkhd->bqhd', p, v)


def blocked_attend(q, k, v, scale):
    b, s, h, dq = q.shape
    nblk = s // Q_BLOCK
    qb = jnp.moveaxis(q.reshape(b, nblk, Q_BLOCK, h, dq), 1, 0)
    ob = lax.map(lambda qi: softmax_attend(qi, k, v, scale), qb)
    return jnp.moveaxis(ob, 0, 1).reshape(b, s, h, v.shape[-1])


def grid_positions(n):
    t = jnp.arange(n)
    return t // GRID_W, t % GRID_W


def rope_1d(t, pos):
    n = t.shape[-1]
    inv = ROPE_BASE ** (-jnp.arange(0, n, 2, dtype=jnp.float32) / n)
    ang = pos.astype(jnp.float32)[:, None] * inv[None, :]
    ang = jnp.concatenate([ang, ang], axis=-1)[None, :, None, :]
    t1, t2 = jnp.split(t, 2, axis=-1)
    rot = jnp.concatenate([-t2, t1], axis=-1)
    return (t * jnp.cos(ang) + rot * jnp.sin(ang)).astype(t.dtype)


def rope_2d(t, rows, cols):
    tr, tc = jnp.split(t, 2, axis=-1)
    return jnp.concatenate([rope_1d(tr, rows), rope_1d(tc, cols)], axis=-1)


def neighbourhood_attention(q, k, v, k_ctx, v_ctx, rpb, scale):
    b, s, h, d = q.shape
    rows = s // GRID_W
    wr = min(NA_WIN_ROWS, rows)
    r_idx = jnp.arange(rows)
    row_start = jnp.clip(r_idx - wr // 2, 0, rows - wr)
    key_rows = row_start[:, None] + jnp.arange(wr)[None, :]
    cols = jnp.arange(GRID_W)
    col_start = jnp.clip(cols - NA_WIN_COLS // 2, 0, GRID_W - NA_WIN_COLS)
    in_win = (cols[None, :] >= col_start[:, None]) & (cols[None, :] < col_start[:, None] + NA_WIN_COLS)
    dr = key_rows - r_idx[:, None] + (NA_WIN_ROWS - 1)
    dc = jnp.clip(cols[None, :] - cols[:, None] + (NA_WIN_COLS - 1), 0, 2 * NA_WIN_COLS - 2)
    bias = rpb[:, dr[:, None, :, None], dc[None, :, None, :]].astype(jnp.float32)
    qg = q.reshape(b, rows, GRID_W, h, d)
    kg = k.reshape(b, rows, GRID_W, h, d)[:, key_rows]
    vg = v.reshape(b, rows, GRID_W, h, d)[:, key_rows]
    s_win = jnp.einsum('brqhd,brwkhd->bhrqwk', qg, kg).astype(jnp.float32) * scale + bias
    s_win = jnp.where(in_win[:, None, :], s_win, NEG_INF).reshape(b, h, rows, GRID_W, wr * GRID_W)
    s_ctx = jnp.einsum('brqhd,bchd->bhrqc', qg, k_ctx).astype(jnp.float32) * scale
    p = jax.nn.softmax(jnp.concatenate([s_win, s_ctx], axis=-1), axis=-1).astype(v.dtype)
    p_win = p[..., :wr * GRID_W].reshape(b, h, rows, GRID_W, wr, GRID_W)
    p_ctx = p[..., wr * GRID_W:]
    out = jnp.einsum('bhrqwk,brwkhd->brqhd', p_win, vg) + jnp.einsum('bhrqc,bchd->brqhd', p_ctx, v_ctx)
    return out.reshape(b, s, h * d)


def short_conv(u, w, bias):
    ch = u.shape[-1]
    y = lax.conv_general_dilated(u, w.astype(u.dtype)[:, None, :], window_strides=(1,),
                                 padding=((HY_SHORT // 2, HY_SHORT // 2),),
                                 dimension_numbers=('NWC', 'WIO', 'NWC'), feature_group_count=ch)
    return y + bias.astype(u.dtype)


def hyena_filters(length, w1, b1, w2, b2, w3, freq):
    f32 = jnp.float32
    t = jnp.linspace(0.0, 1.0, length, dtype=f32)[:, None]
    w = 2.0 * math.pi * jnp.arange(length, dtype=f32)[:, None] / length
    bands = jnp.linspace(1e-4, HY_BANDS - 1, HY_BANDS, dtype=f32)[None, :]
    z = jnp.concatenate([t, jnp.cos(bands * w), -jnp.sin(bands * w)], axis=-1)
    fr = freq.astype(f32)
    hid = jnp.sin(fr * (z @ w1.astype(f32) + b1.astype(f32)))
    hid = jnp.sin(fr * (hid @ w2.astype(f32) + b2.astype(f32)))
    filt = (hid @ w3.astype(f32)).reshape(length, HY_ORDER * HY_DIRS, HY_WIDTH)
    d_lo = math.log(HY_DECAY_TARGET) / HY_DECAY_LONG
    d_hi = math.log(HY_DECAY_TARGET) / HY_DECAY_SHORT
    deltas = jnp.abs(jnp.linspace(d_lo, d_hi, HY_WIDTH, dtype=f32))
    filt = filt * jnp.exp(-t * deltas)[:, None, :]
    return filt * lax.rsqrt(jnp.sum(filt * filt, axis=0, keepdims=True) + EPS)


def bidir_long_conv(u, h_fwd, h_bwd, d_skip):
    n = u.shape[1]
    nfft = 2 * n
    uf = u.astype(jnp.float32)
    hf = jnp.fft.rfft(h_fwd, n=nfft, axis=0)[None]
    hb = jnp.fft.rfft(h_bwd, n=nfft, axis=0)[None]
    y_f = jnp.fft.irfft(jnp.fft.rfft(uf, n=nfft, axis=1) * hf, n=nfft, axis=1)[:, :n]
    y_b = jnp.fft.irfft(jnp.fft.rfft(uf[:, ::-1], n=nfft, axis=1) * hb, n=nfft, axis=1)[:, :n][:, ::-1]
    return (y_f + y_b + uf * d_skip.astype(jnp.float32)).astype(u.dtype)


def hyena_operator(p, short_w, short_b, filt, d_skip):
    parts = jnp.split(short_conv(p, short_w, short_b), HY_ORDER + 1, axis=-1)
    z = parts[0]
    for o in range(HY_ORDER):
        z = parts[o + 1] * bidir_long_conv(z, filt[:, HY_DIRS * o], filt[:, HY_DIRS * o + 1], d_skip[o])
    return z


def na_heads(t):
    return t.reshape(t.shape[:-1] + (NA_HEADS, NA_HEAD_DIM))


def even_mixer(h_lat, h_ctx, w_in, w_out, q_g, k_g, rpb, short_w, short_b,
               f_w1, f_b1, f_w2, f_b2, f_w3, f_freq, hy_d, need_ctx):
    b, s, _ = h_lat.shape
    lc = h_ctx.shape[1]
    scale = NA_HEAD_DIM ** -0.5
    q_l, k_l, v_l, hy_lat = jnp.split(h_lat @ w_in, [NA_WIDTH, 2 * NA_WIDTH, 3 * NA_WIDTH], axis=-1)
    q_l = rms_norm(na_heads(q_l), q_g)
    k_l = rms_norm(na_heads(k_l), k_g)
    v_l = na_heads(v_l)
    if need_ctx:
        q_c, k_c, v_c, hy_ctx = jnp.split(h_ctx @ w_in, [NA_WIDTH, 2 * NA_WIDTH, 3 * NA_WIDTH], axis=-1)
        q_c = rms_norm(na_heads(q_c), q_g)
    else:
        k_c, v_c = jnp.split(h_ctx @ w_in[:, NA_WIDTH:3 * NA_WIDTH], 2, axis=-1)
    k_c = rms_norm(na_heads(k_c), k_g)
    v_c = na_heads(v_c)
    na_l = neighbourhood_attention(q_l, k_l, v_l, k_c, v_c, rpb, scale)
    hy_l = hyena_operator(hy_lat, short_w, short_b, hyena_filters(s, f_w1, f_b1, f_w2, f_b2, f_w3, f_freq), hy_d)
    y_lat = jnp.concatenate([na_l, hy_l], axis=-1) @ w_out
    y_ctx = None
    if need_ctx:
        na_c = softmax_attend(q_c, k_c, v_c, scale).reshape(b, lc, NA_WIDTH)
        hy_c = hyena_operator(hy_ctx, short_w, short_b, hyena_filters(lc, f_w1, f_b1, f_w2, f_b2, f_w3, f_freq), hy_d)
        y_ctx = jnp.concatenate([na_c, hy_c], axis=-1) @ w_out
    return y_lat, y_ctx


def mla_project(h, w_down, qa_g, kva_g, w_uq, w_ukv, q_g, k_g, with_q):
    b, l, _ = h.shape
    if with_q:
        q_a, kv_a, k_r = jnp.split(h @ w_down, [MLA_Q_LORA, MLA_Q_LORA + MLA_KV_LORA], axis=-1)
    else:
        kv_a, k_r = jnp.split(h @ w_down[:, MLA_Q_LORA:], [MLA_KV_LORA], axis=-1)
    kv = (rms_norm(kv_a, kva_g) @ w_ukv).reshape(b, l, MLA_HEADS, MLA_NOPE + MLA_V)
    k_nope, v = jnp.split(kv, [MLA_NOPE], axis=-1)
    k_r = jnp.broadcast_to(k_r[:, :, None, :], (b, l, MLA_HEADS, MLA_ROPE))
    k = rms_norm(jnp.concatenate([k_nope, k_r], axis=-1), k_g)
    q = None
    if with_q:
        q = rms_norm((rms_norm(q_a, qa_g) @ w_uq).reshape(b, l, MLA_HEADS, MLA_QK), q_g)
    return q, k, v


def rope_tail(t, rows, cols):
    return jnp.concatenate([t[..., :MLA_NOPE], rope_2d(t[..., MLA_NOPE:], rows, cols)], axis=-1)


def mla_mixer(h_lat, h_ctx, w_down, qa_g, kva_g, w_uq, w_ukv, q_g, k_g, w_o, need_ctx):
    b, s, _ = h_lat.shape
    lc = h_ctx.shape[1]
    rows, cols = grid_positions(s)
    scale = MLA_QK ** -0.5
    q_l, k_l, v_l = mla_project(h_lat, w_down, qa_g, kva_g, w_uq, w_ukv, q_g, k_g, True)
    q_l = rope_tail(q_l, rows, cols)
    k_l = rope_tail(k_l, rows, cols)
    q_c, k_c, v_c = mla_project(h_ctx, w_down, qa_g, kva_g, w_uq, w_ukv, q_g, k_g, need_ctx)
    k_all = jnp.concatenate([k_c, k_l], axis=1)
    v_all = jnp.concatenate([v_c, v_l], axis=1)
    y_lat = blocked_attend(q_l, k_all, v_all, scale).reshape(b, s, MLA_HEADS * MLA_V) @ w_o
    y_ctx = None
    if need_ctx:
        y_ctx = softmax_attend(q_c, k_c, v_c, scale).reshape(b, lc, MLA_HEADS * MLA_V) @ w_o
    return y_lat, y_ctx


def expert_choice_ffn(h, router_w, w_gate, w_up, w_down):
    b, n, d = h.shape
    cap = max(1, (EC_CAPACITY * n) // N_EXPERTS)
    aff = jax.nn.softmax((h @ router_w).astype(jnp.float32), axis=-1)
    gates, idx = lax.top_k(jnp.swapaxes(aff, 1, 2), cap)
    xs = jax.vmap(lambda hb, ib: hb[ib])(h, idx)
    a = jnp.einsum('becd,edf->becf', xs, w_gate)
    u = jnp.einsum('becd,edf->becf', xs, w_up)
    y = jnp.einsum('becf,efd->becd', jax.nn.silu(a) * u, w_down)
    y = y * gates[..., None].astype(y.dtype)
    return jax.vmap(lambda yb, ib: jax.ops.segment_sum(yb.reshape(-1, d), ib.reshape(-1), num_segments=n))(y, idx)


def setup_inputs(seed: int = 0) -> dict:
    key = jax.random.key(seed)
    ks = iter(jax.random.split(key, 34))
    n_ev = (DEPTH + 1) // 2
    n_od = DEPTH // 2
    d = D_MODEL

    def nrm(shape, scale):
        return jax.random.normal(next(ks), shape, jnp.float32) * scale

    def gain(shape):
        return 1.0 + nrm(shape, 0.05)

    return {
        'x': nrm((BATCH, SEQ, d), 1.0),
        'c': nrm((BATCH, d), 1.0),
        'ctx': nrm((BATCH, CTX_LEN, d), 1.0),
        'c_ctx': nrm((d,), 1.0),
        'ada_w': nrm((DEPTH, d, N_MOD * d), 0.5 * d ** -0.5),
        'ada_b': nrm((DEPTH, N_MOD * d), 0.02),
        'norm1_g': gain((DEPTH, d)),
        'norm2_g': gain((DEPTH, d)),
        'router_w': nrm((DEPTH, d, N_EXPERTS), d ** -0.5),
        'moe_w_gate': nrm((DEPTH, N_EXPERTS, d, EXPERT_FF), d ** -0.5),
        'moe_w_up': nrm((DEPTH, N_EXPERTS, d, EXPERT_FF), d ** -0.5),
        'moe_w_down': nrm((DEPTH, N_EXPERTS, EXPERT_FF, d), EXPERT_FF ** -0.5),
        'ev_w_in': nrm((n_ev, d, EVEN_IN), d ** -0.5),
        'ev_w_out': nrm((n_ev, EVEN_OUT, d), EVEN_OUT ** -0.5),
        'na_q_g': gain((n_ev, NA_HEAD_DIM)),
        'na_k_g': gain((n_ev, NA_HEAD_DIM)),
        'na_rpb': nrm((n_ev, NA_HEADS, 2 * NA_WIN_ROWS - 1, 2 * NA_WIN_COLS - 1), 0.1),
        'hy_short_w': nrm((n_ev, HY_SHORT, (HY_ORDER + 1) * HY_WIDTH), HY_SHORT ** -0.5),
        'hy_short_b': nrm((n_ev, (HY_ORDER + 1) * HY_WIDTH), 0.02),
        'hy_w1': nrm((n_ev, HY_EMB, HY_HIDDEN), HY_EMB ** -0.5),
        'hy_b1': nrm((n_ev, HY_HIDDEN), 0.2),
        'hy_w2': nrm((n_ev, HY_HIDDEN, HY_HIDDEN), HY_HIDDEN ** -0.5),
        'hy_b2': nrm((n_ev, HY_HIDDEN), 0.2),
        'hy_w3': nrm((n_ev, HY_HIDDEN, HY_ORDER * HY_DIRS * HY_WIDTH), HY_HIDDEN ** -0.5),
        'hy_freq': gain((n_ev, HY_HIDDEN)),
        'hy_d': nrm((n_ev, HY_ORDER, HY_WIDTH), 0.5),
        'mla_w_down': nrm((n_od, d, MLA_DOWN), d ** -0.5),
        'mla_qa_g': gain((n_od, MLA_Q_LORA)),
        'mla_kva_g': gain((n_od, MLA_KV_LORA)),
        'mla_w_uq': nrm((n_od, MLA_Q_LORA, MLA_HEADS * MLA_QK), MLA_Q_LORA ** -0.5),
        'mla_w_ukv': nrm((n_od, MLA_KV_LORA, MLA_HEADS * (MLA_NOPE + MLA_V)), MLA_KV_LORA ** -0.5),
        'mla_q_g': gain((n_od, MLA_QK)),
        'mla_k_g': gain((n_od, MLA_QK)),
        'mla_w_o': nrm((n_od, MLA_HEADS * MLA_V, d), (MLA_HEADS * MLA_V) ** -0.5),
    }


def reference(x, c, ctx, c_ctx, ada_w, ada_b, norm1_g, norm2_g, router_w, moe_w_gate, moe_w_up, moe_w_down,
              ev_w_in, ev_w_out, na_q_g, na_k_g, na_rpb, hy_short_w, hy_short_b, hy_w1, hy_b1, hy_w2, hy_b2,
              hy_w3, hy_freq, hy_d, mla_w_down, mla_qa_g, mla_kva_g, mla_w_uq, mla_w_ukv, mla_q_g, mla_k_g,
              mla_w_o):
    sc = jax.nn.silu(c)
    sc_ctx = jax.nn.silu(c_ctx)
    for i in range(DEPTH):
        last = i == DEPTH - 1
        j = i // 2
        m_l = [m[:, None, :] for m in jnp.split(sc @ ada_w[i] + ada_b[i], N_MOD, axis=-1)]
        m_c = jnp.split(sc_ctx @ ada_w[i] + ada_b[i], N_MOD, axis=-1)
        h_l = modulate(rms_norm(x, norm1_g[i]), m_l[0], m_l[1])
        h_c = modulate(rms_norm(ctx, norm1_g[i]), m_c[0], m_c[1])
        if i % 2 == 0:
            y_l, y_c = even_mixer(h_l, h_c, ev_w_in[j], ev_w_out[j], na_q_g[j], na_k_g[j], na_rpb[j],
                                  hy_short_w[j], hy_short_b[j], hy_w1[j], hy_b1[j], hy_w2[j], hy_b2[j],
                                  hy_w3[j], hy_freq[j], hy_d[j], not last)
        else:
            y_l, y_c = mla_mixer(h_l, h_c, mla_w_down[j], mla_qa_g[j], mla_kva_g[j], mla_w_uq[j],
                                 mla_w_ukv[j], mla_q_g[j], mla_k_g[j], mla_w_o[j], not last)
        x = x + m_l[2] * y_l
        x = x + m_l[5] * expert_choice_ffn(modulate(rms_norm(x, norm2_g[i]), m_l[3], m_l[4]),
                                           router_w[i], moe_w_gate[i], moe_w_up[i], moe_w_down[i])
        if not last:
            ctx = ctx + m_c[2] * y_c
            ctx = ctx + m_c[5] * expert_choice_ffn(modulate(rms_norm(ctx, norm2_g[i]), m_c[3], m_c[4]),
                                                   router_w[i], moe_w_gate[i], moe_w_up[i], moe_w_down[i])
    return x
```

```python
import numpy as np
import ml_dtypes
import concourse.bass as bass
import concourse.mybir as mybir
from concourse.bass_utils import run_bass_kernel_spmd
from contextlib import ExitStack

F32 = mybir.dt.float32
BF16 = mybir.dt.bfloat16
I32 = mybir.dt.int32
U32 = mybir.dt.uint32
AF = mybir.ActivationFunctionType
ALU = mybir.AluOpType
AX = mybir.AxisListType
NPBF = ml_dtypes.bfloat16

D_MODEL = 2048
BATCH = 4
SEQ = 4096
DEPTH = 4
CTX = 256
NTOK = CTX + SEQ
NCORES = 8


class Res:
    __slots__ = ("name", "w", "r")

    def __init__(self, name=""):
        self.name = name
        self.w = None
        self.r = {}


class K:
    SEM_ROLL = 30000

    def __init__(self, nc, es, n_dma_sems=40):
        self.nc = nc
        self.es = es
        self.engs = {"pe": nc.tensor, "dve": nc.vector, "act": nc.scalar,
                     "pool": nc.gpsimd, "sp": nc.sync}
        self.sem = {}
        self.cnt = {}
        self.seen = {e: {} for e in self.engs}
        self.nsem = 0
        for e in self.engs:
            self._newsem(e)
        self.dsems = [es.enter_context(nc.semaphore("dq%d" % i)) for i in range(n_dma_sems)]
        self.dval = [0] * n_dma_sems
        self.dnext = 0
        self.ninstr = 0

    def _newsem(self, e):
        self.nsem += 1
        self.sem[e] = self.es.enter_context(self.nc.semaphore("s_%s_%d" % (e, self.nsem)))
        self.cnt[e] = 0

    def ev_wait(self, e, ev):
        sem, val = ev
        if self.seen[e].get(sem, 0) >= val:
            return
        self.engs[e].wait_ge(sem, val)
        self.seen[e][sem] = val

    def _deps(self, e, r, w, skip_same=False):
        deps = []
        for b in r:
            if b.w is not None:
                deps.append(b.w)
        for b in w:
            if b.w is not None:
                deps.append(b.w)
            for s, v in b.r.items():
                deps.append((s, v))
        for ev in deps:
            if skip_same and ev[0] is self.sem[e]:
                continue
            self.ev_wait(e, ev)

    def _mark(self, ev, r, w):
        for b in r:
            if b.r.get(ev[0], 0) < ev[1]:
                b.r[ev[0]] = ev[1]
        for b in w:
            b.w = ev
            b.r = {}

    def op(self, e, fn, r=(), w=(), skip_same=False):
        if e == "pe":
            skip_same = True
        self._deps(e, r, w, skip_same)
        if self.cnt[e] >= self.SEM_ROLL:
            self._newsem(e)
        ins = fn(self.engs[e])
        self.cnt[e] += 1
        ins.then_inc(self.sem[e], 1)
        ev = (self.sem[e], self.cnt[e])
        self._mark(ev, r, w)
        self.ninstr += 1
        return ev

    def dma(self, e, out, in_, r=(), w=(), indirect=None, **kw):
        self._deps(e, r, w)
        i = self.dnext
        self.dnext = (self.dnext + 1) % len(self.dsems)
        if self.dval[i] > 0:
            self.ev_wait(e, (self.dsems[i], self.dval[i]))
        if self.dval[i] >= self.SEM_ROLL:
            self.nsem += 1
            self.dsems[i] = self.es.enter_context(self.nc.semaphore("dq_r%d" % self.nsem))
            self.dval[i] = 0
        if indirect is not None:
            ins = self.engs[e].indirect_dma_start(out=out, in_=in_, **indirect)
        else:
            ins = self.engs[e].dma_start(out=out, in_=in_, **kw)
        self.dval[i] += 16
        ins.then_inc(self.dsems[i], 16)
        ev = (self.dsems[i], self.dval[i])
        self._mark(ev, r, w)
        self.ninstr += 1
        return ev

    def barrier(self):
        evs = [(self.sem[e], self.cnt[e]) for e in self.engs if self.cnt[e] > 0]
        evs += [(self.dsems[i], self.dval[i]) for i in range(len(self.dsems)) if self.dval[i] > 0]
        for e in self.engs:
            for ev in evs:
                if ev[0] is self.sem[e]:
                    continue
                self.ev_wait(e, ev)


class Prog:
    def __init__(self):
        self.nc = bass.Bass("TRN2", target_bir_lowering=False)
        self.es = ExitStack()
        self.k = K(self.nc, self.es)
        self.n = 0

    def din(self, name, shape, dt=F32):
        return self.nc.dram_tensor(name, list(shape), dt, kind="ExternalInput")

    def dout(self, name, shape, dt=F32):
        return self.nc.dram_tensor(name, list(shape), dt, kind="ExternalOutput")

    def dscr(self, name, shape, dt=F32):
        return self.nc.dram_tensor(name, list(shape), dt, kind="Internal")

    def sb(self, shape, dt=F32, name=None):
        self.n += 1
        return self.es.enter_context(self.nc.sbuf_tensor(name or ("sb%d" % self.n), list(shape), dt))

    def ps(self, shape, dt=F32, name=None):
        self.n += 1
        return self.es.enter_context(self.nc.psum_tensor(name or ("ps%d" % self.n), list(shape), dt))

    def finish(self):
        self.k.barrier()
        self.es.close()
        return self.nc


_LAUNCHES = []


def launch(nc, in_maps):
    res = run_bass_kernel_spmd(nc, in_maps, core_ids=list(range(NCORES)))
    return res.results


P0_COLS = 6 * D_MODEL // NCORES


def build_p0():
    P = Prog()
    nc, k = P.nc, P.k
    cT = P.din("cT", [128, 16, 5])
    w = P.din("w", [DEPTH, D_MODEL, P0_COLS])
    b = P.din("b", [DEPTH, 5, P0_COLS])
    o = P.dout("o", [DEPTH, 5, P0_COLS])
    sc = P.sb([128, 16, 5]); rsc = Res()
    k.dma("sp", sc[:], cT[:, :, :], w=[rsc])
    k.op("act", lambda e: e.activation(out=sc[:], in_=sc[:], func=AF.Silu), r=[rsc], w=[rsc])
    wt = [P.sb([128, 16, 512]) for _ in range(2)]; rw = [Res(), Res()]
    acc = [P.ps([128, 512]) for _ in range(2)]; racc = [Res(), Res()]
    bt = [P.sb([5, 512]) for _ in range(2)]; rb = [Res(), Res()]
    ot = [P.sb([5, 512]) for _ in range(2)]; ro = [Res(), Res()]
    it = 0
    for l in range(DEPTH):
        for nb in range(P0_COLS // 512):
            s = it % 2
            it += 1
            cs = slice(nb * 512, nb * 512 + 512)
            k.dma("sp", wt[s][:], w[l, :, cs].rearrange("(kc p) n -> p kc n", p=128), w=[rw[s]])
            k.dma("pool", bt[s][:], b[l, :, cs], w=[rb[s]])
            for kc in range(16):
                k.op("pe", lambda e: e.matmul(acc[s][0:5, :], lhsT=sc[:, kc, :], rhs=wt[s][:, kc, :],
                                               start=(kc == 0), stop=(kc == 15)),
                     r=[rsc, rw[s]], w=[racc[s]], skip_same=(kc > 0))
            k.op("dve", lambda e: e.tensor_tensor(out=ot[s][:], in0=acc[s][0:5, :], in1=bt[s][:], op=ALU.add),
                 r=[racc[s], rb[s]], w=[ro[s]])
            k.dma("sp", o[l, :, cs], ot[s][:], r=[ro[s]])
    return P.finish()


def run_p0(inp):
    c5 = np.concatenate([inp["c"], inp["c_ctx"][None]], 0)
    cT = np.ascontiguousarray(c5.T.reshape(16, 128, 5).transpose(1, 0, 2))
    nc = build_p0()
    maps = []
    for c in range(NCORES):
        cs = slice(c * P0_COLS, (c + 1) * P0_COLS)
        maps.append({"cT": cT,
                     "w": np.ascontiguousarray(inp["ada_w"][:, :, cs]),
                     "b": np.ascontiguousarray(np.broadcast_to(inp["ada_b"][:, None, cs], (DEPTH, 5, P0_COLS)))})
    res = launch(nc, maps)
    mods = np.concatenate([r["o"] for r in res], axis=2)
    return mods.reshape(DEPTH, 5, 6, D_MODEL)


NT = NTOK // 128
EPS = 1e-6
GRID = 64


def bcast_rows(ap_row, n=128):
    return ap_row.to_broadcast([n, ap_row.shape[-1]])


class TokT:
    def __init__(self, c, l):
        self.c, self.l = c, l

    def tile(self, t):
        if t < 2:
            return self.c[t * 128:(t + 1) * 128, :]
        return self.l[(t - 2) * 128:(t - 1) * 128, :]

    def part(self, name):
        return self.c if name == "C" else self.l


class Stage:
    def __init__(self, P):
        self.P = P

    def __enter__(self):
        self.saved = self.P.es
        self.P.es = ExitStack()
        return self.P

    def __exit__(self, *a):
        self.P.k.barrier()
        self.P.es.close()
        self.P.es = self.saved
        return False


def load_mod_AB(P, md, g, ia, ib):
    k = P.k
    gt = P.sb([128, D_MODEL]); rg = Res()
    k.dma("sp", gt[:], bcast_rows(g), w=[rg])
    A, B = [], []
    rAB = Res()
    for s in range(2):
        a = P.sb([128, D_MODEL]); b = P.sb([128, D_MODEL])
        k.dma("sp", a[:], bcast_rows(md[s, ia:ia + 1, :]), w=[rAB])
        k.dma("sp", b[:], bcast_rows(md[s, ib:ib + 1, :]), w=[rAB])
        k.op("dve", lambda e: e.scalar_tensor_tensor(out=a[:], in0=a[:], scalar=1.0, in1=gt[:],
                                                     op0=ALU.add, op1=ALU.mult), r=[rg, rAB], w=[rAB])
        A.append(a); B.append(b)
    return A, B, rAB


def rstd_from_ss(k, rstd, ss, n, r, w, extra=None):
    k.op("dve", lambda e: e.tensor_scalar(out=rstd, in0=ss, scalar1=1.0 / n, scalar2=EPS,
                                          op0=ALU.mult, op1=ALU.add), r=r, w=w)
    k.op("act", lambda e: e.activation(out=rstd, in_=rstd, func=AF.Sqrt), r=w, w=w)
    k.op("dve", lambda e: e.reciprocal(out=rstd, in_=rstd), r=w, w=w)


class NormT:
    def __init__(self, P, X, A, B, rAB, identb, rid):
        self.P, self.X, self.A, self.B, self.rAB, self.identb, self.rid = P, X, A, B, rAB, identb, rid
        self.xt = [P.sb([128, D_MODEL]) for _ in range(2)]; self.rx = [Res(), Res()]
        self.junk = P.sb([128, D_MODEL]); self.rj = Res()
        self.hb = [P.sb([128, D_MODEL], BF16) for _ in range(2)]; self.rh = [Res(), Res()]
        self.ss = P.sb([128, 2]); self.rss = Res()
        self.pT = P.ps([128, D_MODEL], BF16); self.rpT = Res()
        self.n = 0

    def load(self, t):
        s = t % 2
        self.P.k.dma("sp", self.xt[s][:], self.X.tile(t), w=[self.rx[s]])

    def run(self, tiles, hT, rhT):
        k = self.P.k
        self.load(tiles[0])
        for i, t in enumerate(tiles):
            s = t % 2
            if i + 1 < len(tiles):
                self.load(tiles[i + 1])
            xt, junk, ss, hb, pT = self.xt[s], self.junk, self.ss, self.hb[s], self.pT
            rx, rj, rss, rh, rpT = self.rx[s], self.rj, self.rss, self.rh[s], self.rpT
            m = 1 if t < 2 else 0
            k.op("act", lambda e: e.activation(out=junk[:], in_=xt[:], func=AF.Square), r=[rx], w=[rj])
            k.op("dve", lambda e: e.reduce_sum(out=ss[:, 0:1], in_=junk[:], axis=AX.X), r=[rj], w=[rss])
            rstd_from_ss(k, ss[:, 1:2], ss[:, 0:1], D_MODEL, [rss], [rss])
            k.op("dve", lambda e: e.scalar_tensor_tensor(out=junk[:], in0=xt[:], scalar=ss[:, 1:2], in1=self.A[m][:],
                                                         op0=ALU.mult, op1=ALU.mult), r=[rx, rss, self.rAB], w=[rj])
            k.op("dve", lambda e: e.tensor_tensor(out=hb[:], in0=junk[:], in1=self.B[m][:], op=ALU.add),
                 r=[rj, self.rAB], w=[rh])
            for kc in range(16):
                k.op("pe", lambda e: e.transpose(out=pT[:, kc * 128:(kc + 1) * 128], in_=hb[:, kc * 128:(kc + 1) * 128],
                                                 identity=self.identb[:]), r=[rh, self.rid], w=[rpT], skip_same=(kc > 0))
            k.op("act", lambda e: e.copy(out=hT[:, :, i * 128:(i + 1) * 128],
                                         in_=pT[:].rearrange("p (c t) -> p c t", c=16)), r=[rpT], w=[rhT])


def emit_e1(P, X, md, g1, w_in, qkg, identb_d, QT, KT, V, HYP):
    k = P.k
    with Stage(P):
        identb = P.sb([128, 128], BF16); rid = Res()
        k.dma("sp", identb[:], identb_d[:, :], w=[rid])
        gq = P.sb([128, 2, 512]); rgq = Res()
        for i in range(2):
            k.dma("sp", gq[:, i, :], bcast_rows(qkg[i:i + 1, :]), w=[rgq])
        HT = NT // 2
        hT = P.sb([128, 16, HT * 128], BF16); rhT = Res()
        wb = [P.sb([128, 16, 512], BF16) for _ in range(2)]; rw = [Res(), Res()]
        sq = P.sb([128, 512]); rsq = Res()
        s4 = P.sb([128, 8]); rs4 = Res()
        tmp = P.sb([128, 512]); rtmp = Res()
        qn = [P.sb([128, 512], BF16) for _ in range(2)]; rqn = [Res(), Res()]
        qT = [P.sb([128, 512], BF16) for _ in range(2)]; rqT = [Res(), Res()]
        of = [P.sb([128, 512]) for _ in range(2)]; rof = [Res(), Res()]
        acc = [P.ps([128, 512]) for _ in range(2)]; racc = [Res(), Res()]
        pq = [P.ps([128, 512], BF16) for _ in range(2)]; rpq = [Res(), Res()]
        for half in range(2):
            tiles = list(range(half * HT, (half + 1) * HT))
            with Stage(P):
                A, B, rAB = load_mod_AB(P, md, g1, 1, 0)
                NormT(P, X, A, B, rAB, identb, rid).run(tiles, hT, rhT)

            def loadw(nb):
                s = nb % 2
                k.dma("pool", wb[s][:], w_in[:, nb * 512:(nb + 1) * 512].rearrange("(kc p) n -> p kc n", p=128), w=[rw[s]])

            loadw(0)
            it = 0
            for nb in range(12):
                s = nb % 2
                if nb + 1 < 12:
                    loadw(nb + 1)
                for i, t in enumerate(tiles):
                    a = it % 2
                    it += 1
                    rows = slice(t * 128, (t + 1) * 128)
                    for kc in range(16):
                        k.op("pe", lambda e: e.matmul(acc[a][:], lhsT=hT[:, kc, i * 128:(i + 1) * 128], rhs=wb[s][:, kc, :],
                                                      start=(kc == 0), stop=(kc == 15)),
                             r=[rhT, rw[s]], w=[racc[a]], skip_same=(kc > 0))
                    if nb < 4:
                        gi = 0 if nb < 2 else 1
                        dst = QT if nb < 2 else KT
                        hb0 = (nb % 2) * 4
                        k.op("act", lambda e: e.activation(out=sq[:], in_=acc[a][:], func=AF.Square), r=[racc[a]], w=[rsq])
                        k.op("dve", lambda e: e.reduce_sum(out=s4[:, 0:4], in_=sq[:].rearrange("p (h d) -> p h d", h=4),
                                                           axis=AX.X), r=[rsq], w=[rs4])
                        rstd_from_ss(k, s4[:, 4:8], s4[:, 0:4], 128, [rs4], [rs4])
                        k.op("dve", lambda e: e.tensor_tensor(out=tmp[:].rearrange("p (h d) -> p h d", h=4),
                                                              in0=acc[a][:].rearrange("p (h d) -> p h d", h=4),
                                                              in1=s4[:, 4:8].unsqueeze(2).to_broadcast([128, 4, 128]),
                                                              op=ALU.mult), r=[racc[a], rs4], w=[rtmp])
                        k.op("dve", lambda e: e.tensor_tensor(out=qn[a][:], in0=tmp[:], in1=gq[:, gi, :], op=ALU.mult),
                             r=[rtmp, rgq], w=[rqn[a]])
                        for hh in range(4):
                            k.op("pe", lambda e: e.transpose(out=pq[a][:, hh * 128:(hh + 1) * 128],
                                                             in_=qn[a][:, hh * 128:(hh + 1) * 128], identity=identb[:]),
                                 r=[rqn[a], rid], w=[rpq[a]], skip_same=(hh > 0))
                        k.op("act", lambda e: e.copy(out=qT[a][:], in_=pq[a][:]), r=[rpq[a]], w=[rqT[a]])
                        k.dma("sp", dst[hb0 * 128:(hb0 + 4) * 128, rows].rearrange("(h d) t -> d h t", d=128),
                              qT[a][:].rearrange("p (h t) -> p h t", h=4), r=[rqT[a]])
                    elif nb < 6:
                        k.op("act", lambda e: e.copy(out=qn[a][:], in_=acc[a][:]), r=[racc[a]], w=[rqn[a]])
                        k.dma("sp", V[rows, (nb - 4) * 512:(nb - 3) * 512], qn[a][:], r=[rqn[a]])
                    else:
                        k.op("act", lambda e: e.copy(out=of[a][:], in_=acc[a][:]), r=[racc[a]], w=[rof[a]])
                        r0 = (1 + t * 128) if t < 2 else (259 + (t - 2) * 128)
                        k.dma("sp", HYP[r0:r0 + 128, (nb - 6) * 512:(nb - 5) * 512], of[a][:], r=[rof[a]])


def emit_p4(P, CATT, X, w_o, md, g2, rwd, identf_d, H2, AFFT):
    k = P.k
    with Stage(P):
        identf = P.sb([128, 128]); rid = Res()
        k.dma("sp", identf[:], identf_d[:, :], w=[rid])
        rw = P.sb([128, 16, 16]); rrw = Res()
        k.dma("sp", rw[:], rwd.rearrange("(kc p) e -> p kc e", p=128), w=[rrw])
        wo = P.sb([128, 16, D_MODEL], BF16); rwo = Res()
        for q in range(4):
            k.dma("pool", wo[:, :, q * 512:(q + 1) * 512],
                  w_o[:, q * 512:(q + 1) * 512].rearrange("(kc p) n -> p kc n", p=128), w=[rwo])
        A, B, rAB = load_mod_AB(P, md, g2, 4, 3)
        G = []
        rG = Res()
        for s in range(2):
            gt = P.sb([128, D_MODEL])
            k.dma("sp", gt[:], bcast_rows(md[s, 2:3, :]), w=[rG])
            G.append(gt)
        NB = 2
        ct = [P.sb([128, 16, 128], BF16) for _ in range(NB)]; rct = [Res() for _ in range(NB)]
        xm = [P.sb([128, D_MODEL]) for _ in range(NB)]; rxm = [Res() for _ in range(NB)]
        h2 = [P.sb([128, D_MODEL]) for _ in range(NB)]; rh2 = [Res() for _ in range(NB)]
        junk = P.sb([128, D_MODEL]); rj = Res()
        ss = P.sb([128, 2]); rss = Res()
        acc = [P.ps([128, 512]) for _ in range(2)]; racc = [Res(), Res()]
        pT = P.ps([128, D_MODEL]); rpT = Res()
        h2T = P.sb([128, 16, 128]); rh2T = Res()
        lg = P.ps([128, 16]); rlg = Res()
        sm = P.sb([128, 4]); rsm = Res()
        ex = P.sb([128, 16]); rex = Res()
        pA = P.ps([16, 128]); rpA = Res()
        aT = [P.sb([16, 128]) for _ in range(2)]; raT = [Res(), Res()]

        def loads(t):
            s = t % NB
            rows = slice(t * 128, (t + 1) * 128)
            k.dma("sp", ct[s][:], CATT[:, rows].rearrange("(kc p) t -> p kc t", p=128), w=[rct[s]])
            k.dma("sp", xm[s][:], X.tile(t), w=[rxm[s]])

        loads(0)
        it = 0
        for t in range(NT):
            s = t % NB
            m = 1 if t < 2 else 0
            rows = slice(t * 128, (t + 1) * 128)
            if t + 1 < NT:
                loads(t + 1)
            for nb in range(4):
                a = it % 2
                it += 1
                cs = slice(nb * 512, (nb + 1) * 512)
                for kc in range(16):
                    k.op("pe", lambda e: e.matmul(acc[a][:], lhsT=ct[s][:, kc, :], rhs=wo[:, kc, cs],
                                                  start=(kc == 0), stop=(kc == 15)),
                         r=[rct[s], rwo], w=[racc[a]], skip_same=(kc > 0))
                k.op("dve", lambda e: e.tensor_tensor(out=junk[:, cs], in0=acc[a][:], in1=G[m][:, cs], op=ALU.mult),
                     r=[racc[a], rG], w=[rj])
                k.op("pool", lambda e: e.tensor_tensor(out=xm[s][:, cs], in0=xm[s][:, cs], in1=junk[:, cs], op=ALU.add),
                     r=[rj], w=[rxm[s]])
            k.dma("sp", X.tile(t), xm[s][:], r=[rxm[s]])
            k.op("act", lambda e: e.activation(out=junk[:], in_=xm[s][:], func=AF.Square), r=[rxm[s]], w=[rj])
            k.op("dve", lambda e: e.reduce_sum(out=ss[:, 0:1], in_=junk[:], axis=AX.X), r=[rj], w=[rss])
            rstd_from_ss(k, ss[:, 1:2], ss[:, 0:1], D_MODEL, [rss], [rss])
            k.op("dve", lambda e: e.scalar_tensor_tensor(out=junk[:], in0=xm[s][:], scalar=ss[:, 1:2], in1=A[m][:],
                                                         op0=ALU.mult, op1=ALU.mult), r=[rxm[s], rss, rAB], w=[rj])
            k.op("dve", lambda e: e.tensor_tensor(out=h2[s][:], in0=junk[:], in1=B[m][:], op=ALU.add),
                 r=[rj, rAB], w=[rh2[s]])
            k.dma("sp", H2.tile(t), h2[s][:], r=[rh2[s]])
            for kc in range(16):
                k.op("pe", lambda e: e.transpose(out=pT[:, kc * 128:(kc + 1) * 128], in_=h2[s][:, kc * 128:(kc + 1) * 128],
                                                 identity=identf[:]), r=[rh2[s], rid], w=[rpT], skip_same=(kc > 0))
            k.op("act", lambda e: e.copy(out=h2T[:], in_=pT[:].rearrange("p (c t) -> p c t", c=16)), r=[rpT], w=[rh2T])
            for kc in range(16):
                k.op("pe", lambda e: e.matmul(lg[:], lhsT=h2T[:, kc, :], rhs=rw[:, kc, :], start=(kc == 0), stop=(kc == 15)),
                     r=[rh2T, rrw], w=[rlg], skip_same=(kc > 0))
            k.op("dve", lambda e: e.reduce_max(out=sm[:, 0:1], in_=lg[:], axis=AX.X), r=[rlg], w=[rsm])
            k.op("dve", lambda e: e.tensor_scalar(out=sm[:, 1:2], in0=sm[:, 0:1], scalar1=-1.0, scalar2=None, op0=ALU.mult),
                 r=[rsm], w=[rsm])
            k.op("act", lambda e: e.activation(out=ex[:], in_=lg[:], func=AF.Exp, bias=sm[:, 1:2], scale=1.0),
                 r=[rlg, rsm], w=[rex])
            k.op("dve", lambda e: e.reduce_sum(out=sm[:, 2:3], in_=ex[:], axis=AX.X), r=[rex], w=[rsm])
            k.op("dve", lambda e: e.reciprocal(out=sm[:, 3:4], in_=sm[:, 2:3]), r=[rsm], w=[rsm])
            k.op("dve", lambda e: e.tensor_scalar(out=ex[:], in0=ex[:], scalar1=sm[:, 3:4], scalar2=None, op0=ALU.mult),
                 r=[rsm, rex], w=[rex])
            k.op("pe", lambda e: e.transpose(out=pA[:], in_=ex[:], identity=identf[:]), r=[rex, rid], w=[rpA])
            k.op("act", lambda e: e.copy(out=aT[s][:], in_=pA[:]), r=[rpA], w=[raT[s]])
            k.dma("sp", AFFT[:, rows], aT[s][:], r=[raT[s]])


NE = 16
CAP_L, CAP_C = 512, 32


def emit_p5(P, AFFT, H2, X, wg_d, wu_d, wd_d, md, identb_d, scr):
    k = P.k
    idx_s = {"L": scr["idxL"], "C": scr["idxC"]}
    gat_s = {"L": scr["gatL"], "C": scr["gatC"]}
    with Stage(P):
        identb = P.sb([128, 128], BF16); rid = Res()
        k.dma("sp", identb[:], identb_d[:, :], w=[rid])
        M5 = []
        rM5 = Res()
        for s in range(2):
            mt = P.sb([128, D_MODEL])
            k.dma("sp", mt[:], bcast_rows(md[s, 5:6, :]), w=[rM5])
            M5.append(mt)
        idxT = {}; gatT = {}
        rIG = Res()
        for name, cap in (("L", CAP_L), ("C", CAP_C)):
            pp = min(128, cap)
            idxT[name] = P.sb([pp, NE, cap // pp], I32); gatT[name] = P.sb([pp, NE, cap // pp])
        with Stage(P):
            for name, c0, N, cap in (("L", CTX, SEQ, CAP_L), ("C", 0, CTX, CAP_C)):
                work = P.sb([NE, N]); rwk = Res()
                k.dma("sp", work[:], AFFT[:, c0:c0 + N], w=[rwk])
                mx = P.sb([NE, cap]); rmx = Res()
                ix = P.sb([NE, cap], U32); rix = Res()
                for itn in range(cap // 8):
                    sl = slice(itn * 8, itn * 8 + 8)
                    k.op("dve", lambda e: e.max(out=mx[:, sl], in_=work[:]), r=[rwk], w=[rmx])
                    k.op("dve", lambda e: e.max_index(out=ix[:, sl], in_max=mx[:, sl], in_values=work[:]), r=[rwk, rmx], w=[rix])
                    k.op("dve", lambda e: e.match_replace(out=work[:], in_to_replace=mx[:, sl], in_values=work[:], imm_value=0.0),
                         r=[rmx], w=[rwk])
                rs = Res()
                k.dma("sp", idx_s[name][:, :], ix[:], r=[rix], w=[rs])
                k.dma("sp", gat_s[name][:, :], mx[:], r=[rmx], w=[rs])
                pp = min(128, cap)
                nj = cap // pp
                for e_ in range(NE):
                    for j in range(nj):
                        k.dma("sp", idxT[name][:, e_, j:j + 1],
                              idx_s[name].bitcast(I32)[e_:e_ + 1, j * pp:(j + 1) * pp].rearrange("o p -> p o"), r=[rs], w=[rIG])
                        k.dma("sp", gatT[name][:, e_, j:j + 1],
                              gat_s[name][e_:e_ + 1, j * pp:(j + 1) * pp].rearrange("o p -> p o"), r=[rs], w=[rIG])

        wslot = [P.sb([128, 16, 512], BF16) for _ in range(4)]; rws = [Res() for _ in range(4)]
        dslot = [P.sb([128, 4, D_MODEL], BF16) for _ in range(2)]; rds = [Res() for _ in range(2)]
        xs = [P.sb([128, D_MODEL]) for _ in range(2)]; rxs = [Res(), Res()]
        xb = [P.sb([128, D_MODEL], BF16) for _ in range(2)]; rxb = [Res(), Res()]
        pT = P.ps([128, D_MODEL], BF16); rpT = Res()
        xsT = P.sb([128, 16, 512], BF16); rxsT = Res()
        pa = [P.ps([128, 512]) for _ in range(2)]; rpa = [Res(), Res()]
        pu = [P.ps([128, 512]) for _ in range(2)]; rpu = [Res(), Res()]
        pd = [P.ps([128, 512]) for _ in range(2)]; rpd = [Res(), Res()]
        sa = [P.sb([128, 512]) for _ in range(2)]; rsa = [Res(), Res()]
        hT = P.sb([128, 8, 512], BF16); rhT = Res()
        yt = [P.sb([128, D_MODEL]) for _ in range(2)]; ryt = [Res(), Res()]
        rX = Res()

        def loadw_gu(e_):
            for hf in range(2):
                k.dma("pool", wslot[2 * hf][:], wg_d[e_, :, hf * 512:(hf + 1) * 512].rearrange("(kc p) n -> p kc n", p=128),
                      w=[rws[2 * hf]])
                k.dma("pool", wslot[2 * hf + 1][:], wu_d[e_, :, hf * 512:(hf + 1) * 512].rearrange("(kc p) n -> p kc n", p=128),
                      w=[rws[2 * hf + 1]])

        def loadw_d(e_):
            for hf in range(2):
                k.dma("pool", dslot[hf][:], wd_d[e_, hf * 512:(hf + 1) * 512, :].rearrange("(fc p) n -> p fc n", p=128),
                      w=[rds[hf]])

        xsT_C = P.sb([128, 16, CAP_C], BF16); rxsT_C = Res()
        hT_C = P.sb([128, 8, CAP_C], BF16); rhT_C = Res()
        XS = {"L": (xsT, rxsT), "C": (xsT_C, rxsT_C)}
        HT = {"L": (hT, rhT), "C": (hT_C, rhT_C)}
        cnt = {"x": 0, "a": 0, "d": 0, "y": 0}

        def gather_T(e_, name):
            cap = CAP_L if name == "L" else CAP_C
            pp = min(128, cap)
            nj = cap // pp
            it_ = idxT[name]
            xT, rxT = XS[name]
            for j in range(nj):
                s = cnt["x"] % 2; cnt["x"] += 1
                k.dma("pool", xs[s][0:pp, :], H2.part(name)[:, :], r=[rIG], w=[rxs[s]],
                      indirect=dict(out_offset=None, in_offset=bass.IndirectOffsetOnAxis(ap=it_[:, e_, j:j + 1], axis=0)))
                k.op("act", lambda e: e.copy(out=xb[s][0:pp, :], in_=xs[s][0:pp, :]), r=[rxs[s]], w=[rxb[s]])
                for kc in range(16):
                    k.op("pe", lambda e: e.transpose(out=pT[:, kc * 128:kc * 128 + pp], in_=xb[s][0:pp, kc * 128:(kc + 1) * 128],
                                                     identity=identb[0:pp, 0:pp]), r=[rxb[s], rid], w=[rpT])
                k.op("dve", lambda e: e.tensor_copy(out=xT[:, :, j * pp:(j + 1) * pp],
                                                    in_=pT[:].rearrange("p (c t) -> p c t", c=16)[:, :, 0:pp]), r=[rpT], w=[rxT])

        def gate_up(e_):
            for hf in range(2):
                for fc in range(4):
                    fs = slice(fc * 128, (fc + 1) * 128)
                    for name in ("L", "C"):
                        cap = CAP_L if name == "L" else CAP_C
                        xT, rxT = XS[name]
                        hT_, rhT_ = HT[name]
                        a = cnt["a"] % 2; cnt["a"] += 1
                        for kc in range(16):
                            k.op("pe", lambda e: e.matmul(pa[a][:, 0:cap], lhsT=wslot[2 * hf][:, kc, fs], rhs=xT[:, kc, 0:cap],
                                                          start=(kc == 0), stop=(kc == 15)), r=[rws[2 * hf], rxT], w=[rpa[a]])
                        for kc in range(16):
                            k.op("pe", lambda e: e.matmul(pu[a][:, 0:cap], lhsT=wslot[2 * hf + 1][:, kc, fs], rhs=xT[:, kc, 0:cap],
                                                          start=(kc == 0), stop=(kc == 15)), r=[rws[2 * hf + 1], rxT], w=[rpu[a]])
                        k.op("act", lambda e: e.activation(out=sa[a][:, 0:cap], in_=pa[a][:, 0:cap], func=AF.Silu),
                             r=[rpa[a]], w=[rsa[a]])
                        k.op("dve", lambda e: e.tensor_tensor(out=hT_[:, hf * 4 + fc, 0:cap], in0=sa[a][:, 0:cap], in1=pu[a][:, 0:cap],
                                                              op=ALU.mult), r=[rsa[a], rpu[a]], w=[rhT_])

        def down_scatter(e_, name, m):
            cap = CAP_L if name == "L" else CAP_C
            pp = min(128, cap)
            nj = cap // pp
            it_, gt_ = idxT[name], gatT[name]
            hT_, rhT_ = HT[name]
            for j in range(nj):
                y = cnt["y"] % 2; cnt["y"] += 1
                for nb in range(4):
                    d = cnt["d"] % 2; cnt["d"] += 1
                    cs = slice(nb * 512, (nb + 1) * 512)
                    for fc in range(8):
                        k.op("pe", lambda e: e.matmul(pd[d][0:pp, :], lhsT=hT_[:, fc, j * pp:(j + 1) * pp],
                                                      rhs=dslot[fc // 4][:, fc % 4, cs], start=(fc == 0), stop=(fc == 7)),
                             r=[rhT_, rds[fc // 4]], w=[rpd[d]])
                    k.op("dve", lambda e: e.scalar_tensor_tensor(out=yt[y][0:pp, cs], in0=pd[d][0:pp, :],
                                                                 scalar=gt_[:, e_, j:j + 1], in1=M5[m][0:pp, cs],
                                                                 op0=ALU.mult, op1=ALU.mult),
                         r=[rpd[d], rIG, rM5], w=[ryt[y]])
                k.dma("pool", X.part(name)[:, :], yt[y][0:pp, :], r=[ryt[y], rIG], w=[rX],
                      indirect=dict(out_offset=bass.IndirectOffsetOnAxis(ap=it_[:, e_, j:j + 1], axis=0), in_offset=None,
                                    compute_op=ALU.add))

        loadw_gu(0)
        loadw_d(0)
        for e_ in range(NE):
            gather_T(e_, "L")
            gather_T(e_, "C")
            gate_up(e_)
            if e_ + 1 < NE:
                loadw_gu(e_ + 1)
            down_scatter(e_, "L", 0)
            down_scatter(e_, "C", 1)
            if e_ + 1 < NE:
                loadw_d(e_ + 1)


def emit_e2(P, QT, KT, V, BT, MK, CATT):
    k = P.k
    scale = 128 ** -0.5
    with Stage(P):
        ones = P.sb([128, 128], BF16); rones = Res()
        k.op("pool", lambda e: e.memset(ones[:], 1.0), w=[rones])
        mk = P.sb([128, 64]); rmk = Res()
        k.dma("sp", mk[0:64, :], MK[:, :], w=[rmk])
        k.dma("sp", mk[64:128, :], MK[:, :], w=[rmk])
        NBUF = 2
        KTh = [P.sb([128, NTOK], BF16) for _ in range(NBUF)]
        QTh = [P.sb([128, NTOK], BF16) for _ in range(NBUF)]
        Ve = [P.sb([128, 32, 128], BF16) for _ in range(NBUF)]
        Vo = [P.sb([128, 31, 128], BF16) for _ in range(NBUF)]
        Vc = [P.sb([128, 2, 128], BF16) for _ in range(NBUF)]
        TB = [P.sb([128, 14, 64]) for _ in range(NBUF)]
        osb = [P.sb([128, NTOK], BF16) for _ in range(NBUF)]
        rin = [Res() for _ in range(NBUF)]; rTB = [Res() for _ in range(NBUF)]; rosb = [Res() for _ in range(NBUF)]
        S = [P.ps([128, 512]) for _ in range(2)]; rS = [Res(), Res()]
        po = [P.ps([128, 512]) for _ in range(2)]; rpo = [Res(), Res()]
        tmp = [P.sb([128, 256]) for _ in range(2)]; rtmp = [Res(), Res()]
        Pm = [P.sb([128, 512], BF16) for _ in range(2)]; rPm = [Res(), Res()]
        rec = [P.sb([128, 256]) for _ in range(2)]; rrec = [Res(), Res()]

        def loadh(h):
            b = h % NBUF
            hc = slice(h * 128, (h + 1) * 128)
            k.dma("sp", KTh[b][:], KT[hc, :], w=[rin[b]])
            k.dma("sp", QTh[b][:], QT[hc, :], w=[rin[b]])
            k.dma("sp", Ve[b][:], V[CTX:NTOK, hc].rearrange("(j p) d -> p j d", p=128), w=[rin[b]])
            k.dma("sp", Vo[b][:], V[CTX + 64:CTX + 64 + 31 * 128, hc].rearrange("(j p) d -> p j d", p=128), w=[rin[b]])
            k.dma("sp", Vc[b][:], V[0:CTX, hc].rearrange("(j p) d -> p j d", p=128), w=[rin[b]])
            k.dma("sp", TB[b][0:64, :, :], BT[h, 0:14, :, :].rearrange("r k q -> k r q"), w=[rTB[b]])
            k.dma("sp", TB[b][64:128, :, :], BT[h, 1:15, :, :].rearrange("r k q -> k r q"), w=[rTB[b]])
            k.op("pool", lambda e: e.tensor_tensor(out=TB[b][:], in0=TB[b][:],
                                                   in1=mk[:].unsqueeze(1).to_broadcast([128, 14, 64]), op=ALU.add),
                 r=[rmk], w=[rTB[b]])

        cnt = [0]

        def attend1(b, q0, n, ktiles, bias):
            a = cnt[0] % 2; cnt[0] += 1
            nk = len(ktiles)
            Sv = S[a][:, 0:nk * n].rearrange("p (i q) -> p i q", i=nk)
            Pv = Pm[a][:, 0:nk * n].rearrange("p (i q) -> p i q", i=nk)
            for i, (kap, vap) in enumerate(ktiles):
                k.op("pe", lambda e: e.matmul(Sv[:, i, :], lhsT=kap, rhs=QTh[b][:, q0:q0 + n], start=True, stop=True),
                     r=[rin[b]], w=[rS[a]])
            nb_ = 0
            if bias is not None:
                nb_, bap = bias
                tv = tmp[a][:, 0:nb_ * n].rearrange("p (i q) -> p i q", i=nb_)
                k.op("dve", lambda e: e.scalar_tensor_tensor(out=tv, in0=Sv[:, 0:nb_, :], scalar=scale, in1=bap,
                                                             op0=ALU.mult, op1=ALU.add), r=[rS[a], rTB[b]], w=[rtmp[a]])
                k.op("act", lambda e: e.activation(out=Pv[:, 0:nb_, :], in_=tv, func=AF.Exp), r=[rtmp[a]], w=[rPm[a]])
            k.op("act", lambda e: e.activation(out=Pv[:, nb_:nk, :], in_=Sv[:, nb_:nk, :], func=AF.Exp, scale=scale),
                 r=[rS[a]], w=[rPm[a]])
            return (a, b, q0, n, ktiles)

        def attend2(st):
            a, b, q0, n, ktiles = st
            nk = len(ktiles)
            Pv = Pm[a][:, 0:nk * n].rearrange("p (i q) -> p i q", i=nk)
            pov = po[a][:, 0:2 * n].rearrange("p (i q) -> p i q", i=2)
            for i, (kap, vap) in enumerate(ktiles):
                k.op("pe", lambda e: e.matmul(pov[:, 0, :], lhsT=vap, rhs=Pv[:, i, :], start=(i == 0), stop=(i == nk - 1)),
                     r=[rin[b], rPm[a]], w=[rpo[a]])
            for i in range(nk):
                k.op("pe", lambda e: e.matmul(pov[:, 1, :], lhsT=ones[:], rhs=Pv[:, i, :], start=(i == 0), stop=(i == nk - 1)),
                     r=[rones, rPm[a]], w=[rpo[a]])
            k.op("dve", lambda e: e.reciprocal(out=rec[a][:, 0:n], in_=pov[:, 1, :]), r=[rpo[a]], w=[rrec[a]])
            k.op("dve", lambda e: e.tensor_tensor(out=osb[b][:, q0:q0 + n], in0=pov[:, 0, :], in1=rec[a][:, 0:n], op=ALU.mult),
                 r=[rpo[a], rrec[a]], w=[rosb[b]])

        loadh(0)
        for h in range(8):
            b = h % NBUF
            if h + 1 < 8:
                loadh(h + 1)
            ctxk = [(KTh[b][:, i * 128:(i + 1) * 128], Vc[b][:, i, :]) for i in range(2)]
            TB7 = TB[b][:].rearrange("p (a c) q -> p a c q", c=2)
            jobs = [(0, 256, ctxk, None)]
            for r in range(GRID):
                rs = min(max(r - 4, 0), GRID - 8)
                kt = []
                for i in range(4):
                    row = rs + 2 * i
                    tok = CTX + row * 64
                    vap = Ve[b][:, row // 2, :] if row % 2 == 0 else Vo[b][:, (row - 1) // 2, :]
                    kt.append((KTh[b][:, tok:tok + 128], vap))
                dr0 = rs - r + 7
                bap = TB7[:, dr0 // 2:dr0 // 2 + 4, dr0 % 2, :]
                jobs.append((CTX + r * 64, 64, kt + ctxk, (4, bap)))
            st = attend1(b, *jobs[0])
            for ji in range(len(jobs)):
                nxt = attend1(b, *jobs[ji + 1]) if ji + 1 < len(jobs) else None
                attend2(st)
                st = nxt
            k.dma("sp", CATT[h * 128:(h + 1) * 128, :], osb[b][:], r=[rosb[b]])


def emit_e3(P, HYP, sw, sbias, w1, b1, w2, b2, freq, w3, hyd, consts, identb_d, CATT):
    k = P.k
    with Stage(P):
        identb = P.sb([128, 128], BF16); rid = Res()
        k.dma("sp", identb[:], identb_d[:, :], w=[rid])
        ones = P.sb([128, 128]); rones = Res()
        k.op("pool", lambda e: e.memset(ones[:], 1.0), w=[rones])
        w1s = P.sb([33, 64]); w2s = P.sb([64, 64]); pr = P.sb([64, 8]); rpar = Res()
        k.dma("sp", w1s[:], w1[:, :], w=[rpar])
        k.dma("sp", w2s[:], w2[:, :], w=[rpar])
        k.dma("sp", pr[:, 0:1], b1[:, :], w=[rpar])
        k.dma("sp", pr[:, 1:2], b2[:, :], w=[rpar])
        k.dma("sp", pr[:, 2:3], freq[:, :], w=[rpar])
        k.op("dve", lambda e: e.tensor_scalar(out=pr[:, 3:4], in0=pr[:, 2:3], scalar1=1.0 / 3.0, scalar2=None, op0=ALU.mult),
             r=[rpar], w=[rpar])
        k.op("dve", lambda e: e.tensor_tensor(out=pr[:, 4:5], in0=pr[:, 3:4], in1=pr[:, 0:1], op=ALU.mult), r=[rpar], w=[rpar])
        k.op("dve", lambda e: e.tensor_tensor(out=pr[:, 5:6], in0=pr[:, 3:4], in1=pr[:, 1:2], op=ALU.mult), r=[rpar], w=[rpar])

        for (L, beta, tok0, cn) in ((CTX, 1, 0, consts["C"]), (SEQ, 259, CTX, consts["L"])):
            LT = L // 128
            bw = min(512, L)
            with Stage(P):
                hid2T = P.sb([64, L]); rh2 = Res()
                with Stage(P):
                    zT = P.sb([33, L]); rz = Res()
                    k.dma("sp", zT[:], cn["zT"][:, :], w=[rz])
                    hid1T = P.sb([64, L]); rh1 = Res()
                    pm = [P.ps([64, 512]) for _ in range(2)]; rpm = [Res(), Res()]
                    sn = P.sb([64, 512]); rsn = Res()
                    s2 = P.sb([64, 512]); rs2 = Res()
                    it = 0
                    for (lhs, src, rsrc, bcol, dst, rdst) in ((w1s, zT, rz, 4, hid1T, rh1), (w2s, hid1T, rh1, 5, hid2T, rh2)):
                        for nb in range(L // bw):
                            a = it % 2; it += 1
                            cs = slice(nb * bw, (nb + 1) * bw)
                            k.op("pe", lambda e: e.matmul(pm[a][:, 0:bw], lhsT=lhs[:], rhs=src[:, cs], start=True, stop=True),
                                 r=[rpar, rsrc], w=[rpm[a]])
                            k.op("act", lambda e: e.activation(out=sn[:, 0:bw], in_=pm[a][:, 0:bw], func=AF.Sin,
                                                               bias=pr[:, bcol:bcol + 1], scale=pr[:, 3:4]),
                                 r=[rpm[a], rpar], w=[rsn])
                            k.op("dve", lambda e: e.tensor_tensor(out=s2[:, 0:bw], in0=sn[:, 0:bw], in1=sn[:, 0:bw], op=ALU.mult),
                                 r=[rsn], w=[rs2])
                            k.op("dve", lambda e: e.tensor_scalar(out=s2[:, 0:bw], in0=s2[:, 0:bw], scalar1=-4.0, scalar2=3.0,
                                                                  op0=ALU.mult, op1=ALU.add), r=[rs2], w=[rs2])
                            k.op("dve", lambda e: e.tensor_tensor(out=dst[:, cs], in0=s2[:, 0:bw], in1=sn[:, 0:bw], op=ALU.mult),
                                 r=[rs2, rsn], w=[rdst])
                negt = P.sb([128, LT]); cst = P.sb([128, 3, LT]); rcn = Res()
                k.dma("sp", negt[:], cn["negt"][:, :], w=[rcn])
                k.dma("sp", cst[:], cn["cs"][:, :, :], w=[rcn])
                G2 = P.sb([128, LT, 512], BF16); rG = Res()
                U = P.sb([128, LT, 256], BF16); rU = Res()
                YR = P.sb([128, LT, 256], BF16); YI = P.sb([128, LT, 256], BF16); rY = Res()
                outT = P.sb([128, 2, L], BF16); routT = Res()
                Wc = P.sb([128, 3, 3, 256]); Bc = P.sb([128, 3, 256]); rWc = Res()
                dB = P.sb([128, 2, 256]); deltab = P.sb([128, 256]); rdB = Res()
                w3g = P.sb([64, 2, 256]); rw3 = Res()
                cv = [P.sb([128, 3, 256]) for _ in range(2)]; rcv = [Res(), Res()]
                c1 = P.sb([128, 3, 256]); rc1 = Res()
                xp = P.sb([128, 256]); rxp = Res()
                ccnt = [0]

                def conv_tile(part, g, nt):
                    s = ccnt[0] % 2; ccnt[0] += 1
                    col = part * 1024 + g * 256
                    r0 = beta + nt * 128
                    for j in range(3):
                        k.dma("sp", cv[s][:, j, :], HYP[r0 - 1 + j:r0 - 1 + j + 128, col:col + 256], w=[rcv[s]])
                    k.op("pool", lambda e: e.tensor_tensor(out=c1[:], in0=cv[s][:], in1=Wc[:, :, part, :], op=ALU.mult),
                         r=[rcv[s], rWc], w=[rc1])
                    k.op("dve", lambda e: e.tensor_tensor(out=c1[:, 0, :], in0=c1[:, 0, :], in1=c1[:, 1, :], op=ALU.add),
                         r=[rc1], w=[rc1])
                    k.op("dve", lambda e: e.tensor_tensor(out=c1[:, 2, :], in0=c1[:, 2, :], in1=Bc[:, part, :], op=ALU.add),
                         r=[rc1, rWc], w=[rc1])
                    k.op("dve", lambda e: e.tensor_tensor(out=xp[:], in0=c1[:, 0, :], in1=c1[:, 2, :], op=ALU.add),
                         r=[rc1], w=[rxp])

                for g in range(4):
                    gc = slice(g * 256, (g + 1) * 256)
                    for j in range(3):
                        k.dma("sp", Wc[:, j, :, :], sw[j:j + 1, :].rearrange("o (p c) -> o p c", p=3)[:, :, gc].to_broadcast([128, 3, 256]),
                              w=[rWc])
                    k.dma("sp", Bc[:], sbias[0:1, :].rearrange("o (p c) -> o p c", p=3)[:, :, gc].to_broadcast([128, 3, 256]), w=[rWc])
                    k.dma("sp", dB[:], hyd[:, gc].unsqueeze(0).to_broadcast([128, 2, 256]), w=[rdB])
                    k.dma("sp", deltab[:], bcast_rows(consts["deltas"][0:1, gc]), w=[rdB])
                    for o in range(2):
                        k.dma("sp", w3g[:], w3[:, 2 * o * 1024:(2 * o + 2) * 1024].rearrange("k (d c) -> k d c", d=2)[:, :, gc], w=[rw3])
                        with Stage(P):
                            pf = [P.ps([128, 512]) for _ in range(2)]; rpf = [Res(), Res()]
                            pss = P.ps([128, 512]); rpss = Res()
                            dec = [P.sb([128, 256]) for _ in range(2)]; rdec = [Res(), Res()]
                            fsb = [P.sb([128, 2, 256]) for _ in range(2)]; rfsb = [Res(), Res()]
                            sq = [P.sb([128, 512]) for _ in range(2)]; rsq = [Res(), Res()]
                            rstd = P.sb([128, 2, 256]); rrs = Res()
                            for ps_ in range(2):
                                for tt in range(LT):
                                    a = tt % 2
                                    k.op("pe", lambda e: e.matmul(pf[a][:], lhsT=hid2T[:, tt * 128:(tt + 1) * 128],
                                                                  rhs=w3g[:].rearrange("k d c -> k (d c)"), start=True, stop=True),
                                         r=[rh2, rw3], w=[rpf[a]])
                                    k.op("act", lambda e: e.activation(out=dec[a][:], in_=deltab[:], func=AF.Exp,
                                                                       scale=negt[:, tt:tt + 1]), r=[rdB, rcn], w=[rdec[a]])
                                    k.op("dve", lambda e: e.tensor_tensor(out=fsb[a][:], in0=pf[a][:].rearrange("p (d c) -> p d c", d=2),
                                                                          in1=dec[a][:].unsqueeze(1).to_broadcast([128, 2, 256]),
                                                                          op=ALU.mult), r=[rpf[a], rdec[a]], w=[rfsb[a]])
                                    if ps_ == 0:
                                        k.op("pool", lambda e: e.tensor_tensor(out=sq[a][:].rearrange("p (d c) -> p d c", d=2),
                                                                               in0=fsb[a][:], in1=fsb[a][:], op=ALU.mult),
                                             r=[rfsb[a]], w=[rsq[a]])
                                        k.op("pe", lambda e: e.matmul(pss[:], lhsT=ones[:], rhs=sq[a][:], start=(tt == 0),
                                                                      stop=(tt == LT - 1)), r=[rones, rsq[a]], w=[rpss],
                                             skip_same=(tt > 0))
                                    else:
                                        k.op("pool", lambda e: e.tensor_tensor(out=fsb[a][:], in0=fsb[a][:], in1=rstd[:], op=ALU.mult),
                                             r=[rrs], w=[rfsb[a]])
                                        k.op("dve", lambda e: e.tensor_tensor(out=G2[:, tt, 0:256], in0=fsb[a][:, 0, :], in1=fsb[a][:, 1, :],
                                                                              op=ALU.add), r=[rfsb[a]], w=[rG])
                                        k.op("pool", lambda e: e.tensor_tensor(out=G2[:, tt, 256:512], in0=fsb[a][:, 0, :], in1=fsb[a][:, 1, :],
                                                                               op=ALU.subtract), r=[rfsb[a]], w=[rG])
                                if ps_ == 0:
                                    rv = rstd[:].rearrange("p d c -> p (d c)")
                                    k.op("dve", lambda e: e.tensor_scalar(out=rv, in0=pss[:], scalar1=EPS, scalar2=None, op0=ALU.add),
                                         r=[rpss], w=[rrs])
                                    k.op("act", lambda e: e.activation(out=rv, in_=rv, func=AF.Sqrt), r=[rrs], w=[rrs])
                                    k.op("dve", lambda e: e.reciprocal(out=rv, in_=rv), r=[rrs], w=[rrs])
                        if o == 0:
                            for nt in range(LT):
                                conv_tile(0, g, nt)
                                k.op("act", lambda e: e.copy(out=U[:, nt, :], in_=xp[:]), r=[rxp], w=[rU])
                        with Stage(P):
                            Ct = [P.sb([128, LT, 128], BF16) for _ in range(2)]
                            St = [P.sb([128, LT, 128], BF16) for _ in range(2)]
                            rCS = [Res(), Res()]
                            bA = [P.ps([128, 512]) for _ in range(2)]; bB = [P.ps([128, 512]) for _ in range(2)]
                            bC = [P.ps([128, 256]) for _ in range(2)]; bS = [P.ps([128, 256]) for _ in range(2)]
                            raccs = [Res(), Res()]
                            PuS = P.sb([128, 256]); QuS = P.sb([128, 256]); rPQ = Res()
                            t1 = P.sb([128, 256]); t2 = P.sb([128, 256]); rt = Res()
                            Hre = P.sb([128, 256]); Him = P.sb([128, 256]); rH = Res()
                            a1 = P.sb([128, 256]); a2 = P.sb([128, 256]); ra = Res()
                            b1_ = P.sb([128, 256]); b2_ = P.sb([128, 256]); rb = Res()

                            def loadcs(ft):
                                s = ft % 2
                                k.dma("sp", Ct[s][:], cn["C2"][ft, :, :, :], w=[rCS[s]])
                                k.dma("sp", St[s][:], cn["S2"][ft, :, :, :], w=[rCS[s]])

                            loadcs(0)
                            for ft in range(LT):
                                s = ft % 2
                                racc = raccs[s]
                                acc = [bA[s][:, 0:256], bB[s][:, 0:256], bA[s][:, 256:512], bB[s][:, 256:512], bC[s][:], bS[s][:]]
                                if ft + 1 < LT:
                                    loadcs(ft + 1)
                                for tt in range(LT):
                                    st_ = (tt == 0); sp_ = (tt == LT - 1)
                                    k.op("pe", lambda e: e.matmul(bA[s][:], lhsT=Ct[s][:, tt, :], rhs=G2[:, tt, :], start=st_, stop=sp_),
                                         r=[rCS[s], rG], w=[racc])
                                    k.op("pe", lambda e: e.matmul(bC[s][:], lhsT=Ct[s][:, tt, :], rhs=U[:, tt, :], start=st_, stop=sp_),
                                         r=[rCS[s], rU], w=[racc])
                                    k.op("pe", lambda e: e.matmul(bB[s][:], lhsT=St[s][:, tt, :], rhs=G2[:, tt, :], start=st_, stop=sp_),
                                         r=[rCS[s], rG], w=[racc])
                                    k.op("pe", lambda e: e.matmul(bS[s][:], lhsT=St[s][:, tt, :], rhs=U[:, tt, :], start=st_, stop=sp_),
                                         r=[rCS[s], rU], w=[racc])
                                cc = cst[:, 0, ft:ft + 1]; ss_ = cst[:, 1, ft:ft + 1]; nc_ = cst[:, 2, ft:ft + 1]
                                k.op("act", lambda e: e.copy(out=PuS[:], in_=acc[4][:]), r=[racc], w=[rPQ])
                                k.op("act", lambda e: e.copy(out=QuS[:], in_=acc[5][:]), r=[racc], w=[rPQ])
                                k.op("dve", lambda e: e.tensor_scalar(out=t1[:], in0=acc[0][:], scalar1=cc, scalar2=None, op0=ALU.mult),
                                     r=[racc, rcn], w=[rt])
                                k.op("dve", lambda e: e.scalar_tensor_tensor(out=t1[:], in0=acc[1][:], scalar=ss_, in1=t1[:],
                                                                             op0=ALU.mult, op1=ALU.add), r=[racc, rcn], w=[rt])
                                k.op("dve", lambda e: e.tensor_scalar(out=t2[:], in0=acc[2][:], scalar1=ss_, scalar2=None, op0=ALU.mult),
                                     r=[racc, rcn], w=[rt])
                                k.op("dve", lambda e: e.scalar_tensor_tensor(out=Him[:], in0=acc[3][:], scalar=nc_, in1=t2[:],
                                                                             op0=ALU.mult, op1=ALU.add), r=[racc, rcn, rt], w=[rH])
                                k.op("pool", lambda e: e.tensor_tensor(out=Hre[:], in0=t1[:], in1=dB[:, o, :], op=ALU.add),
                                     r=[rt, rdB], w=[rH])
                                k.op("dve", lambda e: e.tensor_tensor(out=a1[:], in0=Hre[:], in1=PuS[:], op=ALU.mult), r=[rH, rPQ], w=[ra])
                                k.op("dve", lambda e: e.tensor_tensor(out=a2[:], in0=Him[:], in1=QuS[:], op=ALU.mult), r=[rH, rPQ], w=[ra])
                                k.op("dve", lambda e: e.tensor_tensor(out=YR[:, ft, :], in0=a1[:], in1=a2[:], op=ALU.add), r=[ra], w=[rY])
                                k.op("pool", lambda e: e.tensor_tensor(out=b1_[:], in0=Hre[:], in1=QuS[:], op=ALU.mult), r=[rH, rPQ], w=[rb])
                                k.op("pool", lambda e: e.tensor_tensor(out=b2_[:], in0=Him[:], in1=PuS[:], op=ALU.mult), r=[rH, rPQ], w=[rb])
                                k.op("pool", lambda e: e.tensor_tensor(out=YI[:, ft, :], in0=b1_[:], in1=b2_[:], op=ALU.subtract),
                                     r=[rb], w=[rY])
                        with Stage(P):
                            Ct = [P.sb([128, LT, 128], BF16) for _ in range(2)]
                            St = [P.sb([128, LT, 128], BF16) for _ in range(2)]
                            rCS = [Res(), Res()]
                            py = [P.ps([128, 256]) for _ in range(2)]; rpy = [Res(), Res()]
                            zb = [P.sb([128, 256], BF16) for _ in range(2)]; rzb = [Res(), Res()]
                            pz = [P.ps([128, 256], BF16) for _ in range(2)]; rpz = [Res(), Res()]

                            def loadcs2(nt):
                                s = nt % 2
                                k.dma("sp", Ct[s][:], cn["C2"][nt, :, :, :], w=[rCS[s]])
                                k.dma("sp", St[s][:], cn["S2"][nt, :, :, :], w=[rCS[s]])

                            loadcs2(0)
                            for nt in range(LT):
                                s = nt % 2
                                if nt + 1 < LT:
                                    loadcs2(nt + 1)
                                for ft in range(LT):
                                    k.op("pe", lambda e: e.matmul(py[s][:], lhsT=Ct[s][:, ft, :], rhs=YR[:, ft, :],
                                                                  start=(ft == 0), stop=False), r=[rCS[s], rY], w=[rpy[s]],
                                         skip_same=(ft > 0))
                                    k.op("pe", lambda e: e.matmul(py[s][:], lhsT=St[s][:, ft, :], rhs=YI[:, ft, :],
                                                                  start=False, stop=(ft == LT - 1)), r=[rCS[s], rY], w=[rpy[s]],
                                         skip_same=True)
                                conv_tile(o + 1, g, nt)
                                if o == 0:
                                    k.op("dve", lambda e: e.scalar_tensor_tensor(out=U[:, nt, :], in0=py[s][:], scalar=1.0 / L, in1=xp[:],
                                                                                 op0=ALU.mult, op1=ALU.mult), r=[rpy[s], rxp], w=[rU])
                                else:
                                    k.op("dve", lambda e: e.scalar_tensor_tensor(out=zb[s][:], in0=py[s][:], scalar=1.0 / L, in1=xp[:],
                                                                                 op0=ALU.mult, op1=ALU.mult), r=[rpy[s], rxp], w=[rzb[s]])
                                    for hh in range(2):
                                        k.op("pe", lambda e: e.transpose(out=pz[s][:, hh * 128:(hh + 1) * 128],
                                                                         in_=zb[s][:, hh * 128:(hh + 1) * 128], identity=identb[:]),
                                             r=[rzb[s], rid], w=[rpz[s]], skip_same=(hh > 0))
                                    k.op("act", lambda e: e.copy(out=outT[:, :, nt * 128:(nt + 1) * 128],
                                                                 in_=pz[s][:].rearrange("p (h t) -> p h t", h=2)), r=[rpz[s]], w=[routT])
                            if o == 1:
                                for hh in range(2):
                                    r0 = 1024 + g * 256 + hh * 128
                                    k.dma("sp", CATT[r0:r0 + 128, tok0:tok0 + L], outT[:, hh, :], r=[routT])


def emit_m1(P, X, md, g1, w_down, qag, kvag, w_uq, w_ukv, qg, kg, rope, identb_d, QAT, KR, QT2, KT2, V2):
    k = P.k
    with Stage(P):
        identb = P.sb([128, 128], BF16); rid = Res()
        k.dma("sp", identb[:], identb_d[:, :], w=[rid])
        wdn = P.sb([128, 16, 1344], BF16); rwd = Res()
        for (c0, c1) in ((0, 512), (512, 1024), (1024, 1344)):
            k.dma("pool", wdn[:, :, c0:c1], w_down[:, c0:c1].rearrange("(kc p) n -> p kc n", p=128), w=[rwd])
        gq = P.sb([128, 1280]); rgq = Res()
        k.dma("sp", gq[:, 0:768], bcast_rows(qag), w=[rgq])
        k.dma("sp", gq[:, 768:1280], bcast_rows(kvag), w=[rgq])
        A, B, rAB = load_mod_AB(P, md, g1, 1, 0)
        nt_ = NormT(P, X, A, B, rAB, identb, rid)
        hT = P.sb([128, 16, 128], BF16); rhT = Res()
        acc = [P.ps([128, 512]) for _ in range(2)]; racc = [Res(), Res()]
        da = P.sb([128, 1344]); rda = Res()
        sq = P.sb([128, 1280]); rsq = Res()
        s2 = P.sb([128, 4]); rs2 = Res()
        nb_ = [P.sb([128, 1280], BF16) for _ in range(2)]; rnb = [Res(), Res()]
        pT = P.ps([128, 1280], BF16); rpT = Res()
        nT = [P.sb([128, 10, 128], BF16) for _ in range(2)]; rnT = [Res(), Res()]
        it = 0
        for t in range(NT):
            s = t % 2
            rows = slice(t * 128, (t + 1) * 128)
            nt_.run([t], hT, rhT)
            for (c0, c1) in ((0, 512), (512, 1024), (1024, 1344)):
                a = it % 2; it += 1
                for kc in range(16):
                    k.op("pe", lambda e: e.matmul(acc[a][:, 0:c1 - c0], lhsT=hT[:, kc, :], rhs=wdn[:, kc, c0:c1],
                                                  start=(kc == 0), stop=(kc == 15)), r=[rhT, rwd], w=[racc[a]], skip_same=(kc > 0))
                k.op("act", lambda e: e.copy(out=da[:, c0:c1], in_=acc[a][:, 0:c1 - c0]), r=[racc[a]], w=[rda])
            k.dma("sp", KR[rows, :], da[:, 1280:1344], r=[rda])
            k.op("act", lambda e: e.activation(out=sq[:], in_=da[:, 0:1280], func=AF.Square), r=[rda], w=[rsq])
            k.op("dve", lambda e: e.reduce_sum(out=s2[:, 0:1], in_=sq[:, 0:768], axis=AX.X), r=[rsq], w=[rs2])
            k.op("dve", lambda e: e.reduce_sum(out=s2[:, 1:2], in_=sq[:, 768:1280], axis=AX.X), r=[rsq], w=[rs2])
            rstd_from_ss(k, s2[:, 2:3], s2[:, 0:1], 768, [rs2], [rs2])
            rstd_from_ss(k, s2[:, 3:4], s2[:, 1:2], 512, [rs2], [rs2])
            k.op("dve", lambda e: e.scalar_tensor_tensor(out=nb_[s][:, 0:768], in0=da[:, 0:768], scalar=s2[:, 2:3], in1=gq[:, 0:768],
                                                         op0=ALU.mult, op1=ALU.mult), r=[rda, rs2, rgq], w=[rnb[s]])
            k.op("dve", lambda e: e.scalar_tensor_tensor(out=nb_[s][:, 768:1280], in0=da[:, 768:1280], scalar=s2[:, 3:4],
                                                         in1=gq[:, 768:1280], op0=ALU.mult, op1=ALU.mult),
                 r=[rda, rs2, rgq], w=[rnb[s]])
            for c in range(10):
                k.op("pe", lambda e: e.transpose(out=pT[:, c * 128:(c + 1) * 128], in_=nb_[s][:, c * 128:(c + 1) * 128],
                                                 identity=identb[:]), r=[rnb[s], rid], w=[rpT], skip_same=(c > 0))
            k.op("act", lambda e: e.copy(out=nT[s][:], in_=pT[:].rearrange("p (c t) -> p c t", c=10)), r=[rpT], w=[rnT[s]])
            k.dma("sp", QAT[:, rows].rearrange("(c p) t -> p c t", p=128), nT[s][:], r=[rnT[s]])
    with Stage(P):
        identb = P.sb([128, 128], BF16); rid = Res()
        k.dma("sp", identb[:], identb_d[:, :], w=[rid])
        wuq = P.sb([128, 6, 3072], BF16); wukv = P.sb([128, 4, 4096], BF16); rwu = Res()
        for q in range(6):
            k.dma("pool", wuq[:, :, q * 512:(q + 1) * 512], w_uq[:, q * 512:(q + 1) * 512].rearrange("(kc p) n -> p kc n", p=128), w=[rwu])
        for q in range(8):
            k.dma("pool", wukv[:, :, q * 512:(q + 1) * 512], w_ukv[:, q * 512:(q + 1) * 512].rearrange("(kc p) n -> p kc n", p=128), w=[rwu])
        gt = P.sb([128, 2, 192]); rgt = Res()
        k.dma("sp", gt[:, 0, :], bcast_rows(qg), w=[rgt])
        k.dma("sp", gt[:, 1, :], bcast_rows(kg), w=[rgt])
        nT = [P.sb([128, 10, 128], BF16) for _ in range(2)]
        kr = [P.sb([128, 64]) for _ in range(2)]
        rp = [P.sb([128, 3, 32]) for _ in range(2)]
        rin = [Res(), Res()]
        acc = [P.ps([128, 512]) for _ in range(2)]; racc = [Res(), Res()]
        sq = P.sb([128, 512]); rsq = Res()
        sk = P.sb([128, 16]); rsk = Res()
        qf = P.sb([128, 2, 192]); rqf = Res()
        kf = P.sb([128, 2, 64]); rkf = Res()
        tA = P.sb([128, 2, 2, 16]); tB = P.sb([128, 2, 2, 16]); rtAB = Res()
        qb = [P.sb([128, 2, 192], BF16) for _ in range(2)]; rqb = [Res(), Res()]
        vb = [P.sb([128, 2, 128], BF16) for _ in range(2)]; rvb = [Res(), Res()]
        pq = [P.ps([128, 4, 128], BF16) for _ in range(2)]; rpq = [Res(), Res()]
        qTs = [P.sb([128, 4, 128], BF16) for _ in range(2)]; rqTs = [Res(), Res()]

        def loadt(t):
            s = t % 2
            rows = slice(t * 128, (t + 1) * 128)
            k.dma("sp", nT[s][:], QAT[:, rows].rearrange("(c p) t -> p c t", p=128), w=[rin[s]])
            k.dma("sp", kr[s][:], KR[rows, :], w=[rin[s]])
            k.dma("sp", rp[s][:], rope[rows, :, :], w=[rin[s]])

        def do_rope(src, dstb, s, r, w):
            for g in range(2):
                x = src[:, :, g * 32:(g + 1) * 32].rearrange("p h (s d) -> p h s d", s=2)
                o = dstb[:, :, g * 32:(g + 1) * 32].rearrange("p h (s d) -> p h s d", s=2)
                cg = rp[s][:, 0, g * 16:(g + 1) * 16]
                sg = rp[s][:, 1, g * 16:(g + 1) * 16]
                ng = rp[s][:, 2, g * 16:(g + 1) * 16]
                k.op("dve", lambda e: e.tensor_tensor(out=tA[:], in0=x, in1=cg.unsqueeze(1).unsqueeze(1).to_broadcast([128, 2, 2, 16]),
                                                      op=ALU.mult), r=r + [rin[s]], w=[rtAB])
                k.op("dve", lambda e: e.tensor_tensor(out=tB[:, :, 0, :], in0=x[:, :, 1, :],
                                                      in1=ng.unsqueeze(1).to_broadcast([128, 2, 16]), op=ALU.mult), r=r + [rin[s]], w=[rtAB])
                k.op("dve", lambda e: e.tensor_tensor(out=tB[:, :, 1, :], in0=x[:, :, 0, :],
                                                      in1=sg.unsqueeze(1).to_broadcast([128, 2, 16]), op=ALU.mult), r=r + [rin[s]], w=[rtAB])
                k.op("dve", lambda e: e.tensor_tensor(out=o, in0=tA[:], in1=tB[:], op=ALU.add), r=[rtAB], w=w)

        def transpose_store(srcb, rsrc, dstT, h0, rows, a):
            for hh in range(2):
                k.op("pe", lambda e: e.transpose(out=pq[a][:, 2 * hh, :], in_=srcb[:, hh, 0:128], identity=identb[:]),
                     r=[rsrc, rid], w=[rpq[a]], skip_same=(hh > 0))
                k.op("pe", lambda e: e.transpose(out=pq[a][0:64, 2 * hh + 1, :], in_=srcb[:, hh, 128:192], identity=identb[:]),
                     r=[rsrc, rid], w=[rpq[a]], skip_same=True)
            qv = qTs[a][:].rearrange("p (h c) t -> p h c t", c=2)
            pv = pq[a][:].rearrange("p (h c) t -> p h c t", c=2)
            k.op("act", lambda e: e.copy(out=qv[:, :, 0, :], in_=pv[:, :, 0, :]), r=[rpq[a]], w=[rqTs[a]])
            k.op("act", lambda e: e.copy(out=qv[0:64, :, 1, :], in_=pv[0:64, :, 1, :]), r=[rpq[a]], w=[rqTs[a]])
            for hh in range(2):
                h = h0 + hh
                k.dma("sp", dstT[h * 192:h * 192 + 128, rows], qTs[a][:, 2 * hh, :], r=[rqTs[a]])
                k.dma("sp", dstT[h * 192 + 128:h * 192 + 192, rows], qTs[a][0:64, 2 * hh + 1, :], r=[rqTs[a]])

        loadt(0)
        it = 0
        for t in range(NT):
            s = t % 2
            rows = slice(t * 128, (t + 1) * 128)
            if t + 1 < NT:
                loadt(t + 1)
            k.op("act", lambda e: e.activation(out=sq[:, 0:64], in_=kr[s][:], func=AF.Square), r=[rin[s]], w=[rsq])
            k.op("dve", lambda e: e.reduce_sum(out=sk[:, 0:1], in_=sq[:, 0:64], axis=AX.X), r=[rsq], w=[rsk])
            for blk in range(8):
                a = it % 2; it += 1
                cs = slice(blk * 384, (blk + 1) * 384)
                for kc in range(6):
                    k.op("pe", lambda e: e.matmul(acc[a][:, 0:384], lhsT=nT[s][:, kc, :], rhs=wuq[:, kc, cs],
                                                  start=(kc == 0), stop=(kc == 5)), r=[rin[s], rwu], w=[racc[a]], skip_same=(kc > 0))
                av = acc[a][:, 0:384].rearrange("p (h d) -> p h d", h=2)
                k.op("act", lambda e: e.activation(out=sq[:, 0:384], in_=acc[a][:, 0:384], func=AF.Square), r=[racc[a]], w=[rsq])
                k.op("dve", lambda e: e.reduce_sum(out=sk[:, 2:4], in_=sq[:, 0:384].rearrange("p (h d) -> p h d", h=2), axis=AX.X),
                     r=[rsq], w=[rsk])
                rstd_from_ss(k, sk[:, 4:6], sk[:, 2:4], 192, [rsk], [rsk])
                k.op("dve", lambda e: e.tensor_tensor(out=qf[:], in0=av, in1=sk[:, 4:6].unsqueeze(2).to_broadcast([128, 2, 192]),
                                                      op=ALU.mult), r=[racc[a], rsk], w=[rqf])
                k.op("dve", lambda e: e.tensor_tensor(out=qf[:], in0=qf[:], in1=gt[:, 0:1, :].to_broadcast([128, 2, 192]),
                                                      op=ALU.mult), r=[rgt], w=[rqf])
                k.op("act", lambda e: e.copy(out=qb[a][:, :, 0:128], in_=qf[:, :, 0:128]), r=[rqf], w=[rqb[a]])
                do_rope(qf[:, :, 128:192], qb[a][:, :, 128:192], s, [rqf], [rqb[a]])
                transpose_store(qb[a][:], rqb[a], QT2, 2 * blk, rows, a)
            for blk in range(8):
                a = it % 2; it += 1
                cs = slice(blk * 512, (blk + 1) * 512)
                for kc in range(4):
                    k.op("pe", lambda e: e.matmul(acc[a][:], lhsT=nT[s][:, 6 + kc, :], rhs=wukv[:, kc, cs],
                                                  start=(kc == 0), stop=(kc == 3)), r=[rin[s], rwu], w=[racc[a]], skip_same=(kc > 0))
                av = acc[a][:].rearrange("p (h c d) -> p h c d", h=2, c=2)
                k.op("act", lambda e: e.activation(out=sq[:], in_=acc[a][:], func=AF.Square), r=[racc[a]], w=[rsq])
                k.op("dve", lambda e: e.reduce_sum(out=sk[:, 8:12], in_=sq[:].rearrange("p (j d) -> p j d", j=4), axis=AX.X),
                     r=[rsq], w=[rsk])
                k.op("dve", lambda e: e.tensor_scalar(out=sk[:, 2:4], in0=sk[:, 8:12].rearrange("p (h c) -> p h c", c=2)[:, :, 0],
                                                      scalar1=sk[:, 0:1], scalar2=None, op0=ALU.add), r=[rsk], w=[rsk])
                rstd_from_ss(k, sk[:, 4:6], sk[:, 2:4], 192, [rsk], [rsk])
                k.op("dve", lambda e: e.tensor_tensor(out=qf[:, :, 0:128], in0=av[:, :, 0, :],
                                                      in1=sk[:, 4:6].unsqueeze(2).to_broadcast([128, 2, 128]), op=ALU.mult),
                     r=[racc[a], rsk], w=[rqf])
                k.op("dve", lambda e: e.tensor_tensor(out=qb[a][:, :, 0:128], in0=qf[:, :, 0:128],
                                                      in1=gt[:, 1:2, 0:128].to_broadcast([128, 2, 128]), op=ALU.mult),
                     r=[rgt], w=[rqf, rqb[a]])
                k.op("dve", lambda e: e.tensor_tensor(out=kf[:], in0=kr[s][:].unsqueeze(1).to_broadcast([128, 2, 64]),
                                                      in1=sk[:, 4:6].unsqueeze(2).to_broadcast([128, 2, 64]), op=ALU.mult),
                     r=[rin[s], rsk], w=[rkf])
                k.op("dve", lambda e: e.tensor_tensor(out=kf[:], in0=kf[:], in1=gt[:, 1:2, 128:192].to_broadcast([128, 2, 64]),
                                                      op=ALU.mult), r=[rgt], w=[rkf])
                do_rope(kf[:], qb[a][:, :, 128:192], s, [rkf], [rqb[a]])
                k.op("act", lambda e: e.copy(out=vb[a][:], in_=av[:, :, 1, :]), r=[racc[a]], w=[rvb[a]])
                k.dma("sp", V2[rows, blk * 256:(blk + 1) * 256], vb[a][:].rearrange("p h d -> p (h d)"), r=[rvb[a]])
                transpose_store(qb[a][:], rqb[a], KT2, 2 * blk, rows, a)


def emit_m2(P, QT2, KT2, V2, CATT):
    k = P.k
    scale = 192 ** -0.5
    with Stage(P):
        ones = P.sb([128, 128]); rones = Res()
        k.op("pool", lambda e: e.memset(ones[:], 1.0), w=[rones])
        KA = [P.sb([128, NTOK], BF16) for _ in range(2)]; KB = [P.sb([64, NTOK], BF16) for _ in range(2)]
        QA = [P.sb([128, NTOK], BF16) for _ in range(2)]; QB = [P.sb([64, NTOK], BF16) for _ in range(2)]
        Vh = [P.sb([128, NT, 128], BF16) for _ in range(2)]
        rin = [Res(), Res()]
        S = [P.ps([128, 512]) for _ in range(2)]; rS = [Res(), Res()]
        po = [P.ps([128, 512]) for _ in range(2)]; rpo = [Res(), Res()]
        pd = [P.ps([128, 512]) for _ in range(2)]; rpd = [Res(), Res()]
        Pm = [P.sb([128, 512], BF16) for _ in range(4)]; rPm = [Res() for _ in range(4)]
        dsum = [[P.sb([128, 512]) for _ in range(2)] for _ in range(2)]; rds = [[Res(), Res()], [Res(), Res()]]
        rec = P.sb([128, 512]); rrec = Res()
        osb = [P.sb([128, 512], BF16) for _ in range(2)]; rosb = [Res(), Res()]

        def loadh(h):
            b = h % 2
            k.dma("sp", KA[b][:], KT2[h * 192:h * 192 + 128, :], w=[rin[b]])
            k.dma("sp", KB[b][:], KT2[h * 192 + 128:h * 192 + 192, :], w=[rin[b]])
            k.dma("sp", QA[b][:], QT2[h * 192:h * 192 + 128, :], w=[rin[b]])
            k.dma("sp", QB[b][:], QT2[h * 192 + 128:h * 192 + 192, :], w=[rin[b]])
            k.dma("sp", Vh[b][:], V2[:, h * 128:(h + 1) * 128].rearrange("(j p) d -> p j d", p=128), w=[rin[b]])

        cnt = {"s": 0, "p": 0, "o": 0}
        loadh(0)
        for h in range(16):
            b = h % 2
            if h + 1 < 16:
                loadh(h + 1)
            blocks = [(0, 256, 2)] + [(CTX + i * 512, 512, NT) for i in range(8)]
            for (q0, n, nk) in blocks:
                o = cnt["o"] % 2; cnt["o"] += 1

                def qk(kt):
                    a = kt % 2
                    ks = slice(kt * 128, (kt + 1) * 128)
                    k.op("pe", lambda e: e.matmul(S[a][:, 0:n], lhsT=KA[b][:, ks], rhs=QA[b][:, q0:q0 + n], start=True, stop=False),
                         r=[rin[b]], w=[rS[a]])
                    k.op("pe", lambda e: e.matmul(S[a][:, 0:n], lhsT=KB[b][:, ks], rhs=QB[b][:, q0:q0 + n], start=False, stop=True),
                         r=[rin[b]], w=[rS[a]])

                qk(0)
                for kt in range(nk):
                    a = kt % 2
                    pi = cnt["p"] % 4; cnt["p"] += 1
                    if kt + 1 < nk:
                        qk(kt + 1)
                    k.op("act", lambda e: e.activation(out=Pm[pi][:, 0:n], in_=S[a][:, 0:n], func=AF.Exp, scale=scale),
                         r=[rS[a]], w=[rPm[pi]])
                    k.op("pe", lambda e: e.matmul(po[o][:, 0:n], lhsT=Vh[b][:, kt, :], rhs=Pm[pi][:, 0:n], start=(kt == 0),
                                                  stop=(kt == nk - 1)), r=[rin[b], rPm[pi]], w=[rpo[o]])
                    eng = "dve" if kt % 2 == 0 else "pool"
                    d_ = dsum[o][kt % 2]; rd_ = rds[o][kt % 2]
                    if kt < 2:
                        k.op(eng, lambda e: e.tensor_copy(out=d_[:, 0:n], in_=Pm[pi][:, 0:n]), r=[rPm[pi]], w=[rd_])
                    else:
                        k.op(eng, lambda e: e.tensor_tensor(out=d_[:, 0:n], in0=d_[:, 0:n], in1=Pm[pi][:, 0:n], op=ALU.add),
                             r=[rPm[pi]], w=[rd_])
                for i_ in range(2):
                    k.op("pe", lambda e: e.matmul(pd[o][:, 0:n], lhsT=ones[:], rhs=dsum[o][i_][:, 0:n], start=(i_ == 0), stop=(i_ == 1)),
                         r=[rones, rds[o][i_]], w=[rpd[o]])
                k.op("dve", lambda e: e.reciprocal(out=rec[:, 0:n], in_=pd[o][:, 0:n]), r=[rpd[o]], w=[rrec])
                k.op("dve", lambda e: e.tensor_tensor(out=osb[o][:, 0:n], in0=po[o][:, 0:n], in1=rec[:, 0:n], op=ALU.mult),
                     r=[rpo[o], rrec], w=[rosb[o]])
                k.dma("sp", CATT[h * 128:(h + 1) * 128, q0:q0 + n], osb[o][:, 0:n], r=[rosb[o]])


def _dft_consts(L):
    LT = L // 128
    n = np.arange(L, dtype=np.float64)
    theta = np.pi * np.outer(2 * n + 1, 2 * n + 1) / (4.0 * L)
    def tiled(m):
        return np.ascontiguousarray(m.reshape(LT, 128, LT, 128).transpose(2, 1, 0, 3)).astype(np.float32).astype(NPBF)
    C2 = tiled(np.cos(theta)); S2 = tiled(np.sin(theta))
    half = np.pi * (2 * n + 1) / (4.0 * L)
    cs = np.stack([np.cos(half), np.sin(half), -np.cos(half)], 0).reshape(3, LT, 128).transpose(2, 0, 1)
    t = np.linspace(0.0, 1.0, L)
    w = 2.0 * np.pi * np.arange(L) / L
    bands = np.linspace(1e-4, 15.0, 16)
    z = np.concatenate([t[:, None], np.cos(bands[None, :] * w[:, None]), -np.sin(bands[None, :] * w[:, None])], axis=1)
    negt = (-t).reshape(LT, 128).T
    return {"C2": C2, "S2": S2, "cs": np.ascontiguousarray(cs).astype(np.float32),
            "zT": np.ascontiguousarray(z.T).astype(np.float32), "negt": np.ascontiguousarray(negt).astype(np.float32)}


def _deltas():
    d_lo = np.log(1e-2) / 1.5
    d_hi = np.log(1e-2) / 0.3
    return np.abs(np.linspace(d_lo, d_hi, 1024)).astype(np.float32)[None, :]


def _rope_table():
    tab = np.zeros((NTOK, 3, 32), np.float32)
    tab[:CTX, 0, :] = 1.0
    t = np.arange(SEQ)
    inv = 10000.0 ** (-np.arange(0, 32, 2, dtype=np.float64) / 32.0)
    for g, pos in enumerate((t // GRID, t % GRID)):
        ang = pos[:, None].astype(np.float64) * inv[None, :]
        tab[CTX:, 0, g * 16:(g + 1) * 16] = np.cos(ang)
        tab[CTX:, 1, g * 16:(g + 1) * 16] = np.sin(ang)
        tab[CTX:, 2, g * 16:(g + 1) * 16] = -np.sin(ang)
    return tab


def _na_tables(rpb):
    kc = np.arange(64)[:, None]; qc = np.arange(64)[None, :]
    dc = np.clip(kc - qc + 15, 0, 30)
    BT = np.ascontiguousarray(rpb[:, :, :, dc])
    cstart = np.clip(qc - 8, 0, 48)
    inwin = (kc >= cstart) & (kc < cstart + 16)
    MK = np.where(inwin, 0.0, -1e30).astype(np.float32)
    return BT.astype(np.float32), np.ascontiguousarray(MK)


IDENTB = np.eye(128, dtype=np.float32).astype(NPBF)
IDENTF = np.eye(128, dtype=np.float32)


def emit_p0(P, c2T, ada_w, ada_b, MD):
    k = P.k
    with Stage(P):
        sc = P.sb([128, 16, 2]); rsc = Res()
        k.dma("sp", sc[:], c2T[:, :, :], w=[rsc])
        k.op("act", lambda e: e.activation(out=sc[:], in_=sc[:], func=AF.Silu), r=[rsc], w=[rsc])
        wt = [P.sb([128, 16, 512]) for _ in range(3)]; rw = [Res() for _ in range(3)]
        acc = [P.ps([128, 512]) for _ in range(2)]; racc = [Res(), Res()]
        bt = [P.sb([2, 512]) for _ in range(2)]; rb = [Res(), Res()]
        ot = [P.sb([2, 512]) for _ in range(2)]; ro = [Res(), Res()]
        blocks = [(l, nb) for l in range(DEPTH) for nb in range(24)]

        def loadw(i):
            l, nb = blocks[i]
            cs = slice(nb * 512, nb * 512 + 512)
            k.dma("sp" if i % 2 == 0 else "pool", wt[i % 3][:], ada_w[l, :, cs].rearrange("(kc p) n -> p kc n", p=128), w=[rw[i % 3]])

        loadw(0); loadw(1)
        for i, (l, nb) in enumerate(blocks):
            s = i % 2
            cs = slice(nb * 512, nb * 512 + 512)
            if i + 2 < len(blocks):
                loadw(i + 2)
            k.dma("sp", bt[s][:], ada_b[l:l + 1, cs].to_broadcast([2, 512]), w=[rb[s]])
            for kc in range(16):
                k.op("pe", lambda e: e.matmul(acc[s][0:2, :], lhsT=sc[:, kc, :], rhs=wt[i % 3][:, kc, :],
                                              start=(kc == 0), stop=(kc == 15)),
                     r=[rsc, rw[i % 3]], w=[racc[s]], skip_same=(kc > 0))
            k.op("dve", lambda e: e.tensor_tensor(out=ot[s][:], in0=acc[s][0:2, :], in1=bt[s][:], op=ALU.add),
                 r=[racc[s], rb[s]], w=[ro[s]])
            k.dma("sp", MD[l, :, cs], ot[s][:], r=[ro[s]])


def build_main(layers=(0, 1, 2, 3), stop_after=None, dump=()):
    P = Prog()
    nc, k = P.nc, P.k
    ins = {}

    def din(name, shape, dt=F32):
        if name not in ins:
            ins[name] = P.din(name, shape, dt)
        return ins[name]

    def scr(name, shape, dt=F32):
        if name in dump:
            return P.dout(name, shape, dt)
        return P.dscr(name, shape, dt)

    x_in = din("x", [SEQ, D_MODEL]); ctx_in = din("ctx", [CTX, D_MODEL])
    XL = P.dout("out", [SEQ, D_MODEL])
    XC = scr("XC", [CTX, D_MODEL])
    X = TokT(XC, XL)
    H2 = TokT(scr("H2C", [CTX, D_MODEL]), scr("H2L", [SEQ, D_MODEL]))
    MD = P.dscr("MD", [DEPTH, 2, 6 * D_MODEL])
    md_all = MD.rearrange("l s (m d) -> l s m d", m=6)
    identb_d = din("identb", [128, 128], BF16); identf_d = din("identf", [128, 128])
    QT = scr("QT", [1024, NTOK], BF16); KT = scr("KT", [1024, NTOK], BF16); V = scr("V", [NTOK, 1024], BF16)
    HYP = scr("HYP", [4356, 3072])
    CATT = scr("CATT", [D_MODEL, NTOK], BF16)
    AFFT = scr("AFFT", [16, NTOK])
    pscr = {"idxL": scr("idxL", [NE, CAP_L], U32), "gatL": scr("gatL", [NE, CAP_L]),
            "idxC": scr("idxC", [NE, CAP_C], U32), "gatC": scr("gatC", [NE, CAP_C])}
    QAT = scr("QAT", [1280, NTOK], BF16); KR = scr("KR", [NTOK, 64])
    QT2 = scr("QT2", [3072, NTOK], BF16); KT2 = scr("KT2", [3072, NTOK], BF16); V2 = scr("V2", [NTOK, D_MODEL], BF16)

    with Stage(P):
        tb = [P.sb([128, D_MODEL]) for _ in range(2)]; rtb = [Res(), Res()]
        for t in range(NT):
            s = t % 2
            src = ctx_in[t * 128:(t + 1) * 128, :] if t < 2 else x_in[(t - 2) * 128:(t - 1) * 128, :]
            k.dma("sp", tb[s][:], src, w=[rtb[s]])
            k.dma("sp", X.tile(t), tb[s][:], r=[rtb[s]])
        zt = P.sb([1, 3072]); rz = Res()
        k.op("pool", lambda e: e.memset(zt[:], 0.0), w=[rz])
        for r_ in (0, 257, 258, 4355):
            k.dma("sp", HYP[r_:r_ + 1, :], zt[:], r=[rz])

    emit_p0(P, din("c2T", [128, 16, 2]), din("ada_w", [DEPTH, D_MODEL, 6 * D_MODEL]), din("ada_b", [DEPTH, 6 * D_MODEL]), MD)

    def done(tag):
        return stop_after is not None and stop_after == tag

    for i in layers:
        j = i // 2
        md = md_all[i]
        g1 = din("norm1_g", [DEPTH, D_MODEL])[i:i + 1, :]
        g2 = din("norm2_g", [DEPTH, D_MODEL])[i:i + 1, :]
        if i % 2 == 0:
            emit_e1(P, X, md, g1, din("ev_w_in", [2, D_MODEL, 6144])[j], din("qkg", [2, 2, 512])[j], identb_d, QT, KT, V, HYP)
            if done("e1%d" % i):
                break
            emit_e2(P, QT, KT, V, din("BT", [2, 8, 15, 64, 64])[j], din("MK", [64, 64]), CATT)
            if done("e2%d" % i):
                break
            consts = {"deltas": din("deltas", [1, 1024])}
            for nm, L in (("C", CTX), ("L", SEQ)):
                LT = L // 128
                consts[nm] = {"C2": din("C2" + nm, [LT, 128, LT, 128], BF16), "S2": din("S2" + nm, [LT, 128, LT, 128], BF16),
                              "cs": din("cs" + nm, [128, 3, LT]), "zT": din("zT" + nm, [33, L]), "negt": din("negt" + nm, [128, LT])}
            emit_e3(P, HYP, din("hy_short_w", [2, 3, 3072])[j], din("hy_short_b", [2, 3072])[j:j + 1, :],
                    din("hy_w1", [2, 33, 64])[j], din("hy_b1", [2, 64, 1])[j], din("hy_w2", [2, 64, 64])[j],
                    din("hy_b2", [2, 64, 1])[j], din("hy_freq", [2, 64, 1])[j], din("hy_w3", [2, 64, 4096])[j],
                    din("hy_d", [2, 2, 1024])[j], consts, identb_d, CATT)
            if done("e3%d" % i):
                break
            w_o = din("ev_w_out", [2, D_MODEL, D_MODEL])[j]
        else:
            emit_m1(P, X, md, g1, din("mla_w_down", [2, D_MODEL, 1344])[j], din("mla_qa_g", [2, 768])[j:j + 1, :],
                    din("mla_kva_g", [2, 512])[j:j + 1, :], din("mla_w_uq", [2, 768, 3072])[j], din("mla_w_ukv", [2, 512, 4096])[j],
                    din("mla_q_g", [2, 192])[j:j + 1, :], din("mla_k_g", [2, 192])[j:j + 1, :], din("rope", [NTOK, 3, 32]),
                    identb_d, QAT, KR, QT2, KT2, V2)
            if done("m1%d" % i):
                break
            emit_m2(P, QT2, KT2, V2, CATT)
            if done("m2%d" % i):
                break
            w_o = din("mla_w_o", [2, D_MODEL, D_MODEL])[j]
        emit_p4(P, CATT, X, w_o, md, g2, din("router_w", [DEPTH, D_MODEL, 16])[i], identf_d, H2, AFFT)
        if done("p4%d" % i):
            break
        emit_p5(P, AFFT, H2, X, din("moe_w_gate", [DEPTH, 16, D_MODEL, 1024])[i], din("moe_w_up", [DEPTH, 16, D_MODEL, 1024])[i],
                din("moe_w_down", [DEPTH, 16, 1024, D_MODEL])[i], md, identb_d, pscr)
    print("main ninstr", k.ninstr, "inputs", list(ins.keys()))
    nc = P.finish()
    return nc, list(ins.keys())


def host_inputs(inp, mods=None):
    BT, MK = _na_tables(inp["na_rpb"])
    d = {
        "identb": IDENTB, "identf": IDENTF, "ada_w": inp["ada_w"], "ada_b": inp["ada_b"],
        "norm1_g": inp["norm1_g"], "norm2_g": inp["norm2_g"], "ev_w_in": inp["ev_w_in"], "ev_w_out": inp["ev_w_out"],
        "qkg": np.ascontiguousarray(np.stack([np.tile(inp["na_q_g"], (1, 4)), np.tile(inp["na_k_g"], (1, 4))], 1)),
        "BT": BT, "MK": MK, "deltas": _deltas(),
        "hy_short_w": inp["hy_short_w"], "hy_short_b": inp["hy_short_b"], "hy_w1": inp["hy_w1"],
        "hy_b1": inp["hy_b1"][:, :, None], "hy_w2": inp["hy_w2"], "hy_b2": inp["hy_b2"][:, :, None],
        "hy_freq": inp["hy_freq"][:, :, None], "hy_w3": inp["hy_w3"], "hy_d": inp["hy_d"],
        "mla_w_down": inp["mla_w_down"], "mla_qa_g": inp["mla_qa_g"], "mla_kva_g": inp["mla_kva_g"],
        "mla_w_uq": inp["mla_w_uq"], "mla_w_ukv": inp["mla_w_ukv"], "mla_q_g": inp["mla_q_g"], "mla_k_g": inp["mla_k_g"],
        "mla_w_o": inp["mla_w_o"], "rope": _rope_table(), "router_w": inp["router_w"],
        "moe_w_gate": inp["moe_w_gate"], "moe_w_up": inp["moe_w_up"], "moe_w_down": inp["moe_w_down"],
    }
    for nm, L in (("C", CTX), ("L", SEQ)):
        c = _dft_consts(L)
        for kk, v in c.items():
            d[kk + nm] = v
    return d


def core_inputs(shared, inp, mods, b, names):
    c2 = np.stack([inp["c"][b], inp["c_ctx"]], 0)
    m = {"x": inp["x"][b], "ctx": inp["ctx"][b], "c2T": np.ascontiguousarray(c2.T.reshape(16, 128, 2).transpose(1, 0, 2))}
    out = {}
    for n in names:
        v = m[n] if n in m else shared[n]
        out[n] = np.ascontiguousarray(v)
    return out


def kernel(**inp):
    inp = {k_: np.asarray(v) for k_, v in inp.items()}
    nc, names = build_main()
    shared = host_inputs(inp)
    maps = [core_inputs(shared, inp, None, b, names) for b in range(BATCH)]
    res = run_bass_kernel_spmd(nc, maps, core_ids=list(range(BATCH)))
    return np.stack([np.asarray(res.results[b]["out"]) for b in range(BATCH)], 0).astype(np.float32)
```

```python
import numpy as np
import ml_dtypes
import concourse.bass as bass
import concourse.mybir as mybir
from concourse.bass_utils import run_bass_kernel_spmd
from contextlib import ExitStack

F32 = mybir.dt.float32
BF16 = mybir.dt.bfloat16
I32 = mybir.dt.int32
U32 = mybir.dt.uint32
AF = mybir.ActivationFunctionType
ALU = mybir.AluOpType
AX = mybir.AxisListType
NPBF = ml_dtypes.bfloat16

D_MODEL = 2048
BATCH = 4
SEQ = 4096
DEPTH = 4
CTX = 256
NTOK = CTX + SEQ
NCORES = 8


class Res:
    __slots__ = ("name", "w", "r")

    def __init__(self, name=""):
        self.name = name
        self.w = None
        self.r = {}


class K:
    SEM_ROLL = 30000

    def __init__(self, nc, es, n_dma_sems=40):
        self.nc = nc
        self.es = es
        self.engs = {"pe": nc.tensor, "dve": nc.vector, "act": nc.scalar,
                     "pool": nc.gpsimd, "sp": nc.sync}
        self.sem = {}
        self.cnt = {}
        self.seen = {e: {} for e in self.engs}
        self.nsem = 0
        for e in self.engs:
            self._newsem(e)
        self.dsems = [es.enter_context(nc.semaphore("dq%d" % i)) for i in range(n_dma_sems)]
        self.dval = [0] * n_dma_sems
        self.dnext = 0
        self.ninstr = 0

    def _newsem(self, e):
        self.nsem += 1
        self.sem[e] = self.es.enter_context(self.nc.semaphore("s_%s_%d" % (e, self.nsem)))
        self.cnt[e] = 0

    def ev_wait(self, e, ev):
        sem, val = ev
        if self.seen[e].get(sem, 0) >= val:
            return
        self.engs[e].wait_ge(sem, val)
        self.seen[e][sem] = val

    def _deps(self, e, r, w, skip_same=False):
        deps = []
        for b in r:
            if b.w is not None:
                deps.append(b.w)
        for b in w:
            if b.w is not None:
                deps.append(b.w)
            for s, v in b.r.items():
                deps.append((s, v))
        for ev in deps:
            if skip_same and ev[0] is self.sem[e]:
                continue
            self.ev_wait(e, ev)

    def _mark(self, ev, r, w):
        for b in r:
            if b.r.get(ev[0], 0) < ev[1]:
                b.r[ev[0]] = ev[1]
        for b in w:
            b.w = ev
            b.r = {}

    def op(self, e, fn, r=(), w=(), skip_same=False):
        if e == "pe":
            skip_same = True
        self._deps(e, r, w, skip_same)
        if self.cnt[e] >= self.SEM_ROLL:
            self._newsem(e)
        ins = fn(self.engs[e])
        self.cnt[e] += 1
        ins.then_inc(self.sem[e], 1)
        ev = (self.sem[e], self.cnt[e])
        self._mark(ev, r, w)
        self.ninstr += 1
        return ev

    def dma(self, e, out, in_, r=(), w=(), indirect=None, **kw):
        self._deps(e, r, w)
        i = self.dnext
        self.dnext = (self.dnext + 1) % len(self.dsems)
        if self.dval[i] > 0:
            self.ev_wait(e, (self.dsems[i], self.dval[i]))
        if self.dval[i] >= self.SEM_ROLL:
            self.nsem += 1
            self.dsems[i] = self.es.enter_context(self.nc.semaphore("dq_r%d" % self.nsem))
            self.dval[i] = 0
        if indirect is not None:
            ins = self.engs[e].indirect_dma_start(out=out, in_=in_, **indirect)
        else:
            ins = self.engs[e].dma_start(out=out, in_=in_, **kw)
        self.dval[i] += 16
        ins.then_inc(self.dsems[i], 16)
        ev = (self.dsems[i], self.dval[i])
        self._mark(ev, r, w)
        self.ninstr += 1
        return ev

    def barrier(self):
        evs = [(self.sem[e], self.cnt[e]) for e in self.engs if self.cnt[e] > 0]
        evs += [(self.dsems[i], self.dval[i]) for i in range(len(self.dsems)) if self.dval[i] > 0]
        for e in self.engs:
            for ev in evs:
                if ev[0] is self.sem[e]:
                    continue
                self.ev_wait(e, ev)


class Prog:
    def __init__(self):
        self.nc = bass.Bass("TRN2", target_bir_lowering=False)
        self.es = ExitStack()
        self.k = K(self.nc, self.es)
        self.n = 0

    def din(self, name, shape, dt=F32):
        return self.nc.dram_tensor(name, list(shape), dt, kind="ExternalInput")

    def dout(self, name, shape, dt=F32):
        return self.nc.dram_tensor(name, list(shape), dt, kind="ExternalOutput")

    def dscr(self, name, shape, dt=F32):
        return self.nc.dram_tensor(name, list(shape), dt, kind="Internal")

    def sb(self, shape, dt=F32, name=None):
        self.n += 1
        return self.es.enter_context(self.nc.sbuf_tensor(name or ("sb%d" % self.n), list(shape), dt))

    def ps(self, shape, dt=F32, name=None):
        self.n += 1
        return self.es.enter_context(self.nc.psum_tensor(name or ("ps%d" % self.n), list(shape), dt))

    def finish(self):
        self.k.barrier()
        self.es.close()
        return self.nc


_LAUNCHES = []


def launch(nc, in_maps):
    res = run_bass_kernel_spmd(nc, in_maps, core_ids=list(range(NCORES)))
    return res.results


P0_COLS = 6 * D_MODEL // NCORES


def build_p0():
    P = Prog()
    nc, k = P.nc, P.k
    cT = P.din("cT", [128, 16, 5])
    w = P.din("w", [DEPTH, D_MODEL, P0_COLS])
    b = P.din("b", [DEPTH, 5, P0_COLS])
    o = P.dout("o", [DEPTH, 5, P0_COLS])
    sc = P.sb([128, 16, 5]); rsc = Res()
    k.dma("sp", sc[:], cT[:, :, :], w=[rsc])
    k.op("act", lambda e: e.activation(out=sc[:], in_=sc[:], func=AF.Silu), r=[rsc], w=[rsc])
    wt = [P.sb([128, 16, 512]) for _ in range(2)]; rw = [Res(), Res()]
    acc = [P.ps([128, 512]) for _ in range(2)]; racc = [Res(), Res()]
    bt = [P.sb([5, 512]) for _ in range(2)]; rb = [Res(), Res()]
    ot = [P.sb([5, 512]) for _ in range(2)]; ro = [Res(), Res()]
    it = 0
    for l in range(DEPTH):
        for nb in range(P0_COLS // 512):
            s = it % 2
            it += 1
            cs = slice(nb * 512, nb * 512 + 512)
            k.dma("sp", wt[s][:], w[l, :, cs].rearrange("(kc p) n -> p kc n", p=128), w=[rw[s]])
            k.dma("pool", bt[s][:], b[l, :, cs], w=[rb[s]])
            for kc in range(16):
                k.op("pe", lambda e: e.matmul(acc[s][0:5, :], lhsT=sc[:, kc, :], rhs=wt[s][:, kc, :],
                                               start=(kc == 0), stop=(kc == 15)),
                     r=[rsc, rw[s]], w=[racc[s]], skip_same=(kc > 0))
            k.op("dve", lambda e: e.tensor_tensor(out=ot[s][:], in0=acc[s][0:5, :], in1=bt[s][:], op=ALU.add),
                 r=[racc[s], rb[s]], w=[ro[s]])
            k.dma("sp", o[l, :, cs], ot[s][:], r=[ro[s]])
    return P.finish()


def run_p0(inp):
    c5 = np.concatenate([inp["c"], inp["c_ctx"][None]], 0)
    cT = np.ascontiguousarray(c5.T.reshape(16, 128, 5).transpose(1, 0, 2))
    nc = build_p0()
    maps = []
    for c in range(NCORES):
        cs = slice(c * P0_COLS, (c + 1) * P0_COLS)
        maps.append({"cT": cT,
                     "w": np.ascontiguousarray(inp["ada_w"][:, :, cs]),
                     "b": np.ascontiguousarray(np.broadcast_to(inp["ada_b"][:, None, cs], (DEPTH, 5, P0_COLS)))})
    res = launch(nc, maps)
    mods = np.concatenate([r["o"] for r in res], axis=2)
    return mods.reshape(DEPTH, 5, 6, D_MODEL)


NT = NTOK // 128
EPS = 1e-6
GRID = 64


def bcast_rows(ap_row, n=128):
    return ap_row.to_broadcast([n, ap_row.shape[-1]])


class TokT:
    def __init__(self, c, l):
        self.c, self.l = c, l

    def tile(self, t):
        if t < 2:
            return self.c[t * 128:(t + 1) * 128, :]
        return self.l[(t - 2) * 128:(t - 1) * 128, :]

    def part(self, name):
        return self.c if name == "C" else self.l


class Stage:
    def __init__(self, P):
        self.P = P

    def __enter__(self):
        self.saved = self.P.es
        self.P.es = ExitStack()
        return self.P

    def __exit__(self, *a):
        self.P.k.barrier()
        self.P.es.close()
        self.P.es = self.saved
        return False


def load_mod_AB(P, md, g, ia, ib):
    k = P.k
    gt = P.sb([128, D_MODEL]); rg = Res()
    k.dma("sp", gt[:], bcast_rows(g), w=[rg])
    A, B = [], []
    rAB = Res()
    for s in range(2):
        a = P.sb([128, D_MODEL]); b = P.sb([128, D_MODEL])
        k.dma("sp", a[:], bcast_rows(md[s, ia:ia + 1, :]), w=[rAB])
        k.dma("sp", b[:], bcast_rows(md[s, ib:ib + 1, :]), w=[rAB])
        k.op("dve", lambda e: e.scalar_tensor_tensor(out=a[:], in0=a[:], scalar=1.0, in1=gt[:],
                                                     op0=ALU.add, op1=ALU.mult), r=[rg, rAB], w=[rAB])
        A.append(a); B.append(b)
    return A, B, rAB


def rstd_from_ss(k, rstd, ss, n, r, w, extra=None):
    k.op("dve", lambda e: e.tensor_scalar(out=rstd, in0=ss, scalar1=1.0 / n, scalar2=EPS,
                                          op0=ALU.mult, op1=ALU.add), r=r, w=w)
    k.op("act", lambda e: e.activation(out=rstd, in_=rstd, func=AF.Sqrt), r=w, w=w)
    k.op("dve", lambda e: e.reciprocal(out=rstd, in_=rstd), r=w, w=w)


class NormT:
    def __init__(self, P, X, A, B, rAB, identb, rid):
        self.P, self.X, self.A, self.B, self.rAB, self.identb, self.rid = P, X, A, B, rAB, identb, rid
        self.xt = [P.sb([128, D_MODEL]) for _ in range(2)]; self.rx = [Res(), Res()]
        self.junk = P.sb([128, D_MODEL]); self.rj = Res()
        self.hb = [P.sb([128, D_MODEL], BF16) for _ in range(2)]; self.rh = [Res(), Res()]
        self.ss = P.sb([128, 2]); self.rss = Res()
        self.pT = P.ps([128, D_MODEL], BF16); self.rpT = Res()
        self.n = 0

    def load(self, t):
        s = t % 2
        self.P.k.dma("sp", self.xt[s][:], self.X.tile(t), w=[self.rx[s]])

    def run(self, tiles, hT, rhT):
        k = self.P.k
        self.load(tiles[0])
        for i, t in enumerate(tiles):
            s = t % 2
            if i + 1 < len(tiles):
                self.load(tiles[i + 1])
            xt, junk, ss, hb, pT = self.xt[s], self.junk, self.ss, self.hb[s], self.pT
            rx, rj, rss, rh, rpT = self.rx[s], self.rj, self.rss, self.rh[s], self.rpT
            m = 1 if t < 2 else 0
            k.op("act", lambda e: e.activation(out=junk[:], in_=xt[:], func=AF.Square), r=[rx], w=[rj])
            k.op("dve", lambda e: e.reduce_sum(out=ss[:, 0:1], in_=junk[:], axis=AX.X), r=[rj], w=[rss])
            rstd_from_ss(k, ss[:, 1:2], ss[:, 0:1], D_MODEL, [rss], [rss])
            k.op("dve", lambda e: e.scalar_tensor_tensor(out=junk[:], in0=xt[:], scalar=ss[:, 1:2], in1=self.A[m][:],
                                                         op0=ALU.mult, op1=ALU.mult), r=[rx, rss, self.rAB], w=[rj])
            k.op("dve", lambda e: e.tensor_tensor(out=hb[:], in0=junk[:], in1=self.B[m][:], op=ALU.add),
                 r=[rj, self.rAB], w=[rh])
            for kc in range(16):
                k.op("pe", lambda e: e.transpose(out=pT[:, kc * 128:(kc + 1) * 128], in_=hb[:, kc * 128:(kc + 1) * 128],
                                                 identity=self.identb[:]), r=[rh, self.rid], w=[rpT], skip_same=(kc > 0))
            k.op("act", lambda e: e.copy(out=hT[:, :, i * 128:(i + 1) * 128],
                                         in_=pT[:].rearrange("p (c t) -> p c t", c=16)), r=[rpT], w=[rhT])


def emit_e1(P, X, md, g1, w_in, qkg, identb_d, QT, KT, V, HYP):
    k = P.k
    with Stage(P):
        identb = P.sb([128, 128], BF16); rid = Res()
        k.dma("sp", identb[:], identb_d[:, :], w=[rid])
        gq = P.sb([128, 2, 512]); rgq = Res()
        for i in range(2):
            k.dma("sp", gq[:, i, :], bcast_rows(qkg[i:i + 1, :]), w=[rgq])
        HT = NT // 2
        hT = P.sb([128, 16, HT * 128], BF16); rhT = Res()
        wb = [P.sb([128, 16, 512], BF16) for _ in range(2)]; rw = [Res(), Res()]
        sq = P.sb([128, 512]); rsq = Res()
        s4 = P.sb([128, 8]); rs4 = Res()
        tmp = P.sb([128, 512]); rtmp = Res()
        qn = [P.sb([128, 512], BF16) for _ in range(2)]; rqn = [Res(), Res()]
        qT = [P.sb([128, 512], BF16) for _ in range(2)]; rqT = [Res(), Res()]
        of = [P.sb([128, 512]) for _ in range(2)]; rof = [Res(), Res()]
        acc = [P.ps([128, 512]) for _ in range(2)]; racc = [Res(), Res()]
        pq = [P.ps([128, 512], BF16) for _ in range(2)]; rpq = [Res(), Res()]
        for half in range(2):
            tiles = list(range(half * HT, (half + 1) * HT))
            with Stage(P):
                A, B, rAB = load_mod_AB(P, md, g1, 1, 0)
                NormT(P, X, A, B, rAB, identb, rid).run(tiles, hT, rhT)

            def loadw(nb):
                s = nb % 2
                k.dma("pool", wb[s][:], w_in[:, nb * 512:(nb + 1) * 512].rearrange("(kc p) n -> p kc n", p=128), w=[rw[s]])

            loadw(0)
            it = 0
            for nb in range(12):
                s = nb % 2
                if nb + 1 < 12:
                    loadw(nb + 1)
                for i, t in enumerate(tiles):
                    a = it % 2
                    it += 1
                    rows = slice(t * 128, (t + 1) * 128)
                    for kc in range(16):
                        k.op("pe", lambda e: e.matmul(acc[a][:], lhsT=hT[:, kc, i * 128:(i + 1) * 128], rhs=wb[s][:, kc, :],
                                                      start=(kc == 0), stop=(kc == 15)),
                             r=[rhT, rw[s]], w=[racc[a]], skip_same=(kc > 0))
                    if nb < 4:
                        gi = 0 if nb < 2 else 1
                        dst = QT if nb < 2 else KT
                        hb0 = (nb % 2) * 4
                        k.op("act", lambda e: e.activation(out=sq[:], in_=acc[a][:], func=AF.Square), r=[racc[a]], w=[rsq])
                        k.op("dve", lambda e: e.reduce_sum(out=s4[:, 0:4], in_=sq[:].rearrange("p (h d) -> p h d", h=4),
                                                           axis=AX.X), r=[rsq], w=[rs4])
                        rstd_from_ss(k, s4[:, 4:8], s4[:, 0:4], 128, [rs4], [rs4])
                        k.op("dve", lambda e: e.tensor_tensor(out=tmp[:].rearrange("p (h d) -> p h d", h=4),
                                                              in0=acc[a][:].rearrange("p (h d) -> p h d", h=4),
                                                              in1=s4[:, 4:8].unsqueeze(2).to_broadcast([128, 4, 128]),
                                                              op=ALU.mult), r=[racc[a], rs4], w=[rtmp])
                        k.op("dve", lambda e: e.tensor_tensor(out=qn[a][:], in0=tmp[:], in1=gq[:, gi, :], op=ALU.mult),
                             r=[rtmp, rgq], w=[rqn[a]])
                        for hh in range(4):
                            k.op("pe", lambda e: e.transpose(out=pq[a][:, hh * 128:(hh + 1) * 128],
                                                             in_=qn[a][:, hh * 128:(hh + 1) * 128], identity=identb[:]),
                                 r=[rqn[a], rid], w=[rpq[a]], skip_same=(hh > 0))
                        k.op("act", lambda e: e.copy(out=qT[a][:], in_=pq[a][:]), r=[rpq[a]], w=[rqT[a]])
                        k.dma("sp", dst[hb0 * 128:(hb0 + 4) * 128, rows].rearrange("(h d) t -> d h t", d=128),
                              qT[a][:].rearrange("p (h t) -> p h t", h=4), r=[rqT[a]])
                    elif nb < 6:
                        k.op("act", lambda e: e.copy(out=qn[a][:], in_=acc[a][:]), r=[racc[a]], w=[rqn[a]])
                        k.dma("sp", V[rows, (nb - 4) * 512:(nb - 3) * 512], qn[a][:], r=[rqn[a]])
                    else:
                        k.op("act", lambda e: e.copy(out=of[a][:], in_=acc[a][:]), r=[racc[a]], w=[rof[a]])
                        r0 = (1 + t * 128) if t < 2 else (259 + (t - 2) * 128)
                        k.dma("sp", HYP[r0:r0 + 128, (nb - 6) * 512:(nb - 5) * 512], of[a][:], r=[rof[a]])


def emit_p4(P, CATT, X, w_o, md, g2, rwd, identf_d, H2, AFFT):
    k = P.k
    with Stage(P):
        identf = P.sb([128, 128]); rid = Res()
        k.dma("sp", identf[:], identf_d[:, :], w=[rid])
        rw = P.sb([128, 16, 16]); rrw = Res()
        k.dma("sp", rw[:], rwd.rearrange("(kc p) e -> p kc e", p=128), w=[rrw])
        wo = P.sb([128, 16, D_MODEL], BF16); rwo = Res()
        for q in range(4):
            k.dma("pool", wo[:, :, q * 512:(q + 1) * 512],
                  w_o[:, q * 512:(q + 1) * 512].rearrange("(kc p) n -> p kc n", p=128), w=[rwo])
        A, B, rAB = load_mod_AB(P, md, g2, 4, 3)
        G = []
        rG = Res()
        for s in range(2):
            gt = P.sb([128, D_MODEL])
            k.dma("sp", gt[:], bcast_rows(md[s, 2:3, :]), w=[rG])
            G.append(gt)
        NB = 2
        ct = [P.sb([128, 16, 128], BF16) for _ in range(NB)]; rct = [Res() for _ in range(NB)]
        xm = [P.sb([128, D_MODEL]) for _ in range(NB)]; rxm = [Res() for _ in range(NB)]
        h2 = [P.sb([128, D_MODEL]) for _ in range(NB)]; rh2 = [Res() for _ in range(NB)]
        junk = P.sb([128, D_MODEL]); rj = Res()
        ss = P.sb([128, 2]); rss = Res()
        acc = [P.ps([128, 512]) for _ in range(2)]; racc = [Res(), Res()]
        pT = P.ps([128, D_MODEL]); rpT = Res()
        h2T = P.sb([128, 16, 128]); rh2T = Res()
        lg = P.ps([128, 16]); rlg = Res()
        sm = P.sb([128, 4]); rsm = Res()
        ex = P.sb([128, 16]); rex = Res()
        pA = P.ps([16, 128]); rpA = Res()
        aT = [P.sb([16, 128]) for _ in range(2)]; raT = [Res(), Res()]

        def loads(t):
            s = t % NB
            rows = slice(t * 128, (t + 1) * 128)
            k.dma("sp", ct[s][:], CATT[:, rows].rearrange("(kc p) t -> p kc t", p=128), w=[rct[s]])
            k.dma("sp", xm[s][:], X.tile(t), w=[rxm[s]])

        loads(0)
        it = 0
        for t in range(NT):
            s = t % NB
            m = 1 if t < 2 else 0
            rows = slice(t * 128, (t + 1) * 128)
            if t + 1 < NT:
                loads(t + 1)
            for nb in range(4):
                a = it % 2
                it += 1
                cs = slice(nb * 512, (nb + 1) * 512)
                for kc in range(16):
                    k.op("pe", lambda e: e.matmul(acc[a][:], lhsT=ct[s][:, kc, :], rhs=wo[:, kc, cs],
                                                  start=(kc == 0), stop=(kc == 15)),
                         r=[rct[s], rwo], w=[racc[a]], skip_same=(kc > 0))
                k.op("dve", lambda e: e.tensor_tensor(out=junk[:, cs], in0=acc[a][:], in1=G[m][:, cs], op=ALU.mult),
                     r=[racc[a], rG], w=[rj])
                k.op("pool", lambda e: e.tensor_tensor(out=xm[s][:, cs], in0=xm[s][:, cs], in1=junk[:, cs], op=ALU.add),
                     r=[rj], w=[rxm[s]])
            k.dma("sp", X.tile(t), xm[s][:], r=[rxm[s]])
            k.op("act", lambda e: e.activation(out=junk[:], in_=xm[s][:], func=AF.Square), r=[rxm[s]], w=[rj])
            k.op("dve", lambda e: e.reduce_sum(out=ss[:, 0:1], in_=junk[:], axis=AX.X), r=[rj], w=[rss])
            rstd_from_ss(k, ss[:, 1:2], ss[:, 0:1], D_MODEL, [rss], [rss])
            k.op("dve", lambda e: e.scalar_tensor_tensor(out=junk[:], in0=xm[s][:], scalar=ss[:, 1:2], in1=A[m][:],
                                                         op0=ALU.mult, op1=ALU.mult), r=[rxm[s], rss, rAB], w=[rj])
            k.op("dve", lambda e: e.tensor_tensor(out=h2[s][:], in0=junk[:], in1=B[m][:], op=ALU.add),
                 r=[rj, rAB], w=[rh2[s]])
            k.dma("sp", H2.tile(t), h2[s][:], r=[rh2[s]])
            for kc in range(16):
                k.op("pe", lambda e: e.transpose(out=pT[:, kc * 128:(kc + 1) * 128], in_=h2[s][:, kc * 128:(kc + 1) * 128],
                                                 identity=identf[:]), r=[rh2[s], rid], w=[rpT], skip_same=(kc > 0))
            k.op("act", lambda e: e.copy(out=h2T[:], in_=pT[:].rearrange("p (c t) -> p c t", c=16)), r=[rpT], w=[rh2T])
            for kc in range(16):
                k.op("pe", lambda e: e.matmul(lg[:], lhsT=h2T[:, kc, :], rhs=rw[:, kc, :], start=(kc == 0), stop=(kc == 15)),
                     r=[rh2T, rrw], w=[rlg], skip_same=(kc > 0))
            k.op("dve", lambda e: e.reduce_max(out=sm[:, 0:1], in_=lg[:], axis=AX.X), r=[rlg], w=[rsm])
            k.op("dve", lambda e: e.tensor_scalar(out=sm[:, 1:2], in0=sm[:, 0:1], scalar1=-1.0, scalar2=None, op0=ALU.mult),
                 r=[rsm], w=[rsm])
            k.op("act", lambda e: e.activation(out=ex[:], in_=lg[:], func=AF.Exp, bias=sm[:, 1:2], scale=1.0),
                 r=[rlg, rsm], w=[rex])
            k.op("dve", lambda e: e.reduce_sum(out=sm[:, 2:3], in_=ex[:], axis=AX.X), r=[rex], w=[rsm])
            k.op("dve", lambda e: e.reciprocal(out=sm[:, 3:4], in_=sm[:, 2:3]), r=[rsm], w=[rsm])
            k.op("dve", lambda e: e.tensor_scalar(out=ex[:], in0=ex[:], scalar1=sm[:, 3:4], scalar2=None, op0=ALU.mult),
                 r=[rsm, rex], w=[rex])
            k.op("pe", lambda e: e.transpose(out=pA[:], in_=ex[:], identity=identf[:]), r=[rex, rid], w=[rpA])
            k.op("act", lambda e: e.copy(out=aT[s][:], in_=pA[:]), r=[rpA], w=[raT[s]])
            k.dma("sp", AFFT[:, rows], aT[s][:], r=[raT[s]])


NE = 16
CAP_L, CAP_C = 512, 32
CUT = 99


def emit_p5(P, AFFT, H2, X, wg_d, wu_d, wd_d, md, identb_d, scr):
    k = P.k
    idx_s = {"L": scr["idxL"], "C": scr["idxC"]}
    gat_s = {"L": scr["gatL"], "C": scr["gatC"]}
    with Stage(P):
        identb = P.sb([128, 128], BF16); rid = Res()
        k.dma("sp", identb[:], identb_d[:, :], w=[rid])
        M5 = []
        rM5 = Res()
        for s in range(2):
            mt = P.sb([128, D_MODEL])
            k.dma("sp", mt[:], bcast_rows(md[s, 5:6, :]), w=[rM5])
            M5.append(mt)
        idxT = {}; gatT = {}
        rIG = Res()
        for name, cap in (("L", CAP_L), ("C", CAP_C)):
            pp = min(128, cap)
            idxT[name] = P.sb([pp, NE, cap // pp], I32); gatT[name] = P.sb([pp, NE, cap // pp])
        with Stage(P):
            for name, c0, N, cap in (("L", CTX, SEQ, CAP_L), ("C", 0, CTX, CAP_C)):
                work = P.sb([NE, N]); rwk = Res()
                k.dma("sp", work[:], AFFT[:, c0:c0 + N], w=[rwk])
                mx = P.sb([NE, cap]); rmx = Res()
                ix = P.sb([NE, cap], U32); rix = Res()
                for itn in range(cap // 8):
                    sl = slice(itn * 8, itn * 8 + 8)
                    k.op("dve", lambda e: e.max(out=mx[:, sl], in_=work[:]), r=[rwk], w=[rmx])
                    k.op("dve", lambda e: e.max_index(out=ix[:, sl], in_max=mx[:, sl], in_values=work[:]), r=[rwk, rmx], w=[rix])
                    k.op("dve", lambda e: e.match_replace(out=work[:], in_to_replace=mx[:, sl], in_values=work[:], imm_value=0.0),
                         r=[rmx], w=[rwk])
                rs = Res()
                k.dma("sp", idx_s[name][:, :], ix[:], r=[rix], w=[rs])
                k.dma("sp", gat_s[name][:, :], mx[:], r=[rmx], w=[rs])
                pp = min(128, cap)
                nj = cap // pp
                for e_ in range(NE):
                    for j in range(nj):
                        k.dma("sp", idxT[name][:, e_, j:j + 1],
                              idx_s[name].bitcast(I32)[e_:e_ + 1, j * pp:(j + 1) * pp].rearrange("o p -> p o"), r=[rs], w=[rIG])
                        k.dma("sp", gatT[name][:, e_, j:j + 1],
                              gat_s[name][e_:e_ + 1, j * pp:(j + 1) * pp].rearrange("o p -> p o"), r=[rs], w=[rIG])

        wslot = [P.sb([128, 16, 512], BF16) for _ in range(4)]; rws = [Res() for _ in range(4)]
        dslot = [P.sb([128, 4, D_MODEL], BF16) for _ in range(2)]; rds = [Res() for _ in range(2)]
        xs = [P.sb([128, D_MODEL]) for _ in range(2)]; rxs = [Res(), Res()]
        xb = [P.sb([128, D_MODEL], BF16) for _ in range(2)]; rxb = [Res(), Res()]
        pT = P.ps([128, D_MODEL], BF16); rpT = Res()
        xsT = P.sb([128, 16, 512], BF16); rxsT = Res()
        pa = [P.ps([128, 512]) for _ in range(2)]; rpa = [Res(), Res()]
        pu = [P.ps([128, 512]) for _ in range(2)]; rpu = [Res(), Res()]
        pd = [P.ps([128, 512]) for _ in range(2)]; rpd = [Res(), Res()]
        sa = [P.sb([128, 512]) for _ in range(2)]; rsa = [Res(), Res()]
        hT = P.sb([128, 8, 512], BF16); rhT = Res()
        yt = [P.sb([128, D_MODEL]) for _ in range(2)]; ryt = [Res(), Res()]
        rX = Res()

        def loadw_gu(e_):
            for hf in range(2):
                k.dma("pool", wslot[2 * hf][:], wg_d[e_, :, hf * 512:(hf + 1) * 512].rearrange("(kc p) n -> p kc n", p=128),
                      w=[rws[2 * hf]])
                k.dma("pool", wslot[2 * hf + 1][:], wu_d[e_, :, hf * 512:(hf + 1) * 512].rearrange("(kc p) n -> p kc n", p=128),
                      w=[rws[2 * hf + 1]])

        def loadw_d(e_):
            for hf in range(2):
                k.dma("pool", dslot[hf][:], wd_d[e_, hf * 512:(hf + 1) * 512, :].rearrange("(fc p) n -> p fc n", p=128),
                      w=[rds[hf]])

        xsT_C = P.sb([128, 16, CAP_C], BF16); rxsT_C = Res()
        hT_C = P.sb([128, 8, CAP_C], BF16); rhT_C = Res()
        XS = {"L": (xsT, rxsT), "C": (xsT_C, rxsT_C)}
        HT = {"L": (hT, rhT), "C": (hT_C, rhT_C)}
        cnt = {"x": 0, "a": 0, "d": 0, "y": 0}

        def gather_T(e_, name):
            cap = CAP_L if name == "L" else CAP_C
            pp = min(128, cap)
            nj = cap // pp
            it_ = idxT[name]
            xT, rxT = XS[name]
            for j in range(nj):
                s = cnt["x"] % 2; cnt["x"] += 1
                k.dma("pool", xs[s][0:pp, :], H2.part(name)[:, :], r=[rIG], w=[rxs[s]],
                      indirect=dict(out_offset=None, in_offset=bass.IndirectOffsetOnAxis(ap=it_[:, e_, j:j + 1], axis=0)))
                k.op("act", lambda e: e.copy(out=xb[s][0:pp, :], in_=xs[s][0:pp, :]), r=[rxs[s]], w=[rxb[s]])
                for kc in range(16):
                    k.op("pe", lambda e: e.transpose(out=pT[:, kc * 128:kc * 128 + pp], in_=xb[s][0:pp, kc * 128:(kc + 1) * 128],
                                                     identity=identb[0:pp, 0:pp]), r=[rxb[s], rid], w=[rpT])
                k.op("dve", lambda e: e.tensor_copy(out=xT[:, :, j * pp:(j + 1) * pp],
                                                    in_=pT[:].rearrange("p (c t) -> p c t", c=16)[:, :, 0:pp]), r=[rpT], w=[rxT])

        def gate_up(e_):
            for hf in range(2):
                for fc in range(4):
                    fs = slice(fc * 128, (fc + 1) * 128)
                    for name in ("L", "C"):
                        cap = CAP_L if name == "L" else CAP_C
                        xT, rxT = XS[name]
                        hT_, rhT_ = HT[name]
                        a = cnt["a"] % 2; cnt["a"] += 1
                        for kc in range(16):
                            k.op("pe", lambda e: e.matmul(pa[a][:, 0:cap], lhsT=wslot[2 * hf][:, kc, fs], rhs=xT[:, kc, 0:cap],
                                                          start=(kc == 0), stop=(kc == 15)), r=[rws[2 * hf], rxT], w=[rpa[a]])
                        for kc in range(16):
                            k.op("pe", lambda e: e.matmul(pu[a][:, 0:cap], lhsT=wslot[2 * hf + 1][:, kc, fs], rhs=xT[:, kc, 0:cap],
                                                          start=(kc == 0), stop=(kc == 15)), r=[rws[2 * hf + 1], rxT], w=[rpu[a]])
                        k.op("act", lambda e: e.activation(out=sa[a][:, 0:cap], in_=pa[a][:, 0:cap], func=AF.Silu),
                             r=[rpa[a]], w=[rsa[a]])
                        k.op("dve", lambda e: e.tensor_tensor(out=hT_[:, hf * 4 + fc, 0:cap], in0=sa[a][:, 0:cap], in1=pu[a][:, 0:cap],
                                                              op=ALU.mult), r=[rsa[a], rpu[a]], w=[rhT_])

        def down_scatter(e_, name, m):
            cap = CAP_L if name == "L" else CAP_C
            pp = min(128, cap)
            nj = cap // pp
            it_, gt_ = idxT[name], gatT[name]
            hT_, rhT_ = HT[name]
            for j in range(nj):
                y = cnt["y"] % 2; cnt["y"] += 1
                for nb in range(4):
                    d = cnt["d"] % 2; cnt["d"] += 1
                    cs = slice(nb * 512, (nb + 1) * 512)
                    for fc in range(8):
                        k.op("pe", lambda e: e.matmul(pd[d][0:pp, :], lhsT=hT_[:, fc, j * pp:(j + 1) * pp],
                                                      rhs=dslot[fc // 4][:, fc % 4, cs], start=(fc == 0), stop=(fc == 7)),
                             r=[rhT_, rds[fc // 4]], w=[rpd[d]])
                    k.op("dve", lambda e: e.scalar_tensor_tensor(out=yt[y][0:pp, cs], in0=pd[d][0:pp, :],
                                                                 scalar=gt_[:, e_, j:j + 1], in1=M5[m][0:pp, cs],
                                                                 op0=ALU.mult, op1=ALU.mult),
                         r=[rpd[d], rIG, rM5], w=[ryt[y]])
                k.dma("pool", X.part(name)[:, :], yt[y][0:pp, :], r=[ryt[y], rIG], w=[rX],
                      indirect=dict(out_offset=bass.IndirectOffsetOnAxis(ap=it_[:, e_, j:j + 1], axis=0), in_offset=None,
                                    compute_op=ALU.add))

        loadw_gu(0)
        loadw_d(0)
        for e_ in range(NE):
            gather_T(e_, "L")
            gather_T(e_, "C")
            gate_up(e_)
            if e_ + 1 < NE:
                loadw_gu(e_ + 1)
            down_scatter(e_, "L", 0)
            down_scatter(e_, "C", 1)
            if e_ + 1 < NE:
                loadw_d(e_ + 1)


def emit_e2(P, QT, KT, V, BT, MK, CATT):
    k = P.k
    scale = 128 ** -0.5
    with Stage(P):
        ones = P.sb([128, 128], BF16); rones = Res()
        k.op("pool", lambda e: e.memset(ones[:], 1.0), w=[rones])
        mk = P.sb([128, 64]); rmk = Res()
        k.dma("sp", mk[0:64, :], MK[:, :], w=[rmk])
        k.dma("sp", mk[64:128, :], MK[:, :], w=[rmk])
        NBUF = 2
        KTh = [P.sb([128, NTOK], BF16) for _ in range(NBUF)]
        QTh = [P.sb([128, NTOK], BF16) for _ in range(NBUF)]
        Ve = [P.sb([128, 32, 128], BF16) for _ in range(NBUF)]
        Vo = [P.sb([128, 31, 128], BF16) for _ in range(NBUF)]
        Vc = [P.sb([128, 2, 128], BF16) for _ in range(NBUF)]
        TB = [P.sb([128, 14, 64]) for _ in range(NBUF)]
        osb = [P.sb([128, NTOK], BF16) for _ in range(NBUF)]
        rin = [Res() for _ in range(NBUF)]; rTB = [Res() for _ in range(NBUF)]; rosb = [Res() for _ in range(NBUF)]
        S = [P.ps([128, 512]) for _ in range(2)]; rS = [Res(), Res()]
        po = [P.ps([128, 512]) for _ in range(2)]; rpo = [Res(), Res()]
        tmp = [P.sb([128, 256]) for _ in range(2)]; rtmp = [Res(), Res()]
        Pm = [P.sb([128, 512], BF16) for _ in range(2)]; rPm = [Res(), Res()]
        rec = [P.sb([128, 256]) for _ in range(2)]; rrec = [Res(), Res()]

        def loadh(h):
            b = h % NBUF
            hc = slice(h * 128, (h + 1) * 128)
            k.dma("sp", KTh[b][:], KT[hc, :], w=[rin[b]])
            k.dma("sp", QTh[b][:], QT[hc, :], w=[rin[b]])
            k.dma("sp", Ve[b][:], V[CTX:NTOK, hc].rearrange("(j p) d -> p j d", p=128), w=[rin[b]])
            k.dma("sp", Vo[b][:], V[CTX + 64:CTX + 64 + 31 * 128, hc].rearrange("(j p) d -> p j d", p=128), w=[rin[b]])
            k.dma("sp", Vc[b][:], V[0:CTX, hc].rearrange("(j p) d -> p j d", p=128), w=[rin[b]])
            k.dma("sp", TB[b][0:64, :, :], BT[h, 0:14, :, :].rearrange("r k q -> k r q"), w=[rTB[b]])
            k.dma("sp", TB[b][64:128, :, :], BT[h, 1:15, :, :].rearrange("r k q -> k r q"), w=[rTB[b]])
            k.op("pool", lambda e: e.tensor_tensor(out=TB[b][:], in0=TB[b][:],
                                                   in1=mk[:].unsqueeze(1).to_broadcast([128, 14, 64]), op=ALU.add),
                 r=[rmk], w=[rTB[b]])

        cnt = [0]

        def attend1(b, q0, n, ktiles, bias):
            a = cnt[0] % 2; cnt[0] += 1
            nk = len(ktiles)
            Sv = S[a][:, 0:nk * n].rearrange("p (i q) -> p i q", i=nk)
            Pv = Pm[a][:, 0:nk * n].rearrange("p (i q) -> p i q", i=nk)
            for i, (kap, vap) in enumerate(ktiles):
                k.op("pe", lambda e: e.matmul(Sv[:, i, :], lhsT=kap, rhs=QTh[b][:, q0:q0 + n], start=True, stop=True),
                     r=[rin[b]], w=[rS[a]])
            nb_ = 0
            if bias is not None:
                nb_, bap = bias
                tv = tmp[a][:, 0:nb_ * n].rearrange("p (i q) -> p i q", i=nb_)
                k.op("dve", lambda e: e.scalar_tensor_tensor(out=tv, in0=Sv[:, 0:nb_, :], scalar=scale, in1=bap,
                                                             op0=ALU.mult, op1=ALU.add), r=[rS[a], rTB[b]], w=[rtmp[a]])
                k.op("act", lambda e: e.activation(out=Pv[:, 0:nb_, :], in_=tv, func=AF.Exp), r=[rtmp[a]], w=[rPm[a]])
            k.op("act", lambda e: e.activation(out=Pv[:, nb_:nk, :], in_=Sv[:, nb_:nk, :], func=AF.Exp, scale=scale),
                 r=[rS[a]], w=[rPm[a]])
            return (a, b, q0, n, ktiles)

        def attend2(st):
            a, b, q0, n, ktiles = st
            nk = len(ktiles)
            Pv = Pm[a][:, 0:nk * n].rearrange("p (i q) -> p i q", i=nk)
            pov = po[a][:, 0:2 * n].rearrange("p (i q) -> p i q", i=2)
            for i, (kap, vap) in enumerate(ktiles):
                k.op("pe", lambda e: e.matmul(pov[:, 0, :], lhsT=vap, rhs=Pv[:, i, :], start=(i == 0), stop=(i == nk - 1)),
                     r=[rin[b], rPm[a]], w=[rpo[a]])
            for i in range(nk):
                k.op("pe", lambda e: e.matmul(pov[:, 1, :], lhsT=ones[:], rhs=Pv[:, i, :], start=(i == 0), stop=(i == nk - 1)),
                     r=[rones, rPm[a]], w=[rpo[a]])
            k.op("dve", lambda e: e.reciprocal(out=rec[a][:, 0:n], in_=pov[:, 1, :]), r=[rpo[a]], w=[rrec[a]])
            k.op("dve", lambda e: e.tensor_tensor(out=osb[b][:, q0:q0 + n], in0=pov[:, 0, :], in1=rec[a][:, 0:n], op=ALU.mult),
                 r=[rpo[a], rrec[a]], w=[rosb[b]])

        loadh(0)
        for h in range(8):
            b = h % NBUF
            if h + 1 < 8:
                loadh(h + 1)
            ctxk = [(KTh[b][:, i * 128:(i + 1) * 128], Vc[b][:, i, :]) for i in range(2)]
            TB7 = TB[b][:].rearrange("p (a c) q -> p a c q", c=2)
            jobs = [(0, 256, ctxk, None)]
            for r in range(GRID):
                rs = min(max(r - 4, 0), GRID - 8)
                kt = []
                for i in range(4):
                    row = rs + 2 * i
                    tok = CTX + row * 64
                    vap = Ve[b][:, row // 2, :] if row % 2 == 0 else Vo[b][:, (row - 1) // 2, :]
                    kt.append((KTh[b][:, tok:tok + 128], vap))
                dr0 = rs - r + 7
                bap = TB7[:, dr0 // 2:dr0 // 2 + 4, dr0 % 2, :]
                jobs.append((CTX + r * 64, 64, kt + ctxk, (4, bap)))
            st = attend1(b, *jobs[0])
            for ji in range(len(jobs)):
                nxt = attend1(b, *jobs[ji + 1]) if ji + 1 < len(jobs) else None
                attend2(st)
                st = nxt
            k.dma("sp", CATT[h * 128:(h + 1) * 128, :], osb[b][:], r=[rosb[b]])


def emit_e3(P, HYP, sw, sbias, w1, b1, w2, b2, freq, w3, hyd, consts, identb_d, CATT):
    k = P.k
    with Stage(P):
        identb = P.sb([128, 128], BF16); rid = Res()
        k.dma("sp", identb[:], identb_d[:, :], w=[rid])
        ones = P.sb([128, 128]); rones = Res()
        k.op("pool", lambda e: e.memset(ones[:], 1.0), w=[rones])
        w1s = P.sb([33, 64]); w2s = P.sb([64, 64]); pr = P.sb([64, 8]); rpar = Res()
        k.dma("sp", w1s[:], w1[:, :], w=[rpar])
        k.dma("sp", w2s[:], w2[:, :], w=[rpar])
        k.dma("sp", pr[:, 0:1], b1[:, :], w=[rpar])
        k.dma("sp", pr[:, 1:2], b2[:, :], w=[rpar])
        k.dma("sp", pr[:, 2:3], freq[:, :], w=[rpar])
        k.op("dve", lambda e: e.tensor_scalar(out=pr[:, 3:4], in0=pr[:, 2:3], scalar1=1.0 / 3.0, scalar2=None, op0=ALU.mult),
             r=[rpar], w=[rpar])
        k.op("dve", lambda e: e.tensor_tensor(out=pr[:, 4:5], in0=pr[:, 3:4], in1=pr[:, 0:1], op=ALU.mult), r=[rpar], w=[rpar])
        k.op("dve", lambda e: e.tensor_tensor(out=pr[:, 5:6], in0=pr[:, 3:4], in1=pr[:, 1:2], op=ALU.mult), r=[rpar], w=[rpar])

        for (L, beta, tok0, cn) in ((CTX, 1, 0, consts["C"]), (SEQ, 259, CTX, consts["L"])):
            LT = L // 128
            bw = min(512, L)
            with Stage(P):
                hid2T = P.sb([64, L]); rh2 = Res()
                with Stage(P):
                    zT = P.sb([33, L]); rz = Res()
                    k.dma("sp", zT[:], cn["zT"][:, :], w=[rz])
                    hid1T = P.sb([64, L]); rh1 = Res()
                    pm = [P.ps([64, 512]) for _ in range(2)]; rpm = [Res(), Res()]
                    sn = P.sb([64, 512]); rsn = Res()
                    s2 = P.sb([64, 512]); rs2 = Res()
                    it = 0
                    for (lhs, src, rsrc, bcol, dst, rdst) in ((w1s, zT, rz, 4, hid1T, rh1), (w2s, hid1T, rh1, 5, hid2T, rh2)):
                        for nb in range(L // bw):
                            a = it % 2; it += 1
                            cs = slice(nb * bw, (nb + 1) * bw)
                            k.op("pe", lambda e: e.matmul(pm[a][:, 0:bw], lhsT=lhs[:], rhs=src[:, cs], start=True, stop=True),
                                 r=[rpar, rsrc], w=[rpm[a]])
                            k.op("act", lambda e: e.activation(out=sn[:, 0:bw], in_=pm[a][:, 0:bw], func=AF.Sin,
                                                               bias=pr[:, bcol:bcol + 1], scale=pr[:, 3:4]),
                                 r=[rpm[a], rpar], w=[rsn])
                            k.op("dve", lambda e: e.tensor_tensor(out=s2[:, 0:bw], in0=sn[:, 0:bw], in1=sn[:, 0:bw], op=ALU.mult),
                                 r=[rsn], w=[rs2])
                            k.op("dve", lambda e: e.tensor_scalar(out=s2[:, 0:bw], in0=s2[:, 0:bw], scalar1=-4.0, scalar2=3.0,
                                                                  op0=ALU.mult, op1=ALU.add), r=[rs2], w=[rs2])
                            k.op("dve", lambda e: e.tensor_tensor(out=dst[:, cs], in0=s2[:, 0:bw], in1=sn[:, 0:bw], op=ALU.mult),
                                 r=[rs2, rsn], w=[rdst])
                negt = P.sb([128, LT]); cst = P.sb([128, 3, LT]); rcn = Res()
                k.dma("sp", negt[:], cn["negt"][:, :], w=[rcn])
                k.dma("sp", cst[:], cn["cs"][:, :, :], w=[rcn])
                G2 = P.sb([128, LT, 512], BF16); rG = Res()
                U = P.sb([128, LT, 256], BF16); rU = Res()
                YR = P.sb([128, LT, 256], BF16); YI = P.sb([128, LT, 256], BF16); rY = Res()
                outT = P.sb([128, 2, L], BF16); routT = Res()
                Wc = P.sb([128, 3, 3, 256]); Bc = P.sb([128, 3, 256]); rWc = Res()
                dB = P.sb([128, 2, 256]); deltab = P.sb([128, 256]); rdB = Res()
                w3g = P.sb([64, 2, 256]); rw3 = Res()
                cv = [P.sb([128, 3, 256]) for _ in range(2)]; rcv = [Res(), Res()]
                c1 = P.sb([128, 3, 256]); rc1 = Res()
                xp = P.sb([128, 256]); rxp = Res()
                ccnt = [0]

                def conv_tile(part, g, nt):
                    s = ccnt[0] % 2; ccnt[0] += 1
                    col = part * 1024 + g * 256
                    r0 = beta + nt * 128
                    for j in range(3):
                        k.dma("sp", cv[s][:, j, :], HYP[r0 - 1 + j:r0 - 1 + j + 128, col:col + 256], w=[rcv[s]])
                    k.op("pool", lambda e: e.tensor_tensor(out=c1[:], in0=cv[s][:], in1=Wc[:, :, part, :], op=ALU.mult),
                         r=[rcv[s], rWc], w=[rc1])
                    k.op("dve", lambda e: e.tensor_tensor(out=c1[:, 0, :], in0=c1[:, 0, :], in1=c1[:, 1, :], op=ALU.add),
                         r=[rc1], w=[rc1])
                    k.op("dve", lambda e: e.tensor_tensor(out=c1[:, 2, :], in0=c1[:, 2, :], in1=Bc[:, part, :], op=ALU.add),
                         r=[rc1, rWc], w=[rc1])
                    k.op("dve", lambda e: e.tensor_tensor(out=xp[:], in0=c1[:, 0, :], in1=c1[:, 2, :], op=ALU.add),
                         r=[rc1], w=[rxp])

                for g in range(4):
                    gc = slice(g * 256, (g + 1) * 256)
                    for j in range(3):
                        k.dma("sp", Wc[:, j, :, :], sw[j:j + 1, :].rearrange("o (p c) -> o p c", p=3)[:, :, gc].to_broadcast([128, 3, 256]),
                              w=[rWc])
                    k.dma("sp", Bc[:], sbias[0:1, :].rearrange("o (p c) -> o p c", p=3)[:, :, gc].to_broadcast([128, 3, 256]), w=[rWc])
                    k.dma("sp", dB[:], hyd[:, gc].unsqueeze(0).to_broadcast([128, 2, 256]), w=[rdB])
                    k.dma("sp", deltab[:], bcast_rows(consts["deltas"][0:1, gc]), w=[rdB])
                    for o in range(2):
                        k.dma("sp", w3g[:], w3[:, 2 * o * 1024:(2 * o + 2) * 1024].rearrange("k (d c) -> k d c", d=2)[:, :, gc], w=[rw3])
                        with Stage(P):
                            pf = [P.ps([128, 512]) for _ in range(2)]; rpf = [Res(), Res()]
                            pss = P.ps([128, 512]); rpss = Res()
                            dec = [P.sb([128, 256]) for _ in range(2)]; rdec = [Res(), Res()]
                            fsb = [P.sb([128, 2, 256]) for _ in range(2)]; rfsb = [Res(), Res()]
                            sq = [P.sb([128, 512]) for _ in range(2)]; rsq = [Res(), Res()]
                            rstd = P.sb([128, 2, 256]); rrs = Res()
                            for ps_ in range(2):
                                for tt in range(LT):
                                    a = tt % 2
                                    k.op("pe", lambda e: e.matmul(pf[a][:], lhsT=hid2T[:, tt * 128:(tt + 1) * 128],
                                                                  rhs=w3g[:].rearrange("k d c -> k (d c)"), start=True, stop=True),
                                         r=[rh2, rw3], w=[rpf[a]])
                                    k.op("act", lambda e: e.activation(out=dec[a][:], in_=deltab[:], func=AF.Exp,
                                                                       scale=negt[:, tt:tt + 1]), r=[rdB, rcn], w=[rdec[a]])
                                    k.op("dve", lambda e: e.tensor_tensor(out=fsb[a][:], in0=pf[a][:].rearrange("p (d c) -> p d c", d=2),
                                                                          in1=dec[a][:].unsqueeze(1).to_broadcast([128, 2, 256]),
                                                                          op=ALU.mult), r=[rpf[a], rdec[a]], w=[rfsb[a]])
                                    if ps_ == 0:
                                        k.op("pool", lambda e: e.tensor_tensor(out=sq[a][:].rearrange("p (d c) -> p d c", d=2),
                                                                               in0=fsb[a][:], in1=fsb[a][:], op=ALU.mult),
                                             r=[rfsb[a]], w=[rsq[a]])
                                        k.op("pe", lambda e: e.matmul(pss[:], lhsT=ones[:], rhs=sq[a][:], start=(tt == 0),
                                                                      stop=(tt == LT - 1)), r=[rones, rsq[a]], w=[rpss],
                                             skip_same=(tt > 0))
                                    else:
                                        k.op("pool", lambda e: e.tensor_tensor(out=fsb[a][:], in0=fsb[a][:], in1=rstd[:], op=ALU.mult),
                                             r=[rrs], w=[rfsb[a]])
                                        k.op("dve", lambda e: e.tensor_tensor(out=G2[:, tt, 0:256], in0=fsb[a][:, 0, :], in1=fsb[a][:, 1, :],
                                                                              op=ALU.add), r=[rfsb[a]], w=[rG])
                                        k.op("pool", lambda e: e.tensor_tensor(out=G2[:, tt, 256:512], in0=fsb[a][:, 0, :], in1=fsb[a][:, 1, :],
                                                                               op=ALU.subtract), r=[rfsb[a]], w=[rG])
                                if ps_ == 0:
                                    rv = rstd[:].rearrange("p d c -> p (d c)")
                                    k.op("dve", lambda e: e.tensor_scalar(out=rv, in0=pss[:], scalar1=EPS, scalar2=None, op0=ALU.add),
                                         r=[rpss], w=[rrs])
                                    k.op("act", lambda e: e.activation(out=rv, in_=rv, func=AF.Sqrt), r=[rrs], w=[rrs])
                                    k.op("dve", lambda e: e.reciprocal(out=rv, in_=rv), r=[rrs], w=[rrs])
                        if o == 0:
                            for nt in range(LT):
                                conv_tile(0, g, nt)
                                k.op("act", lambda e: e.copy(out=U[:, nt, :], in_=xp[:]), r=[rxp], w=[rU])
                        with Stage(P):
                            Ct = [P.sb([128, LT, 128], BF16) for _ in range(2)]
                            St = [P.sb([128, LT, 128], BF16) for _ in range(2)]
                            rCS = [Res(), Res()]
                            bA = [P.ps([128, 512]) for _ in range(2)]; bB = [P.ps([128, 512]) for _ in range(2)]
                            bC = [P.ps([128, 256]) for _ in range(2)]; bS = [P.ps([128, 256]) for _ in range(2)]
                            raccs = [Res(), Res()]
                            PuS = P.sb([128, 256]); QuS = P.sb([128, 256]); rPQ = Res()
                            t1 = P.sb([128, 256]); t2 = P.sb([128, 256]); rt = Res()
                            Hre = P.sb([128, 256]); Him = P.sb([128, 256]); rH = Res()
                            a1 = P.sb([128, 256]); a2 = P.sb([128, 256]); ra = Res()
                            b1_ = P.sb([128, 256]); b2_ = P.sb([128, 256]); rb = Res()

                            def loadcs(ft):
                                s = ft % 2
                                k.dma("sp", Ct[s][:], cn["C2"][ft, :, :, :], w=[rCS[s]])
                                k.dma("sp", St[s][:], cn["S2"][ft, :, :, :], w=[rCS[s]])

                            loadcs(0)
                            for ft in range(LT):
                                s = ft % 2
                                racc = raccs[s]
                                acc = [bA[s][:, 0:256], bB[s][:, 0:256], bA[s][:, 256:512], bB[s][:, 256:512], bC[s][:], bS[s][:]]
                                if ft + 1 < LT:
                                    loadcs(ft + 1)
                                for tt in range(LT):
                                    st_ = (tt == 0); sp_ = (tt == LT - 1)
                                    k.op("pe", lambda e: e.matmul(bA[s][:], lhsT=Ct[s][:, tt, :], rhs=G2[:, tt, :], start=st_, stop=sp_),
                                         r=[rCS[s], rG], w=[racc])
                                    k.op("pe", lambda e: e.matmul(bC[s][:], lhsT=Ct[s][:, tt, :], rhs=U[:, tt, :], start=st_, stop=sp_),
                                         r=[rCS[s], rU], w=[racc])
                                    k.op("pe", lambda e: e.matmul(bB[s][:], lhsT=St[s][:, tt, :], rhs=G2[:, tt, :], start=st_, stop=sp_),
                                         r=[rCS[s], rG], w=[racc])
                                    k.op("pe", lambda e: e.matmul(bS[s][:], lhsT=St[s][:, tt, :], rhs=U[:, tt, :], start=st_, stop=sp_),
                                         r=[rCS[s], rU], w=[racc])
                                cc = cst[:, 0, ft:ft + 1]; ss_ = cst[:, 1, ft:ft + 1]; nc_ = cst[:, 2, ft:ft + 1]
                                k.op("act", lambda e: e.copy(out=PuS[:], in_=acc[4][:]), r=[racc], w=[rPQ])
                                k.op("act", lambda e: e.copy(out=QuS[:], in_=acc[5][:]), r=[racc], w=[rPQ])
                                k.op("dve", lambda e: e.tensor_scalar(out=t1[:], in0=acc[0][:], scalar1=cc, scalar2=None, op0=ALU.mult),
                                     r=[racc, rcn], w=[rt])
                                k.op("dve", lambda e: e.scalar_tensor_tensor(out=t1[:], in0=acc[1][:], scalar=ss_, in1=t1[:],
                                                                             op0=ALU.mult, op1=ALU.add), r=[racc, rcn], w=[rt])
                                k.op("dve", lambda e: e.tensor_scalar(out=t2[:], in0=acc[2][:], scalar1=ss_, scalar2=None, op0=ALU.mult),
                                     r=[racc, rcn], w=[rt])
                                k.op("dve", lambda e: e.scalar_tensor_tensor(out=Him[:], in0=acc[3][:], scalar=nc_, in1=t2[:],
                                                                             op0=ALU.mult, op1=ALU.add), r=[racc, rcn, rt], w=[rH])
                                k.op("pool", lambda e: e.tensor_tensor(out=Hre[:], in0=t1[:], in1=dB[:, o, :], op=ALU.add),
                                     r=[rt, rdB], w=[rH])
                                k.op("dve", lambda e: e.tensor_tensor(out=a1[:], in0=Hre[:], in1=PuS[:], op=ALU.mult), r=[rH, rPQ], w=[ra])
                                k.op("dve", lambda e: e.tensor_tensor(out=a2[:], in0=Him[:], in1=QuS[:], op=ALU.mult), r=[rH, rPQ], w=[ra])
                                k.op("dve", lambda e: e.tensor_tensor(out=YR[:, ft, :], in0=a1[:], in1=a2[:], op=ALU.add), r=[ra], w=[rY])
                                k.op("pool", lambda e: e.tensor_tensor(out=b1_[:], in0=Hre[:], in1=QuS[:], op=ALU.mult), r=[rH, rPQ], w=[rb])
                                k.op("pool", lambda e: e.tensor_tensor(out=b2_[:], in0=Him[:], in1=PuS[:], op=ALU.mult), r=[rH, rPQ], w=[rb])
                                k.op("pool", lambda e: e.tensor_tensor(out=YI[:, ft, :], in0=b1_[:], in1=b2_[:], op=ALU.subtract),
                                     r=[rb], w=[rY])
                        with Stage(P):
                            Ct = [P.sb([128, LT, 128], BF16) for _ in range(2)]
                            St = [P.sb([128, LT, 128], BF16) for _ in range(2)]
                            rCS = [Res(), Res()]
                            py = [P.ps([128, 256]) for _ in range(2)]; rpy = [Res(), Res()]
                            zb = [P.sb([128, 256], BF16) for _ in range(2)]; rzb = [Res(), Res()]
                            pz = [P.ps([128, 256], BF16) for _ in range(2)]; rpz = [Res(), Res()]

                            def loadcs2(nt):
                                s = nt % 2
                                k.dma("sp", Ct[s][:], cn["C2"][nt, :, :, :], w=[rCS[s]])
                                k.dma("sp", St[s][:], cn["S2"][nt, :, :, :], w=[rCS[s]])

                            loadcs2(0)
                            for nt in range(LT):
                                s = nt % 2
                                if nt + 1 < LT:
                                    loadcs2(nt + 1)
                                for ft in range(LT):
                                    k.op("pe", lambda e: e.matmul(py[s][:], lhsT=Ct[s][:, ft, :], rhs=YR[:, ft, :],
                                                                  start=(ft == 0), stop=False), r=[rCS[s], rY], w=[rpy[s]],
                                         skip_same=(ft > 0))
                                    k.op("pe", lambda e: e.matmul(py[s][:], lhsT=St[s][:, ft, :], rhs=YI[:, ft, :],
                                                                  start=False, stop=(ft == LT - 1)), r=[rCS[s], rY], w=[rpy[s]],
                                         skip_same=True)
                                conv_tile(o + 1, g, nt)
                                if o == 0:
                                    k.op("dve", lambda e: e.scalar_tensor_tensor(out=U[:, nt, :], in0=py[s][:], scalar=1.0 / L, in1=xp[:],
                                                                                 op0=ALU.mult, op1=ALU.mult), r=[rpy[s], rxp], w=[rU])
                                else:
                                    k.op("dve", lambda e: e.scalar_tensor_tensor(out=zb[s][:], in0=py[s][:], scalar=1.0 / L, in1=xp[:],
                                                                                 op0=ALU.mult, op1=ALU.mult), r=[rpy[s], rxp], w=[rzb[s]])
                                    for hh in range(2):
                                        k.op("pe", lambda e: e.transpose(out=pz[s][:, hh * 128:(hh + 1) * 128],
                                                                         in_=zb[s][:, hh * 128:(hh + 1) * 128], identity=identb[:]),
                                             r=[rzb[s], rid], w=[rpz[s]], skip_same=(hh > 0))
                                    k.op("act", lambda e: e.copy(out=outT[:, :, nt * 128:(nt + 1) * 128],
                                                                 in_=pz[s][:].rearrange("p (h t) -> p h t", h=2)), r=[rpz[s]], w=[routT])
                            if o == 1:
                                for hh in range(2):
                                    r0 = 1024 + g * 256 + hh * 128
                                    k.dma("sp", CATT[r0:r0 + 128, tok0:tok0 + L], outT[:, hh, :], r=[routT])


def emit_m1(P, X, md, g1, w_down, qag, kvag, w_uq, w_ukv, qg, kg, rope, identb_d, QAT, KR, QT2, KT2, V2):
    k = P.k
    with Stage(P):
        identb = P.sb([128, 128], BF16); rid = Res()
        k.dma("sp", identb[:], identb_d[:, :], w=[rid])
        wdn = P.sb([128, 16, 1344], BF16); rwd = Res()
        for (c0, c1) in ((0, 512), (512, 1024), (1024, 1344)):
            k.dma("pool", wdn[:, :, c0:c1], w_down[:, c0:c1].rearrange("(kc p) n -> p kc n", p=128), w=[rwd])
        gq = P.sb([128, 1280]); rgq = Res()
        k.dma("sp", gq[:, 0:768], bcast_rows(qag), w=[rgq])
        k.dma("sp", gq[:, 768:1280], bcast_rows(kvag), w=[rgq])
        A, B, rAB = load_mod_AB(P, md, g1, 1, 0)
        nt_ = NormT(P, X, A, B, rAB, identb, rid)
        hT = P.sb([128, 16, 128], BF16); rhT = Res()
        acc = [P.ps([128, 512]) for _ in range(2)]; racc = [Res(), Res()]
        da = P.sb([128, 1344]); rda = Res()
        sq = P.sb([128, 1280]); rsq = Res()
        s2 = P.sb([128, 4]); rs2 = Res()
        nb_ = [P.sb([128, 1280], BF16) for _ in range(2)]; rnb = [Res(), Res()]
        pT = P.ps([128, 1280], BF16); rpT = Res()
        nT = [P.sb([128, 10, 128], BF16) for _ in range(2)]; rnT = [Res(), Res()]
        it = 0
        for t in range(NT):
            s = t % 2
            rows = slice(t * 128, (t + 1) * 128)
            nt_.run([t], hT, rhT)
            for (c0, c1) in ((0, 512), (512, 1024), (1024, 1344)):
                a = it % 2; it += 1
                for kc in range(16):
                    k.op("pe", lambda e: e.matmul(acc[a][:, 0:c1 - c0], lhsT=hT[:, kc, :], rhs=wdn[:, kc, c0:c1],
                                                  start=(kc == 0), stop=(kc == 15)), r=[rhT, rwd], w=[racc[a]], skip_same=(kc > 0))
                k.op("act", lambda e: e.copy(out=da[:, c0:c1], in_=acc[a][:, 0:c1 - c0]), r=[racc[a]], w=[rda])
            k.dma("sp", KR[rows, :], da[:, 1280:1344], r=[rda])
            k.op("act", lambda e: e.activation(out=sq[:], in_=da[:, 0:1280], func=AF.Square), r=[rda], w=[rsq])
            k.op("dve", lambda e: e.reduce_sum(out=s2[:, 0:1], in_=sq[:, 0:768], axis=AX.X), r=[rsq], w=[rs2])
            k.op("dve", lambda e: e.reduce_sum(out=s2[:, 1:2], in_=sq[:, 768:1280], axis=AX.X), r=[rsq], w=[rs2])
            rstd_from_ss(k, s2[:, 2:3], s2[:, 0:1], 768, [rs2], [rs2])
            rstd_from_ss(k, s2[:, 3:4], s2[:, 1:2], 512, [rs2], [rs2])
            k.op("dve", lambda e: e.scalar_tensor_tensor(out=nb_[s][:, 0:768], in0=da[:, 0:768], scalar=s2[:, 2:3], in1=gq[:, 0:768],
                                                         op0=ALU.mult, op1=ALU.mult), r=[rda, rs2, rgq], w=[rnb[s]])
            k.op("dve", lambda e: e.scalar_tensor_tensor(out=nb_[s][:, 768:1280], in0=da[:, 768:1280], scalar=s2[:, 3:4],
                                                         in1=gq[:, 768:1280], op0=ALU.mult, op1=ALU.mult),
                 r=[rda, rs2, rgq], w=[rnb[s]])
            for c in range(10):
                k.op("pe", lambda e: e.transpose(out=pT[:, c * 128:(c + 1) * 128], in_=nb_[s][:, c * 128:(c + 1) * 128],
                                                 identity=identb[:]), r=[rnb[s], rid], w=[rpT], skip_same=(c > 0))
            k.op("act", lambda e: e.copy(out=nT[s][:], in_=pT[:].rearrange("p (c t) -> p c t", c=10)), r=[rpT], w=[rnT[s]])
            k.dma("sp", QAT[:, rows].rearrange("(c p) t -> p c t", p=128), nT[s][:], r=[rnT[s]])
    with Stage(P):
        identb = P.sb([128, 128], BF16); rid = Res()
        k.dma("sp", identb[:], identb_d[:, :], w=[rid])
        wuq = P.sb([128, 6, 3072], BF16); wukv = P.sb([128, 4, 4096], BF16); rwu = Res()
        for q in range(6):
            k.dma("pool", wuq[:, :, q * 512:(q + 1) * 512], w_uq[:, q * 512:(q + 1) * 512].rearrange("(kc p) n -> p kc n", p=128), w=[rwu])
        for q in range(8):
            k.dma("pool", wukv[:, :, q * 512:(q + 1) * 512], w_ukv[:, q * 512:(q + 1) * 512].rearrange("(kc p) n -> p kc n", p=128), w=[rwu])
        gt = P.sb([128, 2, 192]); rgt = Res()
        k.dma("sp", gt[:, 0, :], bcast_rows(qg), w=[rgt])
        k.dma("sp", gt[:, 1, :], bcast_rows(kg), w=[rgt])
        nT = [P.sb([128, 10, 128], BF16) for _ in range(2)]
        kr = [P.sb([128, 64]) for _ in range(2)]
        rp = [P.sb([128, 3, 32]) for _ in range(2)]
        rin = [Res(), Res()]
        acc = [P.ps([128, 512]) for _ in range(2)]; racc = [Res(), Res()]
        fall = [P.sb([128, 16, 192]) for _ in range(2)]; rfall = [Res(), Res()]
        sqa = P.sb([128, 16, 192]); rsqa = Res()
        s16 = P.sb([128, 32]); rs16 = Res()
        tA = P.sb([128, 16, 2, 16]); tB = P.sb([128, 16, 2, 16]); rtAB = Res()
        ball = [P.sb([128, 16, 192], BF16) for _ in range(2)]; rball = [Res(), Res()]
        vball = [P.sb([128, 16, 128], BF16) for _ in range(2)]; rvb = [Res(), Res()]
        pq = [P.ps([128, 8, 2, 128], BF16) for _ in range(2)]; rpq = [Res(), Res()]
        qTs = [P.sb([128, 8, 2, 128], BF16) for _ in range(2)]; rqTs = [Res(), Res()]

        def loadt(t):
            s = t % 2
            rows = slice(t * 128, (t + 1) * 128)
            k.dma("sp", nT[s][:], QAT[:, rows].rearrange("(c p) t -> p c t", p=128), w=[rin[s]])
            k.dma("sp", kr[s][:], KR[rows, :], w=[rin[s]])
            k.dma("sp", rp[s][:], rope[rows, :, :], w=[rin[s]])

        hcnt = [0]
        ones3 = P.sb([128, 16, 64]); rones3 = Res()
        k.op("dve", lambda e: e.memset(ones3[:], 1.0), w=[rones3])

        def norm_rope_T(f, gi, dstT, s, rows):
            F_ = fall[f]; rF = rfall[f]; Bq = ball[f]; rB = rball[f]
            k.op("act", lambda e: e.activation(out=sqa[:], in_=F_[:], func=AF.Square), r=[rF], w=[rsqa])
            k.op("dve", lambda e: e.reduce_sum(out=s16[:, 0:16], in_=sqa[:], axis=AX.X), r=[rsqa], w=[rs16])
            if CUT <= 1:
                return
            rstd_from_ss(k, s16[:, 16:32], s16[:, 0:16], 192, [rs16], [rs16])
            k.op("dve", lambda e: e.tensor_tensor(out=F_[:], in0=F_[:], in1=s16[:, 16:32].unsqueeze(2).to_broadcast([128, 16, 192]),
                                                  op=ALU.mult), r=[rs16], w=[rF])
            k.op("dve", lambda e: e.tensor_tensor(out=F_[:], in0=F_[:], in1=gt[:, gi:gi + 1, :].to_broadcast([128, 16, 192]),
                                                   op=ALU.mult), r=[rgt], w=[rF])
            k.op("act", lambda e: e.copy(out=Bq[:, :, 0:128], in_=F_[:, :, 0:128]), r=[rF], w=[rB])
            if CUT <= 2:
                return
            for g in range(2):
                x = F_[:, :, 128 + g * 32:128 + (g + 1) * 32].rearrange("p h (s d) -> p h s d", s=2)
                o = Bq[:, :, 128 + g * 32:128 + (g + 1) * 32].rearrange("p h (s d) -> p h s d", s=2)
                cg = rp[s][:, 0, g * 16:(g + 1) * 16]
                sg = rp[s][:, 1, g * 16:(g + 1) * 16]
                ng = rp[s][:, 2, g * 16:(g + 1) * 16]
                k.op("dve", lambda e: e.tensor_tensor(out=tA[:], in0=x, in1=cg.unsqueeze(1).unsqueeze(1).to_broadcast([128, 16, 2, 16]),
                                                      op=ALU.mult), r=[rF, rin[s]], w=[rtAB])
                k.op("dve", lambda e: e.tensor_tensor(out=tB[:, :, 0, :], in0=x[:, :, 1, :],
                                                       in1=ng.unsqueeze(1).to_broadcast([128, 16, 16]), op=ALU.mult), r=[rF, rin[s]], w=[rtAB])
                k.op("dve", lambda e: e.tensor_tensor(out=tB[:, :, 1, :], in0=x[:, :, 0, :],
                                                       in1=sg.unsqueeze(1).to_broadcast([128, 16, 16]), op=ALU.mult), r=[rF, rin[s]], w=[rtAB])
                k.op("dve", lambda e: e.tensor_tensor(out=o, in0=tA[:], in1=tB[:], op=ALU.add), r=[rtAB], w=[rB])
            if CUT <= 3:
                return
            dv = dstT.rearrange("(h r) t -> r h t", r=192)
            for half in range(2):
                a = hcnt[0] % 2; hcnt[0] += 1
                for hh in range(8):
                    h = half * 8 + hh
                    k.op("pe", lambda e: e.transpose(out=pq[a][:, hh, 0, :], in_=Bq[:, h, 0:128], identity=identb[:]),
                         r=[rB, rid], w=[rpq[a]])
                    k.op("pe", lambda e: e.transpose(out=pq[a][0:64, hh, 1, :], in_=Bq[:, h, 128:192], identity=identb[:]),
                         r=[rB, rid], w=[rpq[a]])
                k.op("act", lambda e: e.copy(out=qTs[a][:, :, 0, :], in_=pq[a][:, :, 0, :]), r=[rpq[a]], w=[rqTs[a]])
                k.op("act", lambda e: e.copy(out=qTs[a][0:64, :, 1, :], in_=pq[a][0:64, :, 1, :]), r=[rpq[a]], w=[rqTs[a]])
                k.dma("sp", dv[0:128, half * 8:(half + 1) * 8, rows], qTs[a][:, :, 0, :], r=[rqTs[a]])
                k.dma("sp", dv[128:192, half * 8:(half + 1) * 8, rows], qTs[a][0:64, :, 1, :], r=[rqTs[a]])

        loadt(0)
        it = 0
        for t in range(NT):
            s = t % 2
            rows = slice(t * 128, (t + 1) * 128)
            if t + 1 < NT:
                loadt(t + 1)
            for blk in range(8):
                a = it % 2; it += 1
                cs = slice(blk * 384, (blk + 1) * 384)
                for kc in range(6):
                    k.op("pe", lambda e: e.matmul(acc[a][:, 0:384], lhsT=nT[s][:, kc, :], rhs=wuq[:, kc, cs],
                                                  start=(kc == 0), stop=(kc == 5)), r=[rin[s], rwu], w=[racc[a]])
                k.op("act", lambda e: e.copy(out=fall[0][:, 2 * blk:2 * blk + 2, :],
                                             in_=acc[a][:, 0:384].rearrange("p (h d) -> p h d", h=2)), r=[racc[a]], w=[rfall[0]])
            norm_rope_T(0, 0, QT2, s, rows)
            if CUT <= 4:
                continue
            for blk in range(8):
                a = it % 2; it += 1
                cs = slice(blk * 512, (blk + 1) * 512)
                for kc in range(4):
                    k.op("pe", lambda e: e.matmul(acc[a][:], lhsT=nT[s][:, 6 + kc, :], rhs=wukv[:, kc, cs],
                                                  start=(kc == 0), stop=(kc == 3)), r=[rin[s], rwu], w=[racc[a]])
                av = acc[a][:].rearrange("p (h c d) -> p h c d", h=2, c=2)
                k.op("act", lambda e: e.copy(out=fall[1][:, 2 * blk:2 * blk + 2, 0:128], in_=av[:, :, 0, :]), r=[racc[a]], w=[rfall[1]])
                k.op("act", lambda e: e.copy(out=vball[s][:, 2 * blk:2 * blk + 2, :], in_=av[:, :, 1, :]), r=[racc[a]], w=[rvb[s]])
            k.op("dve", lambda e: e.tensor_tensor(out=fall[1][:, :, 128:192], in0=ones3[:], in1=kr[s][:].unsqueeze(1).to_broadcast([128, 16, 64]),
                                                  op=ALU.mult), r=[rin[s], rones3], w=[rfall[1]])
            for q_ in range(4):
                k.dma("sp", V2[rows, q_ * 512:(q_ + 1) * 512], vball[s][:, 4 * q_:4 * q_ + 4, :].rearrange("p h d -> p (h d)"), r=[rvb[s]])
            if CUT <= 5:
                continue
            norm_rope_T(1, 1, KT2, s, rows)


def emit_m2(P, QT2, KT2, V2, CATT):
    k = P.k
    scale = 192 ** -0.5
    with Stage(P):
        ones = P.sb([128, 128]); rones = Res()
        k.op("pool", lambda e: e.memset(ones[:], 1.0), w=[rones])
        KA = [P.sb([128, NTOK], BF16) for _ in range(2)]; KB = [P.sb([64, NTOK], BF16) for _ in range(2)]
        QA = [P.sb([128, NTOK], BF16) for _ in range(2)]; QB = [P.sb([64, NTOK], BF16) for _ in range(2)]
        Vh = [P.sb([128, NT, 128], BF16) for _ in range(2)]
        rin = [Res(), Res()]
        S = [P.ps([128, 512]) for _ in range(2)]; rS = [Res(), Res()]
        po = [P.ps([128, 512]) for _ in range(2)]; rpo = [Res(), Res()]
        pd = [P.ps([128, 512]) for _ in range(2)]; rpd = [Res(), Res()]
        Pm = [P.sb([128, 512], BF16) for _ in range(4)]; rPm = [Res() for _ in range(4)]
        dsum = [[P.sb([128, 512]) for _ in range(2)] for _ in range(2)]; rds = [[Res(), Res()], [Res(), Res()]]
        rec = P.sb([128, 512]); rrec = Res()
        osb = [P.sb([128, 512], BF16) for _ in range(2)]; rosb = [Res(), Res()]

        def loadh(h):
            b = h % 2
            k.dma("sp", KA[b][:], KT2[h * 192:h * 192 + 128, :], w=[rin[b]])
            k.dma("sp", KB[b][:], KT2[h * 192 + 128:h * 192 + 192, :], w=[rin[b]])
            k.dma("sp", QA[b][:], QT2[h * 192:h * 192 + 128, :], w=[rin[b]])
            k.dma("sp", QB[b][:], QT2[h * 192 + 128:h * 192 + 192, :], w=[rin[b]])
            k.dma("sp", Vh[b][:], V2[:, h * 128:(h + 1) * 128].rearrange("(j p) d -> p j d", p=128), w=[rin[b]])

        cnt = {"s": 0, "p": 0, "o": 0}
        loadh(0)
        for h in range(16):
            b = h % 2
            if h + 1 < 16:
                loadh(h + 1)
            blocks = [(0, 256, 2)] + [(CTX + i * 512, 512, NT) for i in range(8)]
            for (q0, n, nk) in blocks:
                o = cnt["o"] % 2; cnt["o"] += 1

                def qk(kt):
                    a = kt % 2
                    ks = slice(kt * 128, (kt + 1) * 128)
                    k.op("pe", lambda e: e.matmul(S[a][:, 0:n], lhsT=KA[b][:, ks], rhs=QA[b][:, q0:q0 + n], start=True, stop=False),
                         r=[rin[b]], w=[rS[a]])
                    k.op("pe", lambda e: e.matmul(S[a][:, 0:n], lhsT=KB[b][:, ks], rhs=QB[b][:, q0:q0 + n], start=False, stop=True),
                         r=[rin[b]], w=[rS[a]])

                qk(0)
                for kt in range(nk):
                    a = kt % 2
                    pi = cnt["p"] % 4; cnt["p"] += 1
                    if kt + 1 < nk:
                        qk(kt + 1)
                    k.op("act", lambda e: e.activation(out=Pm[pi][:, 0:n], in_=S[a][:, 0:n], func=AF.Exp, scale=scale),
                         r=[rS[a]], w=[rPm[pi]])
                    k.op("pe", lambda e: e.matmul(po[o][:, 0:n], lhsT=Vh[b][:, kt, :], rhs=Pm[pi][:, 0:n], start=(kt == 0),
                                                  stop=(kt == nk - 1)), r=[rin[b], rPm[pi]], w=[rpo[o]])
                    eng = "dve" if kt % 2 == 0 else "pool"
                    d_ = dsum[o][kt % 2]; rd_ = rds[o][kt % 2]
                    if kt < 2:
                        k.op(eng, lambda e: e.tensor_copy(out=d_[:, 0:n], in_=Pm[pi][:, 0:n]), r=[rPm[pi]], w=[rd_])
                    else:
                        k.op(eng, lambda e: e.tensor_tensor(out=d_[:, 0:n], in0=d_[:, 0:n], in1=Pm[pi][:, 0:n], op=ALU.add),
                             r=[rPm[pi]], w=[rd_])
                for i_ in range(2):
                    k.op("pe", lambda e: e.matmul(pd[o][:, 0:n], lhsT=ones[:], rhs=dsum[o][i_][:, 0:n], start=(i_ == 0), stop=(i_ == 1)),
                         r=[rones, rds[o][i_]], w=[rpd[o]])
                k.op("dve", lambda e: e.reciprocal(out=rec[:, 0:n], in_=pd[o][:, 0:n]), r=[rpd[o]], w=[rrec])
                k.op("dve", lambda e: e.tensor_tensor(out=osb[o][:, 0:n], in0=po[o][:, 0:n], in1=rec[:, 0:n], op=ALU.mult),
                     r=[rpo[o], rrec], w=[rosb[o]])
                k.dma("sp", CATT[h * 128:(h + 1) * 128, q0:q0 + n], osb[o][:, 0:n], r=[rosb[o]])


def _dft_consts(L):
    LT = L // 128
    n = np.arange(L, dtype=np.float64)
    theta = np.pi * np.outer(2 * n + 1, 2 * n + 1) / (4.0 * L)
    def tiled(m):
        return np.ascontiguousarray(m.reshape(LT, 128, LT, 128).transpose(2, 1, 0, 3)).astype(np.float32).astype(NPBF)
    C2 = tiled(np.cos(theta)); S2 = tiled(np.sin(theta))
    half = np.pi * (2 * n + 1) / (4.0 * L)
    cs = np.stack([np.cos(half), np.sin(half), -np.cos(half)], 0).reshape(3, LT, 128).transpose(2, 0, 1)
    t = np.linspace(0.0, 1.0, L)
    w = 2.0 * np.pi * np.arange(L) / L
    bands = np.linspace(1e-4, 15.0, 16)
    z = np.concatenate([t[:, None], np.cos(bands[None, :] * w[:, None]), -np.sin(bands[None, :] * w[:, None])], axis=1)
    negt = (-t).reshape(LT, 128).T
    return {"C2": C2, "S2": S2, "cs": np.ascontiguousarray(cs).astype(np.float32),
            "zT": np.ascontiguousarray(z.T).astype(np.float32), "negt": np.ascontiguousarray(negt).astype(np.float32)}


def _deltas():
    d_lo = np.log(1e-2) / 1.5
    d_hi = np.log(1e-2) / 0.3
    return np.abs(np.linspace(d_lo, d_hi, 1024)).astype(np.float32)[None, :]


def _rope_table():
    tab = np.zeros((NTOK, 3, 32), np.float32)
    tab[:CTX, 0, :] = 1.0
    t = np.arange(SEQ)
    inv = 10000.0 ** (-np.arange(0, 32, 2, dtype=np.float64) / 32.0)
    for g, pos in enumerate((t // GRID, t % GRID)):
        ang = pos[:, None].astype(np.float64) * inv[None, :]
        tab[CTX:, 0, g * 16:(g + 1) * 16] = np.cos(ang)
        tab[CTX:, 1, g * 16:(g + 1) * 16] = np.sin(ang)
        tab[CTX:, 2, g * 16:(g + 1) * 16] = -np.sin(ang)
    return tab


def _na_tables(rpb):
    kc = np.arange(64)[:, None]; qc = np.arange(64)[None, :]
    dc = np.clip(kc - qc + 15, 0, 30)
    BT = np.ascontiguousarray(rpb[:, :, :, dc])
    cstart = np.clip(qc - 8, 0, 48)
    inwin = (kc >= cstart) & (kc < cstart + 16)
    MK = np.where(inwin, 0.0, -1e30).astype(np.float32)
    return BT.astype(np.float32), np.ascontiguousarray(MK)


IDENTB = np.eye(128, dtype=np.float32).astype(NPBF)
IDENTF = np.eye(128, dtype=np.float32)


def emit_p0(P, c2T, ada_w, ada_b, MD):
    k = P.k
    with Stage(P):
        sc = P.sb([128, 16, 2]); rsc = Res()
        k.dma("sp", sc[:], c2T[:, :, :], w=[rsc])
        k.op("act", lambda e: e.activation(out=sc[:], in_=sc[:], func=AF.Silu), r=[rsc], w=[rsc])
        wt = [P.sb([128, 16, 512]) for _ in range(3)]; rw = [Res() for _ in range(3)]
        acc = [P.ps([128, 512]) for _ in range(2)]; racc = [Res(), Res()]
        bt = [P.sb([2, 512]) for _ in range(2)]; rb = [Res(), Res()]
        ot = [P.sb([2, 512]) for _ in range(2)]; ro = [Res(), Res()]
        blocks = [(l, nb) for l in range(DEPTH) for nb in range(24)]

        def loadw(i):
            l, nb = blocks[i]
            cs = slice(nb * 512, nb * 512 + 512)
            k.dma("sp" if i % 2 == 0 else "pool", wt[i % 3][:], ada_w[l, :, cs].rearrange("(kc p) n -> p kc n", p=128), w=[rw[i % 3]])

        loadw(0); loadw(1)
        for i, (l, nb) in enumerate(blocks):
            s = i % 2
            cs = slice(nb * 512, nb * 512 + 512)
            if i + 2 < len(blocks):
                loadw(i + 2)
            k.dma("sp", bt[s][:], ada_b[l:l + 1, cs].to_broadcast([2, 512]), w=[rb[s]])
            for kc in range(16):
                k.op("pe", lambda e: e.matmul(acc[s][0:2, :], lhsT=sc[:, kc, :], rhs=wt[i % 3][:, kc, :],
                                              start=(kc == 0), stop=(kc == 15)),
                     r=[rsc, rw[i % 3]], w=[racc[s]], skip_same=(kc > 0))
            k.op("dve", lambda e: e.tensor_tensor(out=ot[s][:], in0=acc[s][0:2, :], in1=bt[s][:], op=ALU.add),
                 r=[racc[s], rb[s]], w=[ro[s]])
            k.dma("sp", MD[l, :, cs], ot[s][:], r=[ro[s]])


def build_main(layers=(0, 1, 2, 3), stop_after=None, dump=()):
    P = Prog()
    nc, k = P.nc, P.k
    ins = {}

    def din(name, shape, dt=F32):
        if name not in ins:
            ins[name] = P.din(name, shape, dt)
        return ins[name]

    def scr(name, shape, dt=F32):
        if name in dump:
            return P.dout(name, shape, dt)
        return P.dscr(name, shape, dt)

    x_in = din("x", [SEQ, D_MODEL]); ctx_in = din("ctx", [CTX, D_MODEL])
    XL = P.dout("out", [SEQ, D_MODEL])
    XC = scr("XC", [CTX, D_MODEL])
    X = TokT(XC, XL)
    H2 = TokT(scr("H2C", [CTX, D_MODEL]), scr("H2L", [SEQ, D_MODEL]))
    MD = P.dscr("MD", [DEPTH, 2, 6 * D_MODEL])
    md_all = MD.rearrange("l s (m d) -> l s m d", m=6)
    identb_d = din("identb", [128, 128], BF16); identf_d = din("identf", [128, 128])
    QT = scr("QT", [1024, NTOK], BF16); KT = scr("KT", [1024, NTOK], BF16); V = scr("V", [NTOK, 1024], BF16)
    HYP = scr("HYP", [4356, 3072])
    CATT = scr("CATT", [D_MODEL, NTOK], BF16)
    AFFT = scr("AFFT", [16, NTOK])
    pscr = {"idxL": scr("idxL", [NE, CAP_L], U32), "gatL": scr("gatL", [NE, CAP_L]),
            "idxC": scr("idxC", [NE, CAP_C], U32), "gatC": scr("gatC", [NE, CAP_C])}
    QAT = scr("QAT", [1280, NTOK], BF16); KR = scr("KR", [NTOK, 64])
    QT2 = scr("QT2", [3072, NTOK], BF16); KT2 = scr("KT2", [3072, NTOK], BF16); V2 = scr("V2", [NTOK, D_MODEL], BF16)

    with Stage(P):
        tb = [P.sb([128, D_MODEL]) for _ in range(2)]; rtb = [Res(), Res()]
        for t in range(NT):
            s = t % 2
            src = ctx_in[t * 128:(t + 1) * 128, :] if t < 2 else x_in[(t - 2) * 128:(t - 1) * 128, :]
            k.dma("sp", tb[s][:], src, w=[rtb[s]])
            k.dma("sp", X.tile(t), tb[s][:], r=[rtb[s]])
        zt = P.sb([1, 3072]); rz = Res()
        k.op("pool", lambda e: e.memset(zt[:], 0.0), w=[rz])
        for r_ in (0, 257, 258, 4355):
            k.dma("sp", HYP[r_:r_ + 1, :], zt[:], r=[rz])

    emit_p0(P, din("c2T", [128, 16, 2]), din("ada_w", [DEPTH, D_MODEL, 6 * D_MODEL]), din("ada_b", [DEPTH, 6 * D_MODEL]), MD)

    def done(tag):
        return stop_after is not None and stop_after == tag

    for i in layers:
        j = i // 2
        md = md_all[i]
        g1 = din("norm1_g", [DEPTH, D_MODEL])[i:i + 1, :]
        g2 = din("norm2_g", [DEPTH, D_MODEL])[i:i + 1, :]
        if i % 2 == 0:
            emit_e1(P, X, md, g1, din("ev_w_in", [2, D_MODEL, 6144])[j], din("qkg", [2, 2, 512])[j], identb_d, QT, KT, V, HYP)
            if done("e1%d" % i):
                break
            emit_e2(P, QT, KT, V, din("BT", [2, 8, 15, 64, 64])[j], din("MK", [64, 64]), CATT)
            if done("e2%d" % i):
                break
            consts = {"deltas": din("deltas", [1, 1024])}
            for nm, L in (("C", CTX), ("L", SEQ)):
                LT = L // 128
                consts[nm] = {"C2": din("C2" + nm, [LT, 128, LT, 128], BF16), "S2": din("S2" + nm, [LT, 128, LT, 128], BF16),
                              "cs": din("cs" + nm, [128, 3, LT]), "zT": din("zT" + nm, [33, L]), "negt": din("negt" + nm, [128, LT])}
            emit_e3(P, HYP, din("hy_short_w", [2, 3, 3072])[j], din("hy_short_b", [2, 3072])[j:j + 1, :],
                    din("hy_w1", [2, 33, 64])[j], din("hy_b1", [2, 64, 1])[j], din("hy_w2", [2, 64, 64])[j],
                    din("hy_b2", [2, 64, 1])[j], din("hy_freq", [2, 64, 1])[j], din("hy_w3", [2, 64, 4096])[j],
                    din("hy_d", [2, 2, 1024])[j], consts, identb_d, CATT)
            if done("e3%d" % i):
                break
            w_o = din("ev_w_out", [2, D_MODEL, D_MODEL])[j]
        else:
            emit_m1(P, X, md, g1, din("mla_w_down", [2, D_MODEL, 1344])[j], din("mla_qa_g", [2, 768])[j:j + 1, :],
                    din("mla_kva_g", [2, 512])[j:j + 1, :], din("mla_w_uq", [2, 768, 3072])[j], din("mla_w_ukv", [2, 512, 4096])[j],
                    din("mla_q_g", [2, 192])[j:j + 1, :], din("mla_k_g", [2, 192])[j:j + 1, :], din("rope", [NTOK, 3, 32]),
                    identb_d, QAT, KR, QT2, KT2, V2)
            if done("m1%d" % i):
                break
            emit_m2(P, QT2, KT2, V2, CATT)
            if done("m2%d" % i):
                break
            w_o = din("mla_w_o", [2, D_MODEL, D_MODEL])[j]
        emit_p4(P, CATT, X, w_o, md, g2, din("router_w", [DEPTH, D_MODEL, 16])[i], identf_d, H2, AFFT)
        if done("p4%d" % i):
            break
        emit_p5(P, AFFT, H2, X, din("moe_w_gate", [DEPTH, 16, D_MODEL, 1024])[i], din("moe_w_up", [DEPTH, 16, D_MODEL, 1024])[i],
                din("moe_w_down", [DEPTH, 16, 1024, D_MODEL])[i], md, identb_d, pscr)
    print("main ninstr", k.ninstr, "inputs", list(ins.keys()))
    nc = P.finish()
    return nc, list(ins.keys())


def host_inputs(inp, mods=None):
    BT, MK = _na_tables(inp["na_rpb"])
    d = {
        "identb": IDENTB, "identf": IDENTF, "ada_w": inp["ada_w"], "ada_b": inp["ada_b"],
        "norm1_g": inp["norm1_g"], "norm2_g": inp["norm2_g"], "ev_w_in": inp["ev_w_in"], "ev_w_out": inp["ev_w_out"],
        "qkg": np.ascontiguousarray(np.stack([np.tile(inp["na_q_g"], (1, 4)), np.tile(inp["na_k_g"], (1, 4))], 1)),
        "BT": BT, "MK": MK, "deltas": _deltas(),
        "hy_short_w": inp["hy_short_w"], "hy_short_b": inp["hy_short_b"], "hy_w1": inp["hy_w1"],
        "hy_b1": inp["hy_b1"][:, :, None], "hy_w2": inp["hy_w2"], "hy_b2": inp["hy_b2"][:, :, None],
        "hy_freq": inp["hy_freq"][:, :, None], "hy_w3": inp["hy_w3"], "hy_d": inp["hy_d"],
        "mla_w_down": inp["mla_w_down"], "mla_qa_g": inp["mla_qa_g"], "mla_kva_g": inp["mla_kva_g"],
        "mla_w_uq": inp["mla_w_uq"], "mla_w_ukv": inp["mla_w_ukv"], "mla_q_g": inp["mla_q_g"], "mla_k_g": inp["mla_k_g"],
        "mla_w_o": inp["mla_w_o"], "rope": _rope_table(), "router_w": inp["router_w"],
        "moe_w_gate": inp["moe_w_gate"], "moe_w_up": inp["moe_w_up"], "moe_w_down": inp["moe_w_down"],
    }
    for nm, L in (("C", CTX), ("L", SEQ)):
        c = _dft_consts(L)
        for kk, v in c.items():
            d[kk + nm] = v
    return d


def core_inputs(shared, inp, mods, b, names):
    c2 = np.stack([inp["c"][b], inp["c_ctx"]], 0)
    m = {"x": inp["x"][b], "ctx": inp["ctx"][b], "c2T": np.ascontiguousarray(c2.T.reshape(16, 128, 2).transpose(1, 0, 2))}
    out = {}
    for n in names:
        v = m[n] if n in m else shared[n]
        out[n] = np.ascontiguousarray(v)
    return out


def kernel(**inp):
    inp = {k_: np.asarray(v) for k_, v in inp.items()}
    nc, names = build_main()
    shared = host_inputs(inp)
    maps = [core_inputs(shared, inp, None, b, names) for b in range(BATCH)]
    res = run_bass_kernel_spmd(nc, maps, core_ids=list(range(BATCH)))
    return np.stack([np.asarray(res.results[b]["out"]) for b in range(BATCH)], 0).astype(np.float32)
```

```python
import numpy as np
import ml_dtypes
import concourse.bass as bass
import concourse.mybir as mybir
from concourse.bass_utils import run_bass_kernel_spmd
from contextlib import ExitStack

F32 = mybir.dt.float32
BF16 = mybir.dt.bfloat16
I32 = mybir.dt.int32
U32 = mybir.dt.uint32
AF = mybir.ActivationFunctionType
ALU = mybir.AluOpType
AX = mybir.AxisListType
NPBF = ml_dtypes.bfloat16

D_MODEL = 2048
BATCH = 4
SEQ = 4096
DEPTH = 4
CTX = 256
NTOK = CTX + SEQ
NCORES = 8


class Res:
    __slots__ = ("name", "w", "r")

    def __init__(self, name=""):
        self.name = name
        self.w = None
        self.r = {}


class K:
    SEM_ROLL = 30000

    def __init__(self, nc, es, n_dma_sems=40):
        self.nc = nc
        self.es = es
        self.engs = {"pe": nc.tensor, "dve": nc.vector, "act": nc.scalar,
                     "pool": nc.gpsimd, "sp": nc.sync}
        self.sem = {}
        self.cnt = {}
        self.seen = {e: {} for e in self.engs}
        self.nsem = 0
        for e in self.engs:
            self._newsem(e)
        self.dsems = [es.enter_context(nc.semaphore("dq%d" % i)) for i in range(n_dma_sems)]
        self.dval = [0] * n_dma_sems
        self.dnext = 0
        self.ninstr = 0

    def _newsem(self, e):
        self.nsem += 1
        self.sem[e] = self.es.enter_context(self.nc.semaphore("s_%s_%d" % (e, self.nsem)))
        self.cnt[e] = 0

    def ev_wait(self, e, ev):
        sem, val = ev
        if self.seen[e].get(sem, 0) >= val:
            return
        self.engs[e].wait_ge(sem, val)
        self.seen[e][sem] = val

    def _deps(self, e, r, w, skip_same=False):
        deps = []
        for b in r:
            if b.w is not None:
                deps.append(b.w)
        for b in w:
            if b.w is not None:
                deps.append(b.w)
            for s, v in b.r.items():
                deps.append((s, v))
        for ev in deps:
            if skip_same and ev[0] is self.sem[e]:
                continue
            self.ev_wait(e, ev)

    def _mark(self, ev, r, w):
        for b in r:
            if b.r.get(ev[0], 0) < ev[1]:
                b.r[ev[0]] = ev[1]
        for b in w:
            b.w = ev
            b.r = {}

    def op(self, e, fn, r=(), w=(), skip_same=False):
        if e == "pe":
            skip_same = True
        self._deps(e, r, w, skip_same)
        if self.cnt[e] >= self.SEM_ROLL:
            self._newsem(e)
        ins = fn(self.engs[e])
        self.cnt[e] += 1
        ins.then_inc(self.sem[e], 1)
        ev = (self.sem[e], self.cnt[e])
        self._mark(ev, r, w)
        self.ninstr += 1
        return ev

    def dma(self, e, out, in_, r=(), w=(), indirect=None, **kw):
        self._deps(e, r, w)
        i = self.dnext
        self.dnext = (self.dnext + 1) % len(self.dsems)
        if self.dval[i] > 0:
            self.ev_wait(e, (self.dsems[i], self.dval[i]))
        if self.dval[i] >= self.SEM_ROLL:
            self.nsem += 1
            self.dsems[i] = self.es.enter_context(self.nc.semaphore("dq_r%d" % self.nsem))
            self.dval[i] = 0
        if indirect is not None:
            ins = self.engs[e].indirect_dma_start(out=out, in_=in_, **indirect)
        else:
            ins = self.engs[e].dma_start(out=out, in_=in_, **kw)
        self.dval[i] += 16
        ins.then_inc(self.dsems[i], 16)
        ev = (self.dsems[i], self.dval[i])
        self._mark(ev, r, w)
        self.ninstr += 1
        return ev

    def barrier(self):
        evs = [(self.sem[e], self.cnt[e]) for e in self.engs if self.cnt[e] > 0]
        evs += [(self.dsems[i], self.dval[i]) for i in range(len(self.dsems)) if self.dval[i] > 0]
        for e in self.engs:
            for ev in evs:
                if ev[0] is self.sem[e]:
                    continue
                self.ev_wait(e, ev)


class Prog:
    def __init__(self):
        self.nc = bass.Bass("TRN2", target_bir_lowering=False)
        self.es = ExitStack()
        self.k = K(self.nc, self.es)
        self.n = 0

    def din(self, name, shape, dt=F32):
        return self.nc.dram_tensor(name, list(shape), dt, kind="ExternalInput")

    def dout(self, name, shape, dt=F32):
        return self.nc.dram_tensor(name, list(shape), dt, kind="ExternalOutput")

    def dscr(self, name, shape, dt=F32):
        return self.nc.dram_tensor(name, list(shape), dt, kind="Internal")

    def sb(self, shape, dt=F32, name=None):
        self.n += 1
        return self.es.enter_context(self.nc.sbuf_tensor(name or ("sb%d" % self.n), list(shape), dt))

    def ps(self, shape, dt=F32, name=None):
        self.n += 1
        return self.es.enter_context(self.nc.psum_tensor(name or ("ps%d" % self.n), list(shape), dt))

    def finish(self):
        self.k.barrier()
        self.es.close()
        return self.nc


_LAUNCHES = []


def launch(nc, in_maps):
    res = run_bass_kernel_spmd(nc, in_maps, core_ids=list(range(NCORES)))
    return res.results


P0_COLS = 6 * D_MODEL // NCORES


def build_p0():
    P = Prog()
    nc, k = P.nc, P.k
    cT = P.din("cT", [128, 16, 5])
    w = P.din("w", [DEPTH, D_MODEL, P0_COLS])
    b = P.din("b", [DEPTH, 5, P0_COLS])
    o = P.dout("o", [DEPTH, 5, P0_COLS])
    sc = P.sb([128, 16, 5]); rsc = Res()
    k.dma("sp", sc[:], cT[:, :, :], w=[rsc])
    k.op("act", lambda e: e.activation(out=sc[:], in_=sc[:], func=AF.Silu), r=[rsc], w=[rsc])
    wt = [P.sb([128, 16, 512]) for _ in range(2)]; rw = [Res(), Res()]
    acc = [P.ps([128, 512]) for _ in range(2)]; racc = [Res(), Res()]
    bt = [P.sb([5, 512]) for _ in range(2)]; rb = [Res(), Res()]
    ot = [P.sb([5, 512]) for _ in range(2)]; ro = [Res(), Res()]
    it = 0
    for l in range(DEPTH):
        for nb in range(P0_COLS // 512):
            s = it % 2
            it += 1
            cs = slice(nb * 512, nb * 512 + 512)
            k.dma("sp", wt[s][:], w[l, :, cs].rearrange("(kc p) n -> p kc n", p=128), w=[rw[s]])
            k.dma("pool", bt[s][:], b[l, :, cs], w=[rb[s]])
            for kc in range(16):
                k.op("pe", lambda e: e.matmul(acc[s][0:5, :], lhsT=sc[:, kc, :], rhs=wt[s][:, kc, :],
                                               start=(kc == 0), stop=(kc == 15)),
                     r=[rsc, rw[s]], w=[racc[s]], skip_same=(kc > 0))
            k.op("dve", lambda e: e.tensor_tensor(out=ot[s][:], in0=acc[s][0:5, :], in1=bt[s][:], op=ALU.add),
                 r=[racc[s], rb[s]], w=[ro[s]])
            k.dma("sp", o[l, :, cs], ot[s][:], r=[ro[s]])
    return P.finish()


def run_p0(inp):
    c5 = np.concatenate([inp["c"], inp["c_ctx"][None]], 0)
    cT = np.ascontiguousarray(c5.T.reshape(16, 128, 5).transpose(1, 0, 2))
    nc = build_p0()
    maps = []
    for c in range(NCORES):
        cs = slice(c * P0_COLS, (c + 1) * P0_COLS)
        maps.append({"cT": cT,
                     "w": np.ascontiguousarray(inp["ada_w"][:, :, cs]),
                     "b": np.ascontiguousarray(np.broadcast_to(inp["ada_b"][:, None, cs], (DEPTH, 5, P0_COLS)))})
    res = launch(nc, maps)
    mods = np.concatenate([r["o"] for r in res], axis=2)
    return mods.reshape(DEPTH, 5, 6, D_MODEL)


NT = NTOK // 128
EPS = 1e-6
GRID = 64


def bcast_rows(ap_row, n=128):
    return ap_row.to_broadcast([n, ap_row.shape[-1]])


class TokT:
    def __init__(self, c, l):
        self.c, self.l = c, l

    def tile(self, t):
        if t < 2:
            return self.c[t * 128:(t + 1) * 128, :]
        return self.l[(t - 2) * 128:(t - 1) * 128, :]

    def part(self, name):
        return self.c if name == "C" else self.l


class Stage:
    def __init__(self, P):
        self.P = P

    def __enter__(self):
        self.saved = self.P.es
        self.P.es = ExitStack()
        return self.P

    def __exit__(self, *a):
        self.P.k.barrier()
        self.P.es.close()
        self.P.es = self.saved
        return False


def load_mod_AB(P, md, g, ia, ib):
    k = P.k
    gt = P.sb([128, D_MODEL]); rg = Res()
    k.dma("sp", gt[:], bcast_rows(g), w=[rg])
    A, B = [], []
    rAB = Res()
    for s in range(2):
        a = P.sb([128, D_MODEL]); b = P.sb([128, D_MODEL])
        k.dma("sp", a[:], bcast_rows(md[s, ia:ia + 1, :]), w=[rAB])
        k.dma("sp", b[:], bcast_rows(md[s, ib:ib + 1, :]), w=[rAB])
        k.op("dve", lambda e: e.scalar_tensor_tensor(out=a[:], in0=a[:], scalar=1.0, in1=gt[:],
                                                     op0=ALU.add, op1=ALU.mult), r=[rg, rAB], w=[rAB])
        A.append(a); B.append(b)
    return A, B, rAB


def rstd_from_ss(k, rstd, ss, n, r, w, extra=None):
    k.op("dve", lambda e: e.tensor_scalar(out=rstd, in0=ss, scalar1=1.0 / n, scalar2=EPS,
                                          op0=ALU.mult, op1=ALU.add), r=r, w=w)
    k.op("act", lambda e: e.activation(out=rstd, in_=rstd, func=AF.Sqrt), r=w, w=w)
    k.op("dve", lambda e: e.reciprocal(out=rstd, in_=rstd), r=w, w=w)


class NormT:
    def __init__(self, P, X, A, B, rAB, identb, rid):
        self.P, self.X, self.A, self.B, self.rAB, self.identb, self.rid = P, X, A, B, rAB, identb, rid
        self.xt = [P.sb([128, D_MODEL]) for _ in range(2)]; self.rx = [Res(), Res()]
        self.junk = P.sb([128, D_MODEL]); self.rj = Res()
        self.hb = [P.sb([128, D_MODEL], BF16) for _ in range(2)]; self.rh = [Res(), Res()]
        self.ss = P.sb([128, 2]); self.rss = Res()
        self.pT = P.ps([128, D_MODEL], BF16); self.rpT = Res()
        self.n = 0

    def load(self, t):
        s = t % 2
        self.P.k.dma("sp", self.xt[s][:], self.X.tile(t), w=[self.rx[s]])

    def run(self, tiles, hT, rhT):
        k = self.P.k
        self.load(tiles[0])
        for i, t in enumerate(tiles):
            s = t % 2
            if i + 1 < len(tiles):
                self.load(tiles[i + 1])
            xt, junk, ss, hb, pT = self.xt[s], self.junk, self.ss, self.hb[s], self.pT
            rx, rj, rss, rh, rpT = self.rx[s], self.rj, self.rss, self.rh[s], self.rpT
            m = 1 if t < 2 else 0
            k.op("act", lambda e: e.activation(out=junk[:], in_=xt[:], func=AF.Square), r=[rx], w=[rj])
            k.op("dve", lambda e: e.reduce_sum(out=ss[:, 0:1], in_=junk[:], axis=AX.X), r=[rj], w=[rss])
            rstd_from_ss(k, ss[:, 1:2], ss[:, 0:1], D_MODEL, [rss], [rss])
            k.op("dve", lambda e: e.scalar_tensor_tensor(out=junk[:], in0=xt[:], scalar=ss[:, 1:2], in1=self.A[m][:],
                                                         op0=ALU.mult, op1=ALU.mult), r=[rx, rss, self.rAB], w=[rj])
            k.op("dve", lambda e: e.tensor_tensor(out=hb[:], in0=junk[:], in1=self.B[m][:], op=ALU.add),
                 r=[rj, self.rAB], w=[rh])
            for kc in range(16):
                k.op("pe", lambda e: e.transpose(out=pT[:, kc * 128:(kc + 1) * 128], in_=hb[:, kc * 128:(kc + 1) * 128],
                                                 identity=self.identb[:]), r=[rh, self.rid], w=[rpT], skip_same=(kc > 0))
            k.op("act", lambda e: e.copy(out=hT[:, :, i * 128:(i + 1) * 128],
                                         in_=pT[:].rearrange("p (c t) -> p c t", c=16)), r=[rpT], w=[rhT])


def emit_e1(P, X, md, g1, w_in, qkg, identb_d, QT, KT, V, HYP):
    k = P.k
    with Stage(P):
        identb = P.sb([128, 128], BF16); rid = Res()
        k.dma("sp", identb[:], identb_d[:, :], w=[rid])
        gq = P.sb([128, 2, 512]); rgq = Res()
        for i in range(2):
            k.dma("sp", gq[:, i, :], bcast_rows(qkg[i:i + 1, :]), w=[rgq])
        HT = NT // 2
        hT = P.sb([128, 16, HT * 128], BF16); rhT = Res()
        wb = [P.sb([128, 16, 512], BF16) for _ in range(2)]; rw = [Res(), Res()]
        sq = P.sb([128, 512]); rsq = Res()
        s4 = P.sb([128, 8]); rs4 = Res()
        tmp = P.sb([128, 512]); rtmp = Res()
        qn = [P.sb([128, 512], BF16) for _ in range(2)]; rqn = [Res(), Res()]
        qT = [P.sb([128, 512], BF16) for _ in range(2)]; rqT = [Res(), Res()]
        of = [P.sb([128, 512]) for _ in range(2)]; rof = [Res(), Res()]
        acc = [P.ps([128, 512]) for _ in range(2)]; racc = [Res(), Res()]
        pq = [P.ps([128, 512], BF16) for _ in range(2)]; rpq = [Res(), Res()]
        for half in range(2):
            tiles = list(range(half * HT, (half + 1) * HT))
            with Stage(P):
                A, B, rAB = load_mod_AB(P, md, g1, 1, 0)
                NormT(P, X, A, B, rAB, identb, rid).run(tiles, hT, rhT)

            def loadw(nb):
                s = nb % 2
                k.dma("pool", wb[s][:], w_in[:, nb * 512:(nb + 1) * 512].rearrange("(kc p) n -> p kc n", p=128), w=[rw[s]])

            loadw(0)
            it = 0
            for nb in range(12):
                s = nb % 2
                if nb + 1 < 12:
                    loadw(nb + 1)
                for i, t in enumerate(tiles):
                    a = it % 2
                    it += 1
                    rows = slice(t * 128, (t + 1) * 128)
                    for kc in range(16):
                        k.op("pe", lambda e: e.matmul(acc[a][:], lhsT=hT[:, kc, i * 128:(i + 1) * 128], rhs=wb[s][:, kc, :],
                                                      start=(kc == 0), stop=(kc == 15)),
                             r=[rhT, rw[s]], w=[racc[a]], skip_same=(kc > 0))
                    if nb < 4:
                        gi = 0 if nb < 2 else 1
                        dst = QT if nb < 2 else KT
                        hb0 = (nb % 2) * 4
                        k.op("act", lambda e: e.activation(out=sq[:], in_=acc[a][:], func=AF.Square), r=[racc[a]], w=[rsq])
                        k.op("dve", lambda e: e.reduce_sum(out=s4[:, 0:4], in_=sq[:].rearrange("p (h d) -> p h d", h=4),
                                                           axis=AX.X), r=[rsq], w=[rs4])
                        rstd_from_ss(k, s4[:, 4:8], s4[:, 0:4], 128, [rs4], [rs4])
                        k.op("dve", lambda e: e.tensor_tensor(out=tmp[:].rearrange("p (h d) -> p h d", h=4),
                                                              in0=acc[a][:].rearrange("p (h d) -> p h d", h=4),
                                                              in1=s4[:, 4:8].unsqueeze(2).to_broadcast([128, 4, 128]),
                                                              op=ALU.mult), r=[racc[a], rs4], w=[rtmp])
                        k.op("dve", lambda e: e.tensor_tensor(out=qn[a][:], in0=tmp[:], in1=gq[:, gi, :], op=ALU.mult),
                             r=[rtmp, rgq], w=[rqn[a]])
                        for hh in range(4):
                            k.op("pe", lambda e: e.transpose(out=pq[a][:, hh * 128:(hh + 1) * 128],
                                                             in_=qn[a][:, hh * 128:(hh + 1) * 128], identity=identb[:]),
                                 r=[rqn[a], rid], w=[rpq[a]], skip_same=(hh > 0))
                        k.op("act", lambda e: e.copy(out=qT[a][:], in_=pq[a][:]), r=[rpq[a]], w=[rqT[a]])
                        k.dma("sp", dst[hb0 * 128:(hb0 + 4) * 128, rows].rearrange("(h d) t -> d h t", d=128),
                              qT[a][:].rearrange("p (h t) -> p h t", h=4), r=[rqT[a]])
                    elif nb < 6:
                        k.op("act", lambda e: e.copy(out=qn[a][:], in_=acc[a][:]), r=[racc[a]], w=[rqn[a]])
                        k.dma("sp", V[rows, (nb - 4) * 512:(nb - 3) * 512], qn[a][:], r=[rqn[a]])
                    else:
                        k.op("act", lambda e: e.copy(out=of[a][:], in_=acc[a][:]), r=[racc[a]], w=[rof[a]])
                        r0 = (1 + t * 128) if t < 2 else (259 + (t - 2) * 128)
                        k.dma("sp", HYP[r0:r0 + 128, (nb - 6) * 512:(nb - 5) * 512], of[a][:], r=[rof[a]])


def emit_p4(P, CATT, X, w_o, md, g2, rwd, identf_d, H2, AFFT):
    k = P.k
    with Stage(P):
        identf = P.sb([128, 128]); rid = Res()
        k.dma("sp", identf[:], identf_d[:, :], w=[rid])
        rw = P.sb([128, 16, 16]); rrw = Res()
        k.dma("sp", rw[:], rwd.rearrange("(kc p) e -> p kc e", p=128), w=[rrw])
        wo = P.sb([128, 16, D_MODEL], BF16); rwo = Res()
        for q in range(4):
            k.dma("pool", wo[:, :, q * 512:(q + 1) * 512],
                  w_o[:, q * 512:(q + 1) * 512].rearrange("(kc p) n -> p kc n", p=128), w=[rwo])
        A, B, rAB = load_mod_AB(P, md, g2, 4, 3)
        G = []
        rG = Res()
        for s in range(2):
            gt = P.sb([128, D_MODEL])
            k.dma("sp", gt[:], bcast_rows(md[s, 2:3, :]), w=[rG])
            G.append(gt)
        NB = 2
        ct = [P.sb([128, 16, 128], BF16) for _ in range(NB)]; rct = [Res() for _ in range(NB)]
        xm = [P.sb([128, D_MODEL]) for _ in range(NB)]; rxm = [Res() for _ in range(NB)]
        h2 = [P.sb([128, D_MODEL]) for _ in range(NB)]; rh2 = [Res() for _ in range(NB)]
        junk = P.sb([128, D_MODEL]); rj = Res()
        ss = P.sb([128, 2]); rss = Res()
        acc = [P.ps([128, 512]) for _ in range(2)]; racc = [Res(), Res()]
        pT = P.ps([128, D_MODEL]); rpT = Res()
        h2T = P.sb([128, 16, 128]); rh2T = Res()
        lg = P.ps([128, 16]); rlg = Res()
        sm = P.sb([128, 4]); rsm = Res()
        ex = P.sb([128, 16]); rex = Res()
        pA = P.ps([16, 128]); rpA = Res()
        aT = [P.sb([16, 128]) for _ in range(2)]; raT = [Res(), Res()]

        def loads(t):
            s = t % NB
            rows = slice(t * 128, (t + 1) * 128)
            k.dma("sp", ct[s][:], CATT[:, rows].rearrange("(kc p) t -> p kc t", p=128), w=[rct[s]])
            k.dma("sp", xm[s][:], X.tile(t), w=[rxm[s]])

        loads(0)
        itc = [0]

        def phaseA(t):
            s = t % NB
            m = 1 if t < 2 else 0
            rows = slice(t * 128, (t + 1) * 128)
            if t + 1 < NT:
                loads(t + 1)
            for nb in range(4):
                a = itc[0] % 2
                itc[0] += 1
                cs = slice(nb * 512, (nb + 1) * 512)
                for kc in range(16):
                    k.op("pe", lambda e: e.matmul(acc[a][:], lhsT=ct[s][:, kc, :], rhs=wo[:, kc, cs],
                                                  start=(kc == 0), stop=(kc == 15)),
                         r=[rct[s], rwo], w=[racc[a]], skip_same=(kc > 0))
                k.op("dve", lambda e: e.tensor_tensor(out=junk[:, cs], in0=acc[a][:], in1=G[m][:, cs], op=ALU.mult),
                     r=[racc[a], rG], w=[rj])
                k.op("pool", lambda e: e.tensor_tensor(out=xm[s][:, cs], in0=xm[s][:, cs], in1=junk[:, cs], op=ALU.add),
                     r=[rj], w=[rxm[s]])
            k.dma("sp", X.tile(t), xm[s][:], r=[rxm[s]])
            k.op("act", lambda e: e.activation(out=junk[:], in_=xm[s][:], func=AF.Square), r=[rxm[s]], w=[rj])
            k.op("dve", lambda e: e.reduce_sum(out=ss[:, 0:1], in_=junk[:], axis=AX.X), r=[rj], w=[rss])
            rstd_from_ss(k, ss[:, 1:2], ss[:, 0:1], D_MODEL, [rss], [rss])
            k.op("dve", lambda e: e.scalar_tensor_tensor(out=junk[:], in0=xm[s][:], scalar=ss[:, 1:2], in1=A[m][:],
                                                         op0=ALU.mult, op1=ALU.mult), r=[rxm[s], rss, rAB], w=[rj])
            k.op("dve", lambda e: e.tensor_tensor(out=h2[s][:], in0=junk[:], in1=B[m][:], op=ALU.add),
                 r=[rj, rAB], w=[rh2[s]])
            k.dma("sp", H2.tile(t), h2[s][:], r=[rh2[s]])

        def phaseB(t):
            s = t % NB
            rows = slice(t * 128, (t + 1) * 128)
            for kc in range(16):
                k.op("pe", lambda e: e.transpose(out=pT[:, kc * 128:(kc + 1) * 128], in_=h2[s][:, kc * 128:(kc + 1) * 128],
                                                 identity=identf[:]), r=[rh2[s], rid], w=[rpT], skip_same=(kc > 0))
            k.op("act", lambda e: e.copy(out=h2T[:], in_=pT[:].rearrange("p (c t) -> p c t", c=16)), r=[rpT], w=[rh2T])
            for kc in range(16):
                k.op("pe", lambda e: e.matmul(lg[:], lhsT=h2T[:, kc, :], rhs=rw[:, kc, :], start=(kc == 0), stop=(kc == 15)),
                     r=[rh2T, rrw], w=[rlg], skip_same=(kc > 0))
            k.op("dve", lambda e: e.reduce_max(out=sm[:, 0:1], in_=lg[:], axis=AX.X), r=[rlg], w=[rsm])
            k.op("dve", lambda e: e.tensor_scalar(out=sm[:, 1:2], in0=sm[:, 0:1], scalar1=-1.0, scalar2=None, op0=ALU.mult),
                 r=[rsm], w=[rsm])
            k.op("act", lambda e: e.activation(out=ex[:], in_=lg[:], func=AF.Exp, bias=sm[:, 1:2], scale=1.0),
                 r=[rlg, rsm], w=[rex])
            k.op("dve", lambda e: e.reduce_sum(out=sm[:, 2:3], in_=ex[:], axis=AX.X), r=[rex], w=[rsm])
            k.op("dve", lambda e: e.reciprocal(out=sm[:, 3:4], in_=sm[:, 2:3]), r=[rsm], w=[rsm])
            k.op("dve", lambda e: e.tensor_scalar(out=ex[:], in0=ex[:], scalar1=sm[:, 3:4], scalar2=None, op0=ALU.mult),
                 r=[rsm, rex], w=[rex])
            k.op("pe", lambda e: e.transpose(out=pA[:], in_=ex[:], identity=identf[:]), r=[rex, rid], w=[rpA])
            k.op("act", lambda e: e.copy(out=aT[s][:], in_=pA[:]), r=[rpA], w=[raT[s]])
            k.dma("sp", AFFT[:, rows], aT[s][:], r=[raT[s]])

        phaseA(0)
        for t in range(NT):
            if t + 1 < NT:
                phaseA(t + 1)
            phaseB(t)


NE = 16
CAP_L, CAP_C = 512, 32
CUT = 99


def emit_p5(P, AFFT, H2, X, wg_d, wu_d, wd_d, md, identb_d, scr):
    k = P.k
    idx_s = {"L": scr["idxL"], "C": scr["idxC"]}
    gat_s = {"L": scr["gatL"], "C": scr["gatC"]}
    with Stage(P):
        identb = P.sb([128, 128], BF16); rid = Res()
        k.dma("sp", identb[:], identb_d[:, :], w=[rid])
        M5 = []
        rM5 = Res()
        for s in range(2):
            mt = P.sb([128, D_MODEL])
            k.dma("sp", mt[:], bcast_rows(md[s, 5:6, :]), w=[rM5])
            M5.append(mt)
        idxT = {}; gatT = {}
        rIG = Res()
        for name, cap in (("L", CAP_L), ("C", CAP_C)):
            pp = min(128, cap)
            idxT[name] = P.sb([pp, NE, cap // pp], I32); gatT[name] = P.sb([pp, NE, cap // pp])
        with Stage(P):
            for name, c0, N, cap in (("L", CTX, SEQ, CAP_L), ("C", 0, CTX, CAP_C)):
                work = P.sb([NE, N]); rwk = Res()
                k.dma("sp", work[:], AFFT[:, c0:c0 + N], w=[rwk])
                mx = P.sb([NE, cap]); rmx = Res()
                ix = P.sb([NE, cap], U32); rix = Res()
                for itn in range(cap // 8):
                    sl = slice(itn * 8, itn * 8 + 8)
                    k.op("dve", lambda e: e.max(out=mx[:, sl], in_=work[:]), r=[rwk], w=[rmx])
                    k.op("dve", lambda e: e.max_index(out=ix[:, sl], in_max=mx[:, sl], in_values=work[:]), r=[rwk, rmx], w=[rix])
                    k.op("dve", lambda e: e.match_replace(out=work[:], in_to_replace=mx[:, sl], in_values=work[:], imm_value=0.0),
                         r=[rmx], w=[rwk])
                rs = Res()
                k.dma("sp", idx_s[name][:, :], ix[:], r=[rix], w=[rs])
                k.dma("sp", gat_s[name][:, :], mx[:], r=[rmx], w=[rs])
                pp = min(128, cap)
                nj = cap // pp
                for e_ in range(NE):
                    for j in range(nj):
                        k.dma("sp", idxT[name][:, e_, j:j + 1],
                              idx_s[name].bitcast(I32)[e_:e_ + 1, j * pp:(j + 1) * pp].rearrange("o p -> p o"), r=[rs], w=[rIG])
                        k.dma("sp", gatT[name][:, e_, j:j + 1],
                              gat_s[name][e_:e_ + 1, j * pp:(j + 1) * pp].rearrange("o p -> p o"), r=[rs], w=[rIG])

        wslot = [P.sb([128, 16, 512], BF16) for _ in range(4)]; rws = [Res() for _ in range(4)]
        dslot = [P.sb([128, 4, D_MODEL], BF16) for _ in range(2)]; rds = [Res() for _ in range(2)]
        xs = [P.sb([128, D_MODEL]) for _ in range(2)]; rxs = [Res(), Res()]
        xb = [P.sb([128, D_MODEL], BF16) for _ in range(2)]; rxb = [Res(), Res()]
        pT = P.ps([128, D_MODEL], BF16); rpT = Res()
        xsT = P.sb([128, 16, 512], BF16); rxsT = Res()
        pa = [P.ps([128, 512]) for _ in range(2)]; rpa = [Res(), Res()]
        pu = [P.ps([128, 512]) for _ in range(2)]; rpu = [Res(), Res()]
        pd = [P.ps([128, 512]) for _ in range(2)]; rpd = [Res(), Res()]
        sa = [P.sb([128, 512]) for _ in range(2)]; rsa = [Res(), Res()]
        hT = P.sb([128, 8, 512], BF16); rhT = Res()
        yt = [P.sb([128, D_MODEL]) for _ in range(2)]; ryt = [Res(), Res()]
        rX = Res()

        def loadw_gu(e_):
            for hf in range(2):
                k.dma("pool", wslot[2 * hf][:], wg_d[e_, :, hf * 512:(hf + 1) * 512].rearrange("(kc p) n -> p kc n", p=128),
                      w=[rws[2 * hf]])
                k.dma("pool", wslot[2 * hf + 1][:], wu_d[e_, :, hf * 512:(hf + 1) * 512].rearrange("(kc p) n -> p kc n", p=128),
                      w=[rws[2 * hf + 1]])

        def loadw_d(e_):
            for hf in range(2):
                k.dma("pool", dslot[hf][:], wd_d[e_, hf * 512:(hf + 1) * 512, :].rearrange("(fc p) n -> p fc n", p=128),
                      w=[rds[hf]])

        xsT_C = P.sb([128, 16, CAP_C], BF16); rxsT_C = Res()
        hT_C = P.sb([128, 8, CAP_C], BF16); rhT_C = Res()
        XS = {"L": (xsT, rxsT), "C": (xsT_C, rxsT_C)}
        HT = {"L": (hT, rhT), "C": (hT_C, rhT_C)}
        cnt = {"x": 0, "a": 0, "d": 0, "y": 0}

        def gather_T(e_, name):
            cap = CAP_L if name == "L" else CAP_C
            pp = min(128, cap)
            nj = cap // pp
            it_ = idxT[name]
            xT, rxT = XS[name]
            for j in range(nj):
                s = cnt["x"] % 2; cnt["x"] += 1
                k.dma("pool", xs[s][0:pp, :], H2.part(name)[:, :], r=[rIG], w=[rxs[s]],
                      indirect=dict(out_offset=None, in_offset=bass.IndirectOffsetOnAxis(ap=it_[:, e_, j:j + 1], axis=0)))
                k.op("act", lambda e: e.copy(out=xb[s][0:pp, :], in_=xs[s][0:pp, :]), r=[rxs[s]], w=[rxb[s]])
                for kc in range(16):
                    k.op("pe", lambda e: e.transpose(out=pT[:, kc * 128:kc * 128 + pp], in_=xb[s][0:pp, kc * 128:(kc + 1) * 128],
                                                     identity=identb[0:pp, 0:pp]), r=[rxb[s], rid], w=[rpT])
                k.op("dve", lambda e: e.tensor_copy(out=xT[:, :, j * pp:(j + 1) * pp],
                                                    in_=pT[:].rearrange("p (c t) -> p c t", c=16)[:, :, 0:pp]), r=[rpT], w=[rxT])

        def gate_up(e_):
            for hf in range(2):
                for fc in range(4):
                    fs = slice(fc * 128, (fc + 1) * 128)
                    for name in ("L", "C"):
                        cap = CAP_L if name == "L" else CAP_C
                        xT, rxT = XS[name]
                        hT_, rhT_ = HT[name]
                        a = cnt["a"] % 2; cnt["a"] += 1
                        for kc in range(16):
                            k.op("pe", lambda e: e.matmul(pa[a][:, 0:cap], lhsT=wslot[2 * hf][:, kc, fs], rhs=xT[:, kc, 0:cap],
                                                          start=(kc == 0), stop=(kc == 15)), r=[rws[2 * hf], rxT], w=[rpa[a]])
                        for kc in range(16):
                            k.op("pe", lambda e: e.matmul(pu[a][:, 0:cap], lhsT=wslot[2 * hf + 1][:, kc, fs], rhs=xT[:, kc, 0:cap],
                                                          start=(kc == 0), stop=(kc == 15)), r=[rws[2 * hf + 1], rxT], w=[rpu[a]])
                        k.op("act", lambda e: e.activation(out=sa[a][:, 0:cap], in_=pa[a][:, 0:cap], func=AF.Silu),
                             r=[rpa[a]], w=[rsa[a]])
                        k.op("dve", lambda e: e.tensor_tensor(out=hT_[:, hf * 4 + fc, 0:cap], in0=sa[a][:, 0:cap], in1=pu[a][:, 0:cap],
                                                              op=ALU.mult), r=[rsa[a], rpu[a]], w=[rhT_])

        def down_scatter(e_, name, m):
            cap = CAP_L if name == "L" else CAP_C
            pp = min(128, cap)
            nj = cap // pp
            it_, gt_ = idxT[name], gatT[name]
            hT_, rhT_ = HT[name]
            for j in range(nj):
                y = cnt["y"] % 2; cnt["y"] += 1
                for nb in range(4):
                    d = cnt["d"] % 2; cnt["d"] += 1
                    cs = slice(nb * 512, (nb + 1) * 512)
                    for fc in range(8):
                        k.op("pe", lambda e: e.matmul(pd[d][0:pp, :], lhsT=hT_[:, fc, j * pp:(j + 1) * pp],
                                                      rhs=dslot[fc // 4][:, fc % 4, cs], start=(fc == 0), stop=(fc == 7)),
                             r=[rhT_, rds[fc // 4]], w=[rpd[d]])
                    k.op("dve", lambda e: e.scalar_tensor_tensor(out=yt[y][0:pp, cs], in0=pd[d][0:pp, :],
                                                                 scalar=gt_[:, e_, j:j + 1], in1=M5[m][0:pp, cs],
                                                                 op0=ALU.mult, op1=ALU.mult),
                         r=[rpd[d], rIG, rM5], w=[ryt[y]])
                k.dma("pool", X.part(name)[:, :], yt[y][0:pp, :], r=[ryt[y], rIG], w=[rX],
                      indirect=dict(out_offset=bass.IndirectOffsetOnAxis(ap=it_[:, e_, j:j + 1], axis=0), in_offset=None,
                                    compute_op=ALU.add))

        loadw_gu(0)
        loadw_d(0)
        for e_ in range(NE):
            gather_T(e_, "L")
            gather_T(e_, "C")
            gate_up(e_)
            if e_ + 1 < NE:
                loadw_gu(e_ + 1)
            down_scatter(e_, "L", 0)
            down_scatter(e_, "C", 1)
            if e_ + 1 < NE:
                loadw_d(e_ + 1)


def emit_e2(P, QT, KT, V, BT, MK, CATT):
    k = P.k
    scale = 128 ** -0.5
    with Stage(P):
        ones = P.sb([128, 128], BF16); rones = Res()
        k.op("pool", lambda e: e.memset(ones[:], 1.0), w=[rones])
        mk = P.sb([128, 64]); rmk = Res()
        k.dma("sp", mk[0:64, :], MK[:, :], w=[rmk])
        k.dma("sp", mk[64:128, :], MK[:, :], w=[rmk])
        NBUF = 2
        KTh = [P.sb([128, NTOK], BF16) for _ in range(NBUF)]
        QTh = [P.sb([128, NTOK], BF16) for _ in range(NBUF)]
        Ve = [P.sb([128, 32, 128], BF16) for _ in range(NBUF)]
        Vo = [P.sb([128, 31, 128], BF16) for _ in range(NBUF)]
        Vc = [P.sb([128, 2, 128], BF16) for _ in range(NBUF)]
        TB = [P.sb([128, 14, 64]) for _ in range(NBUF)]
        osb = [P.sb([128, NTOK], BF16) for _ in range(NBUF)]
        rin = [Res() for _ in range(NBUF)]; rTB = [Res() for _ in range(NBUF)]; rosb = [Res() for _ in range(NBUF)]
        S = [P.ps([128, 512]) for _ in range(2)]; rS = [Res(), Res()]
        po = [P.ps([128, 512]) for _ in range(2)]; rpo = [Res(), Res()]
        tmp = [P.sb([128, 256]) for _ in range(2)]; rtmp = [Res(), Res()]
        Pm = [P.sb([128, 512], BF16) for _ in range(2)]; rPm = [Res(), Res()]
        rec = [P.sb([128, 256]) for _ in range(2)]; rrec = [Res(), Res()]

        def loadh(h):
            b = h % NBUF
            hc = slice(h * 128, (h + 1) * 128)
            k.dma("sp", KTh[b][:], KT[hc, :], w=[rin[b]])
            k.dma("sp", QTh[b][:], QT[hc, :], w=[rin[b]])
            k.dma("sp", Ve[b][:], V[CTX:NTOK, hc].rearrange("(j p) d -> p j d", p=128), w=[rin[b]])
            k.dma("sp", Vo[b][:], V[CTX + 64:CTX + 64 + 31 * 128, hc].rearrange("(j p) d -> p j d", p=128), w=[rin[b]])
            k.dma("sp", Vc[b][:], V[0:CTX, hc].rearrange("(j p) d -> p j d", p=128), w=[rin[b]])
            k.dma("sp", TB[b][0:64, :, :], BT[h, 0:14, :, :].rearrange("r k q -> k r q"), w=[rTB[b]])
            k.dma("sp", TB[b][64:128, :, :], BT[h, 1:15, :, :].rearrange("r k q -> k r q"), w=[rTB[b]])
            k.op("pool", lambda e: e.tensor_tensor(out=TB[b][:], in0=TB[b][:],
                                                   in1=mk[:].unsqueeze(1).to_broadcast([128, 14, 64]), op=ALU.add),
                 r=[rmk], w=[rTB[b]])

        cnt = [0]

        def attend1(b, q0, n, ktiles, bias):
            a = cnt[0] % 2; cnt[0] += 1
            nk = len(ktiles)
            Sv = S[a][:, 0:nk * n].rearrange("p (i q) -> p i q", i=nk)
            Pv = Pm[a][:, 0:nk * n].rearrange("p (i q) -> p i q", i=nk)
            for i, (kap, vap) in enumerate(ktiles):
                k.op("pe", lambda e: e.matmul(Sv[:, i, :], lhsT=kap, rhs=QTh[b][:, q0:q0 + n], start=True, stop=True),
                     r=[rin[b]], w=[rS[a]])
            nb_ = 0
            if bias is not None:
                nb_, bap = bias
                tv = tmp[a][:, 0:nb_ * n].rearrange("p (i q) -> p i q", i=nb_)
                k.op("dve", lambda e: e.scalar_tensor_tensor(out=tv, in0=Sv[:, 0:nb_, :], scalar=scale, in1=bap,
                                                             op0=ALU.mult, op1=ALU.add), r=[rS[a], rTB[b]], w=[rtmp[a]])
                k.op("act", lambda e: e.activation(out=Pv[:, 0:nb_, :], in_=tv, func=AF.Exp), r=[rtmp[a]], w=[rPm[a]])
            k.op("act", lambda e: e.activation(out=Pv[:, nb_:nk, :], in_=Sv[:, nb_:nk, :], func=AF.Exp, scale=scale),
                 r=[rS[a]], w=[rPm[a]])
            return (a, b, q0, n, ktiles)

        def attend2(st):
            a, b, q0, n, ktiles = st
            nk = len(ktiles)
            Pv = Pm[a][:, 0:nk * n].rearrange("p (i q) -> p i q", i=nk)
            pov = po[a][:, 0:2 * n].rearrange("p (i q) -> p i q", i=2)
            for i, (kap, vap) in enumerate(ktiles):
                k.op("pe", lambda e: e.matmul(pov[:, 0, :], lhsT=vap, rhs=Pv[:, i, :], start=(i == 0), stop=(i == nk - 1)),
                     r=[rin[b], rPm[a]], w=[rpo[a]])
            for i in range(nk):
                k.op("pe", lambda e: e.matmul(pov[:, 1, :], lhsT=ones[:], rhs=Pv[:, i, :], start=(i == 0), stop=(i == nk - 1)),
                     r=[rones, rPm[a]], w=[rpo[a]])
            k.op("dve", lambda e: e.reciprocal(out=rec[a][:, 0:n], in_=pov[:, 1, :]), r=[rpo[a]], w=[rrec[a]])
            k.op("dve", lambda e: e.tensor_tensor(out=osb[b][:, q0:q0 + n], in0=pov[:, 0, :], in1=rec[a][:, 0:n], op=ALU.mult),
                 r=[rpo[a], rrec[a]], w=[rosb[b]])

        loadh(0)
        for h in range(8):
            b = h % NBUF
            if h + 1 < 8:
                loadh(h + 1)
            ctxk = [(KTh[b][:, i * 128:(i + 1) * 128], Vc[b][:, i, :]) for i in range(2)]
            TB7 = TB[b][:].rearrange("p (a c) q -> p a c q", c=2)
            jobs = [(0, 256, ctxk, None)]
            for r in range(GRID):
                rs = min(max(r - 4, 0), GRID - 8)
                kt = []
                for i in range(4):
                    row = rs + 2 * i
                    tok = CTX + row * 64
                    vap = Ve[b][:, row // 2, :] if row % 2 == 0 else Vo[b][:, (row - 1) // 2, :]
                    kt.append((KTh[b][:, tok:tok + 128], vap))
                dr0 = rs - r + 7
                bap = TB7[:, dr0 // 2:dr0 // 2 + 4, dr0 % 2, :]
                jobs.append((CTX + r * 64, 64, kt + ctxk, (4, bap)))
            st = attend1(b, *jobs[0])
            for ji in range(len(jobs)):
                nxt = attend1(b, *jobs[ji + 1]) if ji + 1 < len(jobs) else None
                attend2(st)
                st = nxt
            k.dma("sp", CATT[h * 128:(h + 1) * 128, :], osb[b][:], r=[rosb[b]])


def emit_e3(P, HYP, sw, sbias, w1, b1, w2, b2, freq, w3, hyd, consts, identb_d, CATT):
    k = P.k
    with Stage(P):
        identb = P.sb([128, 128], BF16); rid = Res()
        k.dma("sp", identb[:], identb_d[:, :], w=[rid])
        ones = P.sb([128, 128]); rones = Res()
        k.op("pool", lambda e: e.memset(ones[:], 1.0), w=[rones])
        w1s = P.sb([33, 64]); w2s = P.sb([64, 64]); pr = P.sb([64, 8]); rpar = Res()
        k.dma("sp", w1s[:], w1[:, :], w=[rpar])
        k.dma("sp", w2s[:], w2[:, :], w=[rpar])
        k.dma("sp", pr[:, 0:1], b1[:, :], w=[rpar])
        k.dma("sp", pr[:, 1:2], b2[:, :], w=[rpar])
        k.dma("sp", pr[:, 2:3], freq[:, :], w=[rpar])
        k.op("dve", lambda e: e.tensor_scalar(out=pr[:, 3:4], in0=pr[:, 2:3], scalar1=1.0 / 3.0, scalar2=None, op0=ALU.mult),
             r=[rpar], w=[rpar])
        k.op("dve", lambda e: e.tensor_tensor(out=pr[:, 4:5], in0=pr[:, 3:4], in1=pr[:, 0:1], op=ALU.mult), r=[rpar], w=[rpar])
        k.op("dve", lambda e: e.tensor_tensor(out=pr[:, 5:6], in0=pr[:, 3:4], in1=pr[:, 1:2], op=ALU.mult), r=[rpar], w=[rpar])

        for (L, beta, tok0, cn) in ((CTX, 1, 0, consts["C"]), (SEQ, 259, CTX, consts["L"])):
            LT = L // 128
            bw = min(512, L)
            with Stage(P):
                hid2T = P.sb([64, L]); rh2 = Res()
                with Stage(P):
                    zT = P.sb([33, L]); rz = Res()
                    k.dma("sp", zT[:], cn["zT"][:, :], w=[rz])
                    hid1T = P.sb([64, L]); rh1 = Res()
                    pm = [P.ps([64, 512]) for _ in range(2)]; rpm = [Res(), Res()]
                    sn = P.sb([64, 512]); rsn = Res()
                    s2 = P.sb([64, 512]); rs2 = Res()
                    it = 0
                    for (lhs, src, rsrc, bcol, dst, rdst) in ((w1s, zT, rz, 4, hid1T, rh1), (w2s, hid1T, rh1, 5, hid2T, rh2)):
                        for nb in range(L // bw):
                            a = it % 2; it += 1
                            cs = slice(nb * bw, (nb + 1) * bw)
                            k.op("pe", lambda e: e.matmul(pm[a][:, 0:bw], lhsT=lhs[:], rhs=src[:, cs], start=True, stop=True),
                                 r=[rpar, rsrc], w=[rpm[a]])
                            k.op("act", lambda e: e.activation(out=sn[:, 0:bw], in_=pm[a][:, 0:bw], func=AF.Sin,
                                                               bias=pr[:, bcol:bcol + 1], scale=pr[:, 3:4]),
                                 r=[rpm[a], rpar], w=[rsn])
                            k.op("dve", lambda e: e.tensor_tensor(out=s2[:, 0:bw], in0=sn[:, 0:bw], in1=sn[:, 0:bw], op=ALU.mult),
                                 r=[rsn], w=[rs2])
                            k.op("dve", lambda e: e.tensor_scalar(out=s2[:, 0:bw], in0=s2[:, 0:bw], scalar1=-4.0, scalar2=3.0,
                                                                  op0=ALU.mult, op1=ALU.add), r=[rs2], w=[rs2])
                            k.op("dve", lambda e: e.tensor_tensor(out=dst[:, cs], in0=s2[:, 0:bw], in1=sn[:, 0:bw], op=ALU.mult),
                                 r=[rs2, rsn], w=[rdst])
                negt = P.sb([128, LT]); cst = P.sb([128, 3, LT]); rcn = Res()
                k.dma("sp", negt[:], cn["negt"][:, :], w=[rcn])
                k.dma("sp", cst[:], cn["cs"][:, :, :], w=[rcn])
                G2 = P.sb([128, LT, 512], BF16); rG = Res()
                U = P.sb([128, LT, 256], BF16); rU = Res()
                YR = P.sb([128, LT, 256], BF16); YI = P.sb([128, LT, 256], BF16); rY = Res()
                outT = P.sb([128, 2, L], BF16); routT = Res()
                Wc = P.sb([128, 3, 3, 256]); Bc = P.sb([128, 3, 256]); rWc = Res()
                dB = P.sb([128, 2, 256]); deltab = P.sb([128, 256]); rdB = Res()
                w3g = P.sb([64, 2, 256]); rw3 = Res()
                cv = [P.sb([128, 3, 256]) for _ in range(2)]; rcv = [Res(), Res()]
                c1 = P.sb([128, 3, 256]); rc1 = Res()
                xp = P.sb([128, 256]); rxp = Res()
                ccnt = [0]

                def conv_tile(part, g, nt):
                    s = ccnt[0] % 2; ccnt[0] += 1
                    col = part * 1024 + g * 256
                    r0 = beta + nt * 128
                    for j in range(3):
                        k.dma("sp", cv[s][:, j, :], HYP[r0 - 1 + j:r0 - 1 + j + 128, col:col + 256], w=[rcv[s]])
                    k.op("pool", lambda e: e.tensor_tensor(out=c1[:], in0=cv[s][:], in1=Wc[:, :, part, :], op=ALU.mult),
                         r=[rcv[s], rWc], w=[rc1])
                    k.op("dve", lambda e: e.tensor_tensor(out=c1[:, 0, :], in0=c1[:, 0, :], in1=c1[:, 1, :], op=ALU.add),
                         r=[rc1], w=[rc1])
                    k.op("dve", lambda e: e.tensor_tensor(out=c1[:, 2, :], in0=c1[:, 2, :], in1=Bc[:, part, :], op=ALU.add),
                         r=[rc1, rWc], w=[rc1])
                    k.op("dve", lambda e: e.tensor_tensor(out=xp[:], in0=c1[:, 0, :], in1=c1[:, 2, :], op=ALU.add),
                         r=[rc1], w=[rxp])

                for g in range(4):
                    gc = slice(g * 256, (g + 1) * 256)
                    for j in range(3):
                        k.dma("sp", Wc[:, j, :, :], sw[j:j + 1, :].rearrange("o (p c) -> o p c", p=3)[:, :, gc].to_broadcast([128, 3, 256]),
                              w=[rWc])
                    k.dma("sp", Bc[:], sbias[0:1, :].rearrange("o (p c) -> o p c", p=3)[:, :, gc].to_broadcast([128, 3, 256]), w=[rWc])
                    k.dma("sp", dB[:], hyd[:, gc].unsqueeze(0).to_broadcast([128, 2, 256]), w=[rdB])
                    k.dma("sp", deltab[:], bcast_rows(consts["deltas"][0:1, gc]), w=[rdB])
                    for o in range(2):
                        k.dma("sp", w3g[:], w3[:, 2 * o * 1024:(2 * o + 2) * 1024].rearrange("k (d c) -> k d c", d=2)[:, :, gc], w=[rw3])
                        with Stage(P):
                            pf = [P.ps([128, 512]) for _ in range(2)]; rpf = [Res(), Res()]
                            pss = P.ps([128, 512]); rpss = Res()
                            dec = [P.sb([128, 256]) for _ in range(2)]; rdec = [Res(), Res()]
                            fsb = [P.sb([128, 2, 256]) for _ in range(2)]; rfsb = [Res(), Res()]
                            sq = [P.sb([128, 512]) for _ in range(2)]; rsq = [Res(), Res()]
                            rstd = P.sb([128, 2, 256]); rrs = Res()
                            for ps_ in range(2):
                                for tt in range(LT):
                                    a = tt % 2
                                    k.op("pe", lambda e: e.matmul(pf[a][:], lhsT=hid2T[:, tt * 128:(tt + 1) * 128],
                                                                  rhs=w3g[:].rearrange("k d c -> k (d c)"), start=True, stop=True),
                                         r=[rh2, rw3], w=[rpf[a]])
                                    k.op("act", lambda e: e.activation(out=dec[a][:], in_=deltab[:], func=AF.Exp,
                                                                       scale=negt[:, tt:tt + 1]), r=[rdB, rcn], w=[rdec[a]])
                                    k.op("dve", lambda e: e.tensor_tensor(out=fsb[a][:], in0=pf[a][:].rearrange("p (d c) -> p d c", d=2),
                                                                          in1=dec[a][:].unsqueeze(1).to_broadcast([128, 2, 256]),
                                                                          op=ALU.mult), r=[rpf[a], rdec[a]], w=[rfsb[a]])
                                    if ps_ == 0:
                                        k.op("pool", lambda e: e.tensor_tensor(out=sq[a][:].rearrange("p (d c) -> p d c", d=2),
                                                                               in0=fsb[a][:], in1=fsb[a][:], op=ALU.mult),
                                             r=[rfsb[a]], w=[rsq[a]])
                                        k.op("pe", lambda e: e.matmul(pss[:], lhsT=ones[:], rhs=sq[a][:], start=(tt == 0),
                                                                      stop=(tt == LT - 1)), r=[rones, rsq[a]], w=[rpss],
                                             skip_same=(tt > 0))
                                    else:
                                        k.op("pool", lambda e: e.tensor_tensor(out=fsb[a][:], in0=fsb[a][:], in1=rstd[:], op=ALU.mult),
                                             r=[rrs], w=[rfsb[a]])
                                        k.op("dve", lambda e: e.tensor_tensor(out=G2[:, tt, 0:256], in0=fsb[a][:, 0, :], in1=fsb[a][:, 1, :],
                                                                              op=ALU.add), r=[rfsb[a]], w=[rG])
                                        k.op("pool", lambda e: e.tensor_tensor(out=G2[:, tt, 256:512], in0=fsb[a][:, 0, :], in1=fsb[a][:, 1, :],
                                                                               op=ALU.subtract), r=[rfsb[a]], w=[rG])
                                if ps_ == 0:
                                    rv = rstd[:].rearrange("p d c -> p (d c)")
                                    k.op("dve", lambda e: e.tensor_scalar(out=rv, in0=pss[:], scalar1=EPS, scalar2=None, op0=ALU.add),
                                         r=[rpss], w=[rrs])
                                    k.op("act", lambda e: e.activation(out=rv, in_=rv, func=AF.Sqrt), r=[rrs], w=[rrs])
                                    k.op("dve", lambda e: e.reciprocal(out=rv, in_=rv), r=[rrs], w=[rrs])
                        if o == 0:
                            for nt in range(LT):
                                conv_tile(0, g, nt)
                                k.op("act", lambda e: e.copy(out=U[:, nt, :], in_=xp[:]), r=[rxp], w=[rU])
                        with Stage(P):
                            Ct = [P.sb([128, LT, 128], BF16) for _ in range(2)]
                            St = [P.sb([128, LT, 128], BF16) for _ in range(2)]
                            rCS = [Res(), Res()]
                            bA = [P.ps([128, 512]) for _ in range(2)]; bB = [P.ps([128, 512]) for _ in range(2)]
                            bC = [P.ps([128, 256]) for _ in range(2)]; bS = [P.ps([128, 256]) for _ in range(2)]
                            raccs = [Res(), Res()]
                            PuS = P.sb([128, 256]); QuS = P.sb([128, 256]); rPQ = Res()
                            t1 = P.sb([128, 256]); t2 = P.sb([128, 256]); rt = Res()
                            Hre = P.sb([128, 256]); Him = P.sb([128, 256]); rH = Res()
                            a1 = P.sb([128, 256]); a2 = P.sb([128, 256]); ra = Res()
                            b1_ = P.sb([128, 256]); b2_ = P.sb([128, 256]); rb = Res()

                            def loadcs(ft):
                                s = ft % 2
                                k.dma("sp", Ct[s][:], cn["C2"][ft, :, :, :], w=[rCS[s]])
                                k.dma("sp", St[s][:], cn["S2"][ft, :, :, :], w=[rCS[s]])

                            loadcs(0)
                            for ft in range(LT):
                                s = ft % 2
                                racc = raccs[s]
                                acc = [bA[s][:, 0:256], bB[s][:, 0:256], bA[s][:, 256:512], bB[s][:, 256:512], bC[s][:], bS[s][:]]
                                if ft + 1 < LT:
                                    loadcs(ft + 1)
                                for tt in range(LT):
                                    st_ = (tt == 0); sp_ = (tt == LT - 1)
                                    k.op("pe", lambda e: e.matmul(bA[s][:], lhsT=Ct[s][:, tt, :], rhs=G2[:, tt, :], start=st_, stop=sp_),
                                         r=[rCS[s], rG], w=[racc])
                                    k.op("pe", lambda e: e.matmul(bC[s][:], lhsT=Ct[s][:, tt, :], rhs=U[:, tt, :], start=st_, stop=sp_),
                                         r=[rCS[s], rU], w=[racc])
                                    k.op("pe", lambda e: e.matmul(bB[s][:], lhsT=St[s][:, tt, :], rhs=G2[:, tt, :], start=st_, stop=sp_),
                                         r=[rCS[s], rG], w=[racc])
                                    k.op("pe", lambda e: e.matmul(bS[s][:], lhsT=St[s][:, tt, :], rhs=U[:, tt, :], start=st_, stop=sp_),
                                         r=[rCS[s], rU], w=[racc])
                                cc = cst[:, 0, ft:ft + 1]; ss_ = cst[:, 1, ft:ft + 1]; nc_ = cst[:, 2, ft:ft + 1]
                                k.op("act", lambda e: e.copy(out=PuS[:], in_=acc[4][:]), r=[racc], w=[rPQ])
                                k.op("act", lambda e: e.copy(out=QuS[:], in_=acc[5][:]), r=[racc], w=[rPQ])
                                k.op("dve", lambda e: e.tensor_scalar(out=t1[:], in0=acc[0][:], scalar1=cc, scalar2=None, op0=ALU.mult),
                                     r=[racc, rcn], w=[rt])
                                k.op("dve", lambda e: e.scalar_tensor_tensor(out=t1[:], in0=acc[1][:], scalar=ss_, in1=t1[:],
                                                                             op0=ALU.mult, op1=ALU.add), r=[racc, rcn], w=[rt])
                                k.op("dve", lambda e: e.tensor_scalar(out=t2[:], in0=acc[2][:], scalar1=ss_, scalar2=None, op0=ALU.mult),
                                     r=[racc, rcn], w=[rt])
                                k.op("dve", lambda e: e.scalar_tensor_tensor(out=Him[:], in0=acc[3][:], scalar=nc_, in1=t2[:],
                                                                             op0=ALU.mult, op1=ALU.add), r=[racc, rcn, rt], w=[rH])
                                k.op("pool", lambda e: e.tensor_tensor(out=Hre[:], in0=t1[:], in1=dB[:, o, :], op=ALU.add),
                                     r=[rt, rdB], w=[rH])
                                k.op("dve", lambda e: e.tensor_tensor(out=a1[:], in0=Hre[:], in1=PuS[:], op=ALU.mult), r=[rH, rPQ], w=[ra])
                                k.op("dve", lambda e: e.tensor_tensor(out=a2[:], in0=Him[:], in1=QuS[:], op=ALU.mult), r=[rH, rPQ], w=[ra])
                                k.op("dve", lambda e: e.tensor_tensor(out=YR[:, ft, :], in0=a1[:], in1=a2[:], op=ALU.add), r=[ra], w=[rY])
                                k.op("pool", lambda e: e.tensor_tensor(out=b1_[:], in0=Hre[:], in1=QuS[:], op=ALU.mult), r=[rH, rPQ], w=[rb])
                                k.op("pool", lambda e: e.tensor_tensor(out=b2_[:], in0=Him[:], in1=PuS[:], op=ALU.mult), r=[rH, rPQ], w=[rb])
                                k.op("pool", lambda e: e.tensor_tensor(out=YI[:, ft, :], in0=b1_[:], in1=b2_[:], op=ALU.subtract),
                                     r=[rb], w=[rY])
                        with Stage(P):
                            Ct = [P.sb([128, LT, 128], BF16) for _ in range(2)]
                            St = [P.sb([128, LT, 128], BF16) for _ in range(2)]
                            rCS = [Res(), Res()]
                            py = [P.ps([128, 256]) for _ in range(2)]; rpy = [Res(), Res()]
                            zb = [P.sb([128, 256], BF16) for _ in range(2)]; rzb = [Res(), Res()]
                            pz = [P.ps([128, 256], BF16) for _ in range(2)]; rpz = [Res(), Res()]

                            def loadcs2(nt):
                                s = nt % 2
                                k.dma("sp", Ct[s][:], cn["C2"][nt, :, :, :], w=[rCS[s]])
                                k.dma("sp", St[s][:], cn["S2"][nt, :, :, :], w=[rCS[s]])

                            loadcs2(0)
                            for nt in range(LT):
                                s = nt % 2
                                if nt + 1 < LT:
                                    loadcs2(nt + 1)
                                for ft in range(LT):
                                    k.op("pe", lambda e: e.matmul(py[s][:], lhsT=Ct[s][:, ft, :], rhs=YR[:, ft, :],
                                                                  start=(ft == 0), stop=False), r=[rCS[s], rY], w=[rpy[s]],
                                         skip_same=(ft > 0))
                                    k.op("pe", lambda e: e.matmul(py[s][:], lhsT=St[s][:, ft, :], rhs=YI[:, ft, :],
                                                                  start=False, stop=(ft == LT - 1)), r=[rCS[s], rY], w=[rpy[s]],
                                         skip_same=True)
                                conv_tile(o + 1, g, nt)
                                if o == 0:
                                    k.op("dve", lambda e: e.scalar_tensor_tensor(out=U[:, nt, :], in0=py[s][:], scalar=1.0 / L, in1=xp[:],
                                                                                 op0=ALU.mult, op1=ALU.mult), r=[rpy[s], rxp], w=[rU])
                                else:
                                    k.op("dve", lambda e: e.scalar_tensor_tensor(out=zb[s][:], in0=py[s][:], scalar=1.0 / L, in1=xp[:],
                                                                                 op0=ALU.mult, op1=ALU.mult), r=[rpy[s], rxp], w=[rzb[s]])
                                    for hh in range(2):
                                        k.op("pe", lambda e: e.transpose(out=pz[s][:, hh * 128:(hh + 1) * 128],
                                                                         in_=zb[s][:, hh * 128:(hh + 1) * 128], identity=identb[:]),
                                             r=[rzb[s], rid], w=[rpz[s]], skip_same=(hh > 0))
                                    k.op("act", lambda e: e.copy(out=outT[:, :, nt * 128:(nt + 1) * 128],
                                                                 in_=pz[s][:].rearrange("p (h t) -> p h t", h=2)), r=[rpz[s]], w=[routT])
                            if o == 1:
                                for hh in range(2):
                                    r0 = 1024 + g * 256 + hh * 128
                                    k.dma("sp", CATT[r0:r0 + 128, tok0:tok0 + L], outT[:, hh, :], r=[routT])


def emit_m1(P, X, md, g1, w_down, qag, kvag, w_uq, w_ukv, qg, kg, rope, identb_d, QAT, KR, QT2, KT2, V2):
    k = P.k
    with Stage(P):
        identb = P.sb([128, 128], BF16); rid = Res()
        k.dma("sp", identb[:], identb_d[:, :], w=[rid])
        wdn = P.sb([128, 16, 1344], BF16); rwd = Res()
        for (c0, c1) in ((0, 512), (512, 1024), (1024, 1344)):
            k.dma("pool", wdn[:, :, c0:c1], w_down[:, c0:c1].rearrange("(kc p) n -> p kc n", p=128), w=[rwd])
        gq = P.sb([128, 1280]); rgq = Res()
        k.dma("sp", gq[:, 0:768], bcast_rows(qag), w=[rgq])
        k.dma("sp", gq[:, 768:1280], bcast_rows(kvag), w=[rgq])
        A, B, rAB = load_mod_AB(P, md, g1, 1, 0)
        nt_ = NormT(P, X, A, B, rAB, identb, rid)
        hT = P.sb([128, 16, 128], BF16); rhT = Res()
        acc = [P.ps([128, 512]) for _ in range(2)]; racc = [Res(), Res()]
        da = P.sb([128, 1344]); rda = Res()
        sq = P.sb([128, 1280]); rsq = Res()
        s2 = P.sb([128, 4]); rs2 = Res()
        nb_ = [P.sb([128, 1280], BF16) for _ in range(2)]; rnb = [Res(), Res()]
        pT = P.ps([128, 1280], BF16); rpT = Res()
        nT = [P.sb([128, 10, 128], BF16) for _ in range(2)]; rnT = [Res(), Res()]
        it = 0
        for t in range(NT):
            s = t % 2
            rows = slice(t * 128, (t + 1) * 128)
            nt_.run([t], hT, rhT)
            for (c0, c1) in ((0, 512), (512, 1024), (1024, 1344)):
                a = it % 2; it += 1
                for kc in range(16):
                    k.op("pe", lambda e: e.matmul(acc[a][:, 0:c1 - c0], lhsT=hT[:, kc, :], rhs=wdn[:, kc, c0:c1],
                                                  start=(kc == 0), stop=(kc == 15)), r=[rhT, rwd], w=[racc[a]], skip_same=(kc > 0))
                k.op("act", lambda e: e.copy(out=da[:, c0:c1], in_=acc[a][:, 0:c1 - c0]), r=[racc[a]], w=[rda])
            k.dma("sp", KR[rows, :], da[:, 1280:1344], r=[rda])
            k.op("act", lambda e: e.activation(out=sq[:], in_=da[:, 0:1280], func=AF.Square), r=[rda], w=[rsq])
            k.op("dve", lambda e: e.reduce_sum(out=s2[:, 0:1], in_=sq[:, 0:768], axis=AX.X), r=[rsq], w=[rs2])
            k.op("dve", lambda e: e.reduce_sum(out=s2[:, 1:2], in_=sq[:, 768:1280], axis=AX.X), r=[rsq], w=[rs2])
            rstd_from_ss(k, s2[:, 2:3], s2[:, 0:1], 768, [rs2], [rs2])
            rstd_from_ss(k, s2[:, 3:4], s2[:, 1:2], 512, [rs2], [rs2])
            k.op("dve", lambda e: e.scalar_tensor_tensor(out=nb_[s][:, 0:768], in0=da[:, 0:768], scalar=s2[:, 2:3], in1=gq[:, 0:768],
                                                         op0=ALU.mult, op1=ALU.mult), r=[rda, rs2, rgq], w=[rnb[s]])
            k.op("dve", lambda e: e.scalar_tensor_tensor(out=nb_[s][:, 768:1280], in0=da[:, 768:1280], scalar=s2[:, 3:4],
                                                         in1=gq[:, 768:1280], op0=ALU.mult, op1=ALU.mult),
                 r=[rda, rs2, rgq], w=[rnb[s]])
            for c in range(10):
                k.op("pe", lambda e: e.transpose(out=pT[:, c * 128:(c + 1) * 128], in_=nb_[s][:, c * 128:(c + 1) * 128],
                                                 identity=identb[:]), r=[rnb[s], rid], w=[rpT], skip_same=(c > 0))
            k.op("act", lambda e: e.copy(out=nT[s][:], in_=pT[:].rearrange("p (c t) -> p c t", c=10)), r=[rpT], w=[rnT[s]])
            k.dma("sp", QAT[:, rows].rearrange("(c p) t -> p c t", p=128), nT[s][:], r=[rnT[s]])
    with Stage(P):
        identb = P.sb([128, 128], BF16); rid = Res()
        k.dma("sp", identb[:], identb_d[:, :], w=[rid])
        wuq = P.sb([128, 6, 3072], BF16); wukv = P.sb([128, 4, 4096], BF16); rwu = Res()
        for q in range(6):
            k.dma("pool", wuq[:, :, q * 512:(q + 1) * 512], w_uq[:, q * 512:(q + 1) * 512].rearrange("(kc p) n -> p kc n", p=128), w=[rwu])
        for q in range(8):
            k.dma("pool", wukv[:, :, q * 512:(q + 1) * 512], w_ukv[:, q * 512:(q + 1) * 512].rearrange("(kc p) n -> p kc n", p=128), w=[rwu])
        gt = P.sb([128, 2, 192]); rgt = Res()
        k.dma("sp", gt[:, 0, :], bcast_rows(qg), w=[rgt])
        k.dma("sp", gt[:, 1, :], bcast_rows(kg), w=[rgt])
        nT = [P.sb([128, 10, 128], BF16) for _ in range(2)]
        kr = [P.sb([128, 64]) for _ in range(2)]
        rp = [P.sb([128, 3, 32]) for _ in range(2)]
        rin = [Res(), Res()]
        acc = [P.ps([128, 512]) for _ in range(2)]; racc = [Res(), Res()]
        fall = [P.sb([128, 16, 192]) for _ in range(2)]; rfall = [Res(), Res()]
        sqa = P.sb([128, 16, 192]); rsqa = Res()
        s16 = P.sb([128, 32]); rs16 = Res()
        tA = P.sb([128, 16, 2, 16]); tB = P.sb([128, 16, 2, 16]); rtAB = Res()
        ball = [P.sb([128, 16, 192], BF16) for _ in range(2)]; rball = [Res(), Res()]
        vball = [P.sb([128, 16, 128], BF16) for _ in range(2)]; rvb = [Res(), Res()]
        pq = [P.ps([128, 8, 2, 128], BF16) for _ in range(2)]; rpq = [Res(), Res()]
        qTs = [P.sb([128, 8, 2, 128], BF16) for _ in range(2)]; rqTs = [Res(), Res()]

        def loadt(t):
            s = t % 2
            rows = slice(t * 128, (t + 1) * 128)
            k.dma("sp", nT[s][:], QAT[:, rows].rearrange("(c p) t -> p c t", p=128), w=[rin[s]])
            k.dma("sp", kr[s][:], KR[rows, :], w=[rin[s]])
            k.dma("sp", rp[s][:], rope[rows, :, :], w=[rin[s]])

        hcnt = [0]
        ones3 = P.sb([128, 16, 64]); rones3 = Res()
        k.op("dve", lambda e: e.memset(ones3[:], 1.0), w=[rones3])

        def norm_rope_T(f, gi, dstT, s, rows):
            F_ = fall[f]; rF = rfall[f]; Bq = ball[f]; rB = rball[f]
            k.op("act", lambda e: e.activation(out=sqa[:], in_=F_[:], func=AF.Square), r=[rF], w=[rsqa])
            k.op("dve", lambda e: e.reduce_sum(out=s16[:, 0:16], in_=sqa[:], axis=AX.X), r=[rsqa], w=[rs16])
            if CUT <= 1:
                return
            rstd_from_ss(k, s16[:, 16:32], s16[:, 0:16], 192, [rs16], [rs16])
            k.op("dve", lambda e: e.tensor_tensor(out=F_[:], in0=F_[:], in1=s16[:, 16:32].unsqueeze(2).to_broadcast([128, 16, 192]),
                                                  op=ALU.mult), r=[rs16], w=[rF])
            k.op("dve", lambda e: e.tensor_tensor(out=F_[:], in0=F_[:], in1=gt[:, gi:gi + 1, :].to_broadcast([128, 16, 192]),
                                                   op=ALU.mult), r=[rgt], w=[rF])
            k.op("act", lambda e: e.copy(out=Bq[:, :, 0:128], in_=F_[:, :, 0:128]), r=[rF], w=[rB])
            if CUT <= 2:
                return
            for g in range(2):
                x = F_[:, :, 128 + g * 32:128 + (g + 1) * 32].rearrange("p h (s d) -> p h s d", s=2)
                o = Bq[:, :, 128 + g * 32:128 + (g + 1) * 32].rearrange("p h (s d) -> p h s d", s=2)
                cg = rp[s][:, 0, g * 16:(g + 1) * 16]
                sg = rp[s][:, 1, g * 16:(g + 1) * 16]
                ng = rp[s][:, 2, g * 16:(g + 1) * 16]
                k.op("dve", lambda e: e.tensor_tensor(out=tA[:], in0=x, in1=cg.unsqueeze(1).unsqueeze(1).to_broadcast([128, 16, 2, 16]),
                                                      op=ALU.mult), r=[rF, rin[s]], w=[rtAB])
                k.op("dve", lambda e: e.tensor_tensor(out=tB[:, :, 0, :], in0=x[:, :, 1, :],
                                                       in1=ng.unsqueeze(1).to_broadcast([128, 16, 16]), op=ALU.mult), r=[rF, rin[s]], w=[rtAB])
                k.op("dve", lambda e: e.tensor_tensor(out=tB[:, :, 1, :], in0=x[:, :, 0, :],
                                                       in1=sg.unsqueeze(1).to_broadcast([128, 16, 16]), op=ALU.mult), r=[rF, rin[s]], w=[rtAB])
                k.op("dve", lambda e: e.tensor_tensor(out=o, in0=tA[:], in1=tB[:], op=ALU.add), r=[rtAB], w=[rB])
            if CUT <= 3:
                return
            dv = dstT.rearrange("(h r) t -> r h t", r=192)
            for half in range(2):
                a = hcnt[0] % 2; hcnt[0] += 1
                for hh in range(8):
                    h = half * 8 + hh
                    k.op("pe", lambda e: e.transpose(out=pq[a][:, hh, 0, :], in_=Bq[:, h, 0:128], identity=identb[:]),
                         r=[rB, rid], w=[rpq[a]])
                    k.op("pe", lambda e: e.transpose(out=pq[a][0:64, hh, 1, :], in_=Bq[:, h, 128:192], identity=identb[:]),
                         r=[rB, rid], w=[rpq[a]])
                k.op("act", lambda e: e.copy(out=qTs[a][:, :, 0, :], in_=pq[a][:, :, 0, :]), r=[rpq[a]], w=[rqTs[a]])
                k.op("act", lambda e: e.copy(out=qTs[a][0:64, :, 1, :], in_=pq[a][0:64, :, 1, :]), r=[rpq[a]], w=[rqTs[a]])
                k.dma("sp", dv[0:128, half * 8:(half + 1) * 8, rows], qTs[a][:, :, 0, :], r=[rqTs[a]])
                k.dma("sp", dv[128:192, half * 8:(half + 1) * 8, rows], qTs[a][0:64, :, 1, :], r=[rqTs[a]])

        loadt(0)
        it = 0
        for t in range(NT):
            s = t % 2
            rows = slice(t * 128, (t + 1) * 128)
            if t + 1 < NT:
                loadt(t + 1)
            for blk in range(8):
                a = it % 2; it += 1
                cs = slice(blk * 384, (blk + 1) * 384)
                for kc in range(6):
                    k.op("pe", lambda e: e.matmul(acc[a][:, 0:384], lhsT=nT[s][:, kc, :], rhs=wuq[:, kc, cs],
                                                  start=(kc == 0), stop=(kc == 5)), r=[rin[s], rwu], w=[racc[a]])
                k.op("act", lambda e: e.copy(out=fall[0][:, 2 * blk:2 * blk + 2, :],
                                             in_=acc[a][:, 0:384].rearrange("p (h d) -> p h d", h=2)), r=[racc[a]], w=[rfall[0]])
            norm_rope_T(0, 0, QT2, s, rows)
            if CUT <= 4:
                continue
            for blk in range(8):
                a = it % 2; it += 1
                cs = slice(blk * 512, (blk + 1) * 512)
                for kc in range(4):
                    k.op("pe", lambda e: e.matmul(acc[a][:], lhsT=nT[s][:, 6 + kc, :], rhs=wukv[:, kc, cs],
                                                  start=(kc == 0), stop=(kc == 3)), r=[rin[s], rwu], w=[racc[a]])
                av = acc[a][:].rearrange("p (h c d) -> p h c d", h=2, c=2)
                k.op("act", lambda e: e.copy(out=fall[1][:, 2 * blk:2 * blk + 2, 0:128], in_=av[:, :, 0, :]), r=[racc[a]], w=[rfall[1]])
                k.op("act", lambda e: e.copy(out=vball[s][:, 2 * blk:2 * blk + 2, :], in_=av[:, :, 1, :]), r=[racc[a]], w=[rvb[s]])
            k.op("dve", lambda e: e.tensor_tensor(out=fall[1][:, :, 128:192], in0=ones3[:], in1=kr[s][:].unsqueeze(1).to_broadcast([128, 16, 64]),
                                                  op=ALU.mult), r=[rin[s], rones3], w=[rfall[1]])
            for q_ in range(4):
                k.dma("sp", V2[rows, q_ * 512:(q_ + 1) * 512], vball[s][:, 4 * q_:4 * q_ + 4, :].rearrange("p h d -> p (h d)"), r=[rvb[s]])
            if CUT <= 5:
                continue
            norm_rope_T(1, 1, KT2, s, rows)


def emit_m2(P, QT2, KT2, V2, CATT):
    k = P.k
    scale = 192 ** -0.5
    with Stage(P):
        ones = P.sb([128, 128]); rones = Res()
        k.op("pool", lambda e: e.memset(ones[:], 1.0), w=[rones])
        KA = [P.sb([128, NTOK], BF16) for _ in range(2)]; KB = [P.sb([64, NTOK], BF16) for _ in range(2)]
        QA = [P.sb([128, NTOK], BF16) for _ in range(2)]; QB = [P.sb([64, NTOK], BF16) for _ in range(2)]
        Vh = [P.sb([128, NT, 128], BF16) for _ in range(2)]
        rin = [Res(), Res()]
        S = [P.ps([128, 512]) for _ in range(2)]; rS = [Res(), Res()]
        po = [P.ps([128, 512]) for _ in range(2)]; rpo = [Res(), Res()]
        pd = [P.ps([128, 512]) for _ in range(2)]; rpd = [Res(), Res()]
        Pm = [P.sb([128, 512], BF16) for _ in range(4)]; rPm = [Res() for _ in range(4)]
        dsum = [[P.sb([128, 512]) for _ in range(2)] for _ in range(2)]; rds = [[Res(), Res()], [Res(), Res()]]
        rec = P.sb([128, 512]); rrec = Res()
        osb = [P.sb([128, 512], BF16) for _ in range(2)]; rosb = [Res(), Res()]

        def loadh(h):
            b = h % 2
            k.dma("sp", KA[b][:], KT2[h * 192:h * 192 + 128, :], w=[rin[b]])
            k.dma("sp", KB[b][:], KT2[h * 192 + 128:h * 192 + 192, :], w=[rin[b]])
            k.dma("sp", QA[b][:], QT2[h * 192:h * 192 + 128, :], w=[rin[b]])
            k.dma("sp", QB[b][:], QT2[h * 192 + 128:h * 192 + 192, :], w=[rin[b]])
            k.dma("sp", Vh[b][:], V2[:, h * 128:(h + 1) * 128].rearrange("(j p) d -> p j d", p=128), w=[rin[b]])

        cnt = {"s": 0, "p": 0, "o": 0}
        loadh(0)
        for h in range(16):
            b = h % 2
            if h + 1 < 16:
                loadh(h + 1)
            blocks = [(0, 256, 2)] + [(CTX + i * 512, 512, NT) for i in range(8)]
            for (q0, n, nk) in blocks:
                o = cnt["o"] % 2; cnt["o"] += 1

                def qk(kt):
                    a = kt % 2
                    ks = slice(kt * 128, (kt + 1) * 128)
                    k.op("pe", lambda e: e.matmul(S[a][:, 0:n], lhsT=KA[b][:, ks], rhs=QA[b][:, q0:q0 + n], start=True, stop=False),
                         r=[rin[b]], w=[rS[a]])
                    k.op("pe", lambda e: e.matmul(S[a][:, 0:n], lhsT=KB[b][:, ks], rhs=QB[b][:, q0:q0 + n], start=False, stop=True),
                         r=[rin[b]], w=[rS[a]])

                qk(0)
                for kt in range(nk):
                    a = kt % 2
                    pi = cnt["p"] % 4; cnt["p"] += 1
                    if kt + 1 < nk:
                        qk(kt + 1)
                    k.op("act", lambda e: e.activation(out=Pm[pi][:, 0:n], in_=S[a][:, 0:n], func=AF.Exp, scale=scale),
                         r=[rS[a]], w=[rPm[pi]])
                    k.op("pe", lambda e: e.matmul(po[o][:, 0:n], lhsT=Vh[b][:, kt, :], rhs=Pm[pi][:, 0:n], start=(kt == 0),
                                                  stop=(kt == nk - 1)), r=[rin[b], rPm[pi]], w=[rpo[o]])
                    eng = "dve" if kt % 2 == 0 else "pool"
                    d_ = dsum[o][kt % 2]; rd_ = rds[o][kt % 2]
                    if kt < 2:
                        k.op(eng, lambda e: e.tensor_copy(out=d_[:, 0:n], in_=Pm[pi][:, 0:n]), r=[rPm[pi]], w=[rd_])
                    else:
                        k.op(eng, lambda e: e.tensor_tensor(out=d_[:, 0:n], in0=d_[:, 0:n], in1=Pm[pi][:, 0:n], op=ALU.add),
                             r=[rPm[pi]], w=[rd_])
                for i_ in range(2):
                    k.op("pe", lambda e: e.matmul(pd[o][:, 0:n], lhsT=ones[:], rhs=dsum[o][i_][:, 0:n], start=(i_ == 0), stop=(i_ == 1)),
                         r=[rones, rds[o][i_]], w=[rpd[o]])
                k.op("dve", lambda e: e.reciprocal(out=rec[:, 0:n], in_=pd[o][:, 0:n]), r=[rpd[o]], w=[rrec])
                k.op("dve", lambda e: e.tensor_tensor(out=osb[o][:, 0:n], in0=po[o][:, 0:n], in1=rec[:, 0:n], op=ALU.mult),
                     r=[rpo[o], rrec], w=[rosb[o]])
                k.dma("sp", CATT[h * 128:(h + 1) * 128, q0:q0 + n], osb[o][:, 0:n], r=[rosb[o]])


def _dft_consts(L):
    LT = L // 128
    n = np.arange(L, dtype=np.float64)
    theta = np.pi * np.outer(2 * n + 1, 2 * n + 1) / (4.0 * L)
    def tiled(m):
        return np.ascontiguousarray(m.reshape(LT, 128, LT, 128).transpose(2, 1, 0, 3)).astype(np.float32).astype(NPBF)
    C2 = tiled(np.cos(theta)); S2 = tiled(np.sin(theta))
    half = np.pi * (2 * n + 1) / (4.0 * L)
    cs = np.stack([np.cos(half), np.sin(half), -np.cos(half)], 0).reshape(3, LT, 128).transpose(2, 0, 1)
    t = np.linspace(0.0, 1.0, L)
    w = 2.0 * np.pi * np.arange(L) / L
    bands = np.linspace(1e-4, 15.0, 16)
    z = np.concatenate([t[:, None], np.cos(bands[None, :] * w[:, None]), -np.sin(bands[None, :] * w[:, None])], axis=1)
    negt = (-t).reshape(LT, 128).T
    return {"C2": C2, "S2": S2, "cs": np.ascontiguousarray(cs).astype(np.float32),
            "zT": np.ascontiguousarray(z.T).astype(np.float32), "negt": np.ascontiguousarray(negt).astype(np.float32)}


def _deltas():
    d_lo = np.log(1e-2) / 1.5
    d_hi = np.log(1e-2) / 0.3
    return np.abs(np.linspace(d_lo, d_hi, 1024)).astype(np.float32)[None, :]


def _rope_table():
    tab = np.zeros((NTOK, 3, 32), np.float32)
    tab[:CTX, 0, :] = 1.0
    t = np.arange(SEQ)
    inv = 10000.0 ** (-np.arange(0, 32, 2, dtype=np.float64) / 32.0)
    for g, pos in enumerate((t // GRID, t % GRID)):
        ang = pos[:, None].astype(np.float64) * inv[None, :]
        tab[CTX:, 0, g * 16:(g + 1) * 16] = np.cos(ang)
        tab[CTX:, 1, g * 16:(g + 1) * 16] = np.sin(ang)
        tab[CTX:, 2, g * 16:(g + 1) * 16] = -np.sin(ang)
    return tab


def _na_tables(rpb):
    kc = np.arange(64)[:, None]; qc = np.arange(64)[None, :]
    dc = np.clip(kc - qc + 15, 0, 30)
    BT = np.ascontiguousarray(rpb[:, :, :, dc])
    cstart = np.clip(qc - 8, 0, 48)
    inwin = (kc >= cstart) & (kc < cstart + 16)
    MK = np.where(inwin, 0.0, -1e30).astype(np.float32)
    return BT.astype(np.float32), np.ascontiguousarray(MK)


IDENTB = np.eye(128, dtype=np.float32).astype(NPBF)
IDENTF = np.eye(128, dtype=np.float32)


def emit_p0(P, c2T, ada_w, ada_b, MD):
    k = P.k
    with Stage(P):
        sc = P.sb([128, 16, 2]); rsc = Res()
        k.dma("sp", sc[:], c2T[:, :, :], w=[rsc])
        k.op("act", lambda e: e.activation(out=sc[:], in_=sc[:], func=AF.Silu), r=[rsc], w=[rsc])
        wt = [P.sb([128, 16, 512]) for _ in range(3)]; rw = [Res() for _ in range(3)]
        acc = [P.ps([128, 512]) for _ in range(2)]; racc = [Res(), Res()]
        bt = [P.sb([2, 512]) for _ in range(2)]; rb = [Res(), Res()]
        ot = [P.sb([2, 512]) for _ in range(2)]; ro = [Res(), Res()]
        blocks = [(l, nb) for l in range(DEPTH) for nb in range(24)]

        def loadw(i):
            l, nb = blocks[i]
            cs = slice(nb * 512, nb * 512 + 512)
            k.dma("sp" if i % 2 == 0 else "pool", wt[i % 3][:], ada_w[l, :, cs].rearrange("(kc p) n -> p kc n", p=128), w=[rw[i % 3]])

        loadw(0); loadw(1)
        for i, (l, nb) in enumerate(blocks):
            s = i % 2
            cs = slice(nb * 512, nb * 512 + 512)
            if i + 2 < len(blocks):
                loadw(i + 2)
            k.dma("sp", bt[s][:], ada_b[l:l + 1, cs].to_broadcast([2, 512]), w=[rb[s]])
            for kc in range(16):
                k.op("pe", lambda e: e.matmul(acc[s][0:2, :], lhsT=sc[:, kc, :], rhs=wt[i % 3][:, kc, :],
                                              start=(kc == 0), stop=(kc == 15)),
                     r=[rsc, rw[i % 3]], w=[racc[s]], skip_same=(kc > 0))
            k.op("dve", lambda e: e.tensor_tensor(out=ot[s][:], in0=acc[s][0:2, :], in1=bt[s][:], op=ALU.add),
                 r=[racc[s], rb[s]], w=[ro[s]])
            k.dma("sp", MD[l, :, cs], ot[s][:], r=[ro[s]])


def build_main(layers=(0, 1, 2, 3), stop_after=None, dump=()):
    P = Prog()
    nc, k = P.nc, P.k
    ins = {}

    def din(name, shape, dt=F32):
        if name not in ins:
            ins[name] = P.din(name, shape, dt)
        return ins[name]

    def scr(name, shape, dt=F32):
        if name in dump:
            return P.dout(name, shape, dt)
        return P.dscr(name, shape, dt)

    x_in = din("x", [SEQ, D_MODEL]); ctx_in = din("ctx", [CTX, D_MODEL])
    XL = P.dout("out", [SEQ, D_MODEL])
    XC = scr("XC", [CTX, D_MODEL])
    X = TokT(XC, XL)
    H2 = TokT(scr("H2C", [CTX, D_MODEL]), scr("H2L", [SEQ, D_MODEL]))
    MD = P.dscr("MD", [DEPTH, 2, 6 * D_MODEL])
    md_all = MD.rearrange("l s (m d) -> l s m d", m=6)
    identb_d = din("identb", [128, 128], BF16); identf_d = din("identf", [128, 128])
    QT = scr("QT", [1024, NTOK], BF16); KT = scr("KT", [1024, NTOK], BF16); V = scr("V", [NTOK, 1024], BF16)
    HYP = scr("HYP", [4356, 3072])
    CATT = scr("CATT", [D_MODEL, NTOK], BF16)
    AFFT = scr("AFFT", [16, NTOK])
    pscr = {"idxL": scr("idxL", [NE, CAP_L], U32), "gatL": scr("gatL", [NE, CAP_L]),
            "idxC": scr("idxC", [NE, CAP_C], U32), "gatC": scr("gatC", [NE, CAP_C])}
    QAT = scr("QAT", [1280, NTOK], BF16); KR = scr("KR", [NTOK, 64])
    QT2 = scr("QT2", [3072, NTOK], BF16); KT2 = scr("KT2", [3072, NTOK], BF16); V2 = scr("V2", [NTOK, D_MODEL], BF16)

    with Stage(P):
        tb = [P.sb([128, D_MODEL]) for _ in range(2)]; rtb = [Res(), Res()]
        for t in range(NT):
            s = t % 2
            src = ctx_in[t * 128:(t + 1) * 128, :] if t < 2 else x_in[(t - 2) * 128:(t - 1) * 128, :]
            k.dma("sp", tb[s][:], src, w=[rtb[s]])
            k.dma("sp", X.tile(t), tb[s][:], r=[rtb[s]])
        zt = P.sb([1, 3072]); rz = Res()
        k.op("pool", lambda e: e.memset(zt[:], 0.0), w=[rz])
        for r_ in (0, 257, 258, 4355):
            k.dma("sp", HYP[r_:r_ + 1, :], zt[:], r=[rz])

    emit_p0(P, din("c2T", [128, 16, 2]), din("ada_w", [DEPTH, D_MODEL, 6 * D_MODEL]), din("ada_b", [DEPTH, 6 * D_MODEL]), MD)

    def done(tag):
        return stop_after is not None and stop_after == tag

    for i in layers:
        j = i // 2
        md = md_all[i]
        g1 = din("norm1_g", [DEPTH, D_MODEL])[i:i + 1, :]
        g2 = din("norm2_g", [DEPTH, D_MODEL])[i:i + 1, :]
        if i % 2 == 0:
            emit_e1(P, X, md, g1, din("ev_w_in", [2, D_MODEL, 6144])[j], din("qkg", [2, 2, 512])[j], identb_d, QT, KT, V, HYP)
            if done("e1%d" % i):
                break
            emit_e2(P, QT, KT, V, din("BT", [2, 8, 15, 64, 64])[j], din("MK", [64, 64]), CATT)
            if done("e2%d" % i):
                break
            consts = {"deltas": din("deltas", [1, 1024])}
            for nm, L in (("C", CTX), ("L", SEQ)):
                LT = L // 128
                consts[nm] = {"C2": din("C2" + nm, [LT, 128, LT, 128], BF16), "S2": din("S2" + nm, [LT, 128, LT, 128], BF16),
                              "cs": din("cs" + nm, [128, 3, LT]), "zT": din("zT" + nm, [33, L]), "negt": din("negt" + nm, [128, LT])}
            emit_e3(P, HYP, din("hy_short_w", [2, 3, 3072])[j], din("hy_short_b", [2, 3072])[j:j + 1, :],
                    din("hy_w1", [2, 33, 64])[j], din("hy_b1", [2, 64, 1])[j], din("hy_w2", [2, 64, 64])[j],
                    din("hy_b2", [2, 64, 1])[j], din("hy_freq", [2, 64, 1])[j], din("hy_w3", [2, 64, 4096])[j],
                    din("hy_d", [2, 2, 1024])[j], consts, identb_d, CATT)
            if done("e3%d" % i):
                break
            w_o = din("ev_w_out", [2, D_MODEL, D_MODEL])[j]
        else:
            emit_m1(P, X, md, g1, din("mla_w_down", [2, D_MODEL, 1344])[j], din("mla_qa_g", [2, 768])[j:j + 1, :],
                    din("mla_kva_g", [2, 512])[j:j + 1, :], din("mla_w_uq", [2, 768, 3072])[j], din("mla_w_ukv", [2, 512, 4096])[j],
                    din("mla_q_g", [2, 192])[j:j + 1, :], din("mla_k_g", [2, 192])[j:j + 1, :], din("rope", [NTOK, 3, 32]),
                    identb_d, QAT, KR, QT2, KT2, V2)
            if done("m1%d" % i):
                break
            emit_m2(P, QT2, KT2, V2, CATT)
            if done("m2%d" % i):
                break
            w_o = din("mla_w_o", [2, D_MODEL, D_MODEL])[j]
        emit_p4(P, CATT, X, w_o, md, g2, din("router_w", [DEPTH, D_MODEL, 16])[i], identf_d, H2, AFFT)
        if done("p4%d" % i):
            break
        emit_p5(P, AFFT, H2, X, din("moe_w_gate", [DEPTH, 16, D_MODEL, 1024])[i], din("moe_w_up", [DEPTH, 16, D_MODEL, 1024])[i],
                din("moe_w_down", [DEPTH, 16, 1024, D_MODEL])[i], md, identb_d, pscr)
    print("main ninstr", k.ninstr, "inputs", list(ins.keys()))
    nc = P.finish()
    return nc, list(ins.keys())


def host_inputs(inp, mods=None):
    BT, MK = _na_tables(inp["na_rpb"])
    d = {
        "identb": IDENTB, "identf": IDENTF, "ada_w": inp["ada_w"], "ada_b": inp["ada_b"],
        "norm1_g": inp["norm1_g"], "norm2_g": inp["norm2_g"], "ev_w_in": inp["ev_w_in"], "ev_w_out": inp["ev_w_out"],
        "qkg": np.ascontiguousarray(np.stack([np.tile(inp["na_q_g"], (1, 4)), np.tile(inp["na_k_g"], (1, 4))], 1)),
        "BT": BT, "MK": MK, "deltas": _deltas(),
        "hy_short_w": inp["hy_short_w"], "hy_short_b": inp["hy_short_b"], "hy_w1": inp["hy_w1"],
        "hy_b1": inp["hy_b1"][:, :, None], "hy_w2": inp["hy_w2"], "hy_b2": inp["hy_b2"][:, :, None],
        "hy_freq": inp["hy_freq"][:, :, None], "hy_w3": inp["hy_w3"], "hy_d": inp["hy_d"],
        "mla_w_down": inp["mla_w_down"], "mla_qa_g": inp["mla_qa_g"], "mla_kva_g": inp["mla_kva_g"],
        "mla_w_uq": inp["mla_w_uq"], "mla_w_ukv": inp["mla_w_ukv"], "mla_q_g": inp["mla_q_g"], "mla_k_g": inp["mla_k_g"],
        "mla_w_o": inp["mla_w_o"], "rope": _rope_table(), "router_w": inp["router_w"],
        "moe_w_gate": inp["moe_w_gate"], "moe_w_up": inp["moe_w_up"], "moe_w_down": inp["moe_w_down"],
    }
    for nm, L in (("C", CTX), ("L", SEQ)):
        c = _dft_consts(L)
        for kk, v in c.items():
            d[kk + nm] = v
    return d


def core_inputs(shared, inp, mods, b, names):
    c2 = np.stack([inp["c"][b], inp["c_ctx"]], 0)
    m = {"x": inp["x"][b], "ctx": inp["ctx"][b], "c2T": np.ascontiguousarray(c2.T.reshape(16, 128, 2).transpose(1, 0, 2))}
    out = {}
    for n in names:
        v = m[n] if n in m else shared[n]
        out[n] = np.ascontiguousarray(v)
    return out


def kernel(**inp):
    inp = {k_: np.asarray(v) for k_, v in inp.items()}
    nc, names = build_main()
    shared = host_inputs(inp)
    maps = [core_inputs(shared, inp, None, b, names) for b in range(BATCH)]
    res = run_bass_kernel_spmd(nc, maps, core_ids=list(range(BATCH)))
    return np.stack([np.asarray(res.results[b]["out"]) for b in range(BATCH)], 0).astype(np.float32)
```
